# Optimizing a Trainium2 kernel written in Bass

```python
import jax
import jax.numpy as jnp
from jax import lax
import numpy as np

D_MODEL = 2048
BATCH = 4
SEQ = 8192
DEPTH = 1

EPS = 1e-6
NEG = -1e30
Q_BLOCK = 128

MLA_HEADS = 8
MLA_Q_LORA = 512
MLA_KV_LORA = 256
MLA_NOPE = 128
MLA_ROPE = 64
MLA_V = 128
ROPE_THETA = 10000.0

NSA_HEADS = 16
NSA_GROUPS = 2
NSA_HPG = NSA_HEADS // NSA_GROUPS
NSA_DK = 64
CMP_LEN = 32
CMP_STRIDE = 16
CMP_HIDDEN = 128
SLC_LEN = 64
SLC_TOPK = 16
WINDOW = 512
FORCE_SCORE = 1e4

MIX_WIDTH = MLA_HEADS * MLA_V + NSA_HEADS * NSA_DK
IN_SIZES = (MLA_Q_LORA, MLA_KV_LORA, MLA_ROPE, NSA_HEADS * NSA_DK) + (NSA_GROUPS * NSA_DK,) * 6 + (3 * NSA_HEADS,)
IN_COLS = sum(IN_SIZES)
D_FF = -(-8 * D_MODEL // (3 * 256)) * 256

kernel_name = 'hybrid_mla_nsa_parallel_heads'


def rmsnorm(x, g):
    xf = x.astype(jnp.float32)
    y = xf * lax.rsqrt(jnp.mean(xf * xf, axis=-1, keepdims=True) + EPS)
    return (y * g.astype(jnp.float32)).astype(x.dtype)


def apply_rope(x, pos):
    d = x.shape[-1]
    inv = ROPE_THETA ** (-jnp.arange(0, d, 2, dtype=jnp.float32) / d)
    ang = pos[:, None] * inv[None, :]
    cos = jnp.cos(ang)[None, :, None, :].astype(x.dtype)
    sin = jnp.sin(ang)[None, :, None, :].astype(x.dtype)
    x1, x2 = x[..., : d // 2], x[..., d // 2:]
    return jnp.concatenate([x1 * cos - x2 * sin, x1 * sin + x2 * cos], axis=-1)


def mla_group(c_q_raw, c_kv_raw, k_rope_raw, g_q, g_kv, w_uq, w_uk, w_uv):
    B, S, _ = c_q_raw.shape
    pos = jnp.arange(S, dtype=jnp.float32)
    c_q = rmsnorm(c_q_raw, g_q)
    c_kv = rmsnorm(c_kv_raw, g_kv)
    q = (c_q @ w_uq).reshape(B, S, MLA_HEADS, MLA_NOPE + MLA_ROPE)
    q_nope = q[..., :MLA_NOPE]
    q_rope = apply_rope(q[..., MLA_NOPE:], pos)
    k_rope = apply_rope(k_rope_raw[:, :, None, :], pos)[:, :, 0]
    k_nope = (c_kv @ w_uk).reshape(B, S, MLA_HEADS, MLA_NOPE)
    v = (c_kv @ w_uv).reshape(B, S, MLA_HEADS, MLA_V)
    scale = (MLA_NOPE + MLA_ROPE) ** -0.5
    key_pos = jnp.arange(S)

    def block(i):
        start = i * Q_BLOCK
        qn = lax.dynamic_slice_in_dim(q_nope, start, Q_BLOCK, axis=1)
        qr = lax.dynamic_slice_in_dim(q_rope, start, Q_BLOCK, axis=1)
        s = jnp.einsum('bqhd,bkhd->bhqk', qn, k_nope) + jnp.einsum('bqhd,bkd->bhqk', qr, k_rope)
        s = s.astype(jnp.float32) * scale
        qpos = start + jnp.arange(Q_BLOCK)
        s = jnp.where(key_pos[None, :] <= qpos[:, None], s, NEG)
        p = jax.nn.softmax(s, axis=-1).astype(v.dtype)
        return jnp.einsum('bhqk,bkhd->bqhd', p, v)

    o = lax.map(block, jnp.arange(S // Q_BLOCK))
    return o.transpose(1, 0, 2, 3, 4).reshape(B, S, MLA_HEADS * MLA_V)


def compress_blocks(k, idx, pos_emb, w1, w2):
    B = k.shape[0]
    n_cmp = idx.shape[0]
    blocks = k[:, idx] + pos_emb[None, None, :, None, :]
    blocks = blocks.transpose(0, 1, 3, 2, 4).reshape(B, n_cmp, NSA_GROUPS, CMP_LEN * NSA_DK)
    return jax.nn.silu(blocks @ w1) @ w2


def nsa_group(q_raw, k_c, v_c, k_s, v_s, k_w, v_w, g_raw, pos_k, pos_v, w_ck1, w_ck2, w_cv1, w_cv2):
    B, S, _ = q_raw.shape
    G, HPG, DK = NSA_GROUPS, NSA_HPG, NSA_DK
    q = q_raw.reshape(B, S, G, HPG, DK)
    gates = jax.nn.sigmoid(g_raw).reshape(B, S, G, HPG, 3)
    k_c = k_c.reshape(B, S, G, DK)
    v_c = v_c.reshape(B, S, G, DK)
    k_s_t = k_s.reshape(B, S, G, DK).transpose(0, 2, 1, 3)
    v_s_t = v_s.reshape(B, S, G, DK).transpose(0, 2, 1, 3)
    pad = ((0, 0), (WINDOW, 0), (0, 0), (0, 0))
    k_w_pad = jnp.pad(k_w.reshape(B, S, G, DK), pad)
    v_w_pad = jnp.pad(v_w.reshape(B, S, G, DK), pad)
    scale = DK ** -0.5

    n_cmp = (S - CMP_LEN) // CMP_STRIDE + 1
    cmp_start = CMP_STRIDE * jnp.arange(n_cmp)
    idx = cmp_start[:, None] + jnp.arange(CMP_LEN)[None, :]
    kc = compress_blocks(k_c, idx, pos_k, w_ck1, w_ck2)
    vc = compress_blocks(v_c, idx, pos_v, w_cv1, w_cv2)
    cmp_end = cmp_start + CMP_LEN - 1

    n_slc = S // SLC_LEN
    topk = min(SLC_TOPK, n_slc)
    slc_start = SLC_LEN * jnp.arange(n_slc)
    overlap = jnp.clip(
        jnp.minimum(cmp_start[:, None] + CMP_LEN, slc_start[None, :] + SLC_LEN)
        - jnp.maximum(cmp_start[:, None], slc_start[None, :]), 0, None
    ).astype(jnp.float32) / CMP_STRIDE
    blk_ids = jnp.arange(n_slc)
    in_blk = jnp.arange(SLC_LEN)

    slopes = (2.0 ** (-8.0 * jnp.arange(1, NSA_HEADS + 1, dtype=jnp.float32) / NSA_HEADS)).reshape(G, HPG)
    slopes5 = slopes[None, :, :, None, None]
    gather = jax.vmap(jax.vmap(lambda src, ix: src[ix]))

    def block(i):
        start = i * Q_BLOCK
        qb = lax.dynamic_slice_in_dim(q, start, Q_BLOCK, axis=1)
        t = start + jnp.arange(Q_BLOCK)

        dist_c = t[:, None] - cmp_end[None, :]
        valid_c = dist_c >= 0
        s = jnp.einsum('bqghd,bcgd->bghqc', qb, kc).astype(jnp.float32) * scale
        s = jnp.where(valid_c, s - slopes5 * dist_c.astype(jnp.float32), NEG)
        p_c = jax.nn.softmax(s, axis=-1) * valid_c
        o_c = jnp.einsum('bghqc,bcgd->bqghd', p_c.astype(vc.dtype), vc)

        imp = jnp.einsum('bghqc,cn->bgqn', p_c, overlap)
        blk_t = t // SLC_LEN
        valid_s = blk_ids[None, :] <= blk_t[:, None]
        forced = (blk_ids[None, :] == 0) | (blk_ids[None, :] == blk_t[:, None]) | (blk_ids[None, :] == blk_t[:, None] - 1)
        imp = jnp.where(forced, FORCE_SCORE, jnp.where(valid_s, imp, -1.0))
        _, sel = lax.top_k(imp, topk)
        tok = (sel[..., None] * SLC_LEN + in_blk).reshape(B, G, Q_BLOCK * topk * SLC_LEN)
        nk = topk * SLC_LEN
        ks_g = gather(k_s_t, tok).reshape(B, G, Q_BLOCK, nk, DK)
        vs_g = gather(v_s_t, tok).reshape(B, G, Q_BLOCK, nk, DK)
        dist_s = (t[None, None, :, None] - tok.reshape(B, G, Q_BLOCK, nk))[:, :, None]
        s = jnp.einsum('bqghd,bgqkd->bghqk', qb, ks_g).astype(jnp.float32) * scale
        s = jnp.where(dist_s >= 0, s - slopes5 * dist_s.astype(jnp.float32), NEG)
        p_s = jax.nn.softmax(s, axis=-1).astype(vs_g.dtype)
        o_s = jnp.einsum('bghqk,bgqkd->bqghd', p_s, vs_g)

        kwb = lax.dynamic_slice_in_dim(k_w_pad, start, Q_BLOCK + WINDOW, axis=1)
        vwb = lax.dynamic_slice_in_dim(v_w_pad, start, Q_BLOCK + WINDOW, axis=1)
        s_pos = start - WINDOW + jnp.arange(Q_BLOCK + WINDOW)
        dist_w = t[:, None] - s_pos[None, :]
        valid_w = (dist_w >= 0) & (dist_w < WINDOW) & (s_pos[None, :] >= 0)
        s = jnp.einsum('bqghd,bkgd->bghqk', qb, kwb).astype(jnp.float32) * scale
        s = jnp.where(valid_w, s - slopes5 * dist_w.astype(jnp.float32), NEG)
        p_w = jax.nn.softmax(s, axis=-1).astype(vwb.dtype)
        o_w = jnp.einsum('bghqk,bkgd->bqghd', p_w, vwb)

        gb = lax.dynamic_slice_in_dim(gates, start, Q_BLOCK, axis=1)
        return gb[..., 0:1] * o_c + gb[..., 1:2] * o_s + gb[..., 2:3] * o_w

    o = lax.map(block, jnp.arange(S // Q_BLOCK))
    return o.transpose(1, 0, 2, 3, 4, 5).reshape(B, S, NSA_HEADS * DK)


def setup_inputs(seed: int = 0) -> dict:
    key = jax.random.key(seed)
    ks = jax.random.split(key, 24)
    f32 = jnp.float32

    def nrm(k, shape, fan_in):
        return jax.random.normal(k, shape, f32) * (fan_in ** -0.5)

    def gain(k, shape):
        return 1.0 + 0.01 * jax.random.normal(k, shape, f32)

    L = DEPTH
    return {
        'x': jax.random.normal(ks[0], (BATCH, SEQ, D_MODEL), f32),
        'attn_norm_g': gain(ks[1], (L, D_MODEL)),
        'w_in': nrm(ks[2], (L, D_MODEL, IN_COLS), D_MODEL),
        'mla_q_norm_g': gain(ks[3], (L, MLA_Q_LORA)),
        'mla_kv_norm_g': gain(ks[4], (L, MLA_KV_LORA)),
        'w_uq': nrm(ks[5], (L, MLA_Q_LORA, MLA_HEADS * (MLA_NOPE + MLA_ROPE)), MLA_Q_LORA),
        'w_uk': nrm(ks[6], (L, MLA_KV_LORA, MLA_HEADS * MLA_NOPE), MLA_KV_LORA),
        'w_uv': nrm(ks[7], (L, MLA_KV_LORA, MLA_HEADS * MLA_V), MLA_KV_LORA),
        'cmp_pos_k': 0.1 * jax.random.normal(ks[8], (L, CMP_LEN, NSA_DK), f32),
        'cmp_pos_v': 0.1 * jax.random.normal(ks[9], (L, CMP_LEN, NSA_DK), f32),
        'w_cmp_k1': nrm(ks[10], (L, CMP_LEN * NSA_DK, CMP_HIDDEN), CMP_LEN * NSA_DK),
        'w_cmp_k2': nrm(ks[11], (L, CMP_HIDDEN, NSA_DK), CMP_HIDDEN),
        'w_cmp_v1': nrm(ks[12], (L, CMP_LEN * NSA_DK, CMP_HIDDEN), CMP_LEN * NSA_DK),
        'w_cmp_v2': nrm(ks[13], (L, CMP_HIDDEN, NSA_DK), CMP_HIDDEN),
        'w_o': nrm(ks[14], (L, MIX_WIDTH, D_MODEL), MIX_WIDTH),
        'ffn_norm_g': gain(ks[15], (L, D_MODEL)),
        'w_gate': nrm(ks[16], (L, D_MODEL, D_FF), D_MODEL),
        'w_up': nrm(ks[17], (L, D_MODEL, D_FF), D_MODEL),
        'w_down': nrm(ks[18], (L, D_FF, D_MODEL), D_FF),
        'final_norm_g': gain(ks[19], (D_MODEL,)),
    }


def reference(x, attn_norm_g, w_in, mla_q_norm_g, mla_kv_norm_g, w_uq, w_uk, w_uv,
              cmp_pos_k, cmp_pos_v, w_cmp_k1, w_cmp_k2, w_cmp_v1, w_cmp_v2,
              w_o, ffn_norm_g, w_gate, w_up, w_down, final_norm_g):
    offsets = np.cumsum(IN_SIZES)[:-1].tolist()
    for l in range(DEPTH):
        h = rmsnorm(x, attn_norm_g[l])
        proj = h @ w_in[l]
        (c_q, c_kv, k_rope, nsa_q, k_c, v_c, k_s, v_s, k_w, v_w, g_raw) = jnp.split(proj, offsets, axis=-1)
        o_mla = mla_group(c_q, c_kv, k_rope, mla_q_norm_g[l], mla_kv_norm_g[l], w_uq[l], w_uk[l], w_uv[l])
        o_nsa = nsa_group(nsa_q, k_c, v_c, k_s, v_s, k_w, v_w, g_raw, cmp_pos_k[l], cmp_pos_v[l],
                          w_cmp_k1[l], w_cmp_k2[l], w_cmp_v1[l], w_cmp_v2[l])
        x = x + jnp.concatenate([o_mla, o_nsa], axis=-1) @ w_o[l]
        h = rmsnorm(x, ffn_norm_g[l])
        x = x + (jax.nn.silu(h @ w_gate[l]) * (h @ w_up[l])) @ w_down[l]
    return rmsnorm(x, final_norm_g)
```

```python
import numpy as np
import ml_dtypes
from contextlib import ExitStack
import concourse.bass as bass
import concourse.mybir as mybir
from concourse.bass_utils import run_bass_kernel_spmd

F32 = mybir.dt.float32
BF16 = mybir.dt.bfloat16
ALU = mybir.AluOpType
AF = mybir.ActivationFunctionType
AX = mybir.AxisListType

D = 2048
SEQ = 8192
NQ = 4096
NSLOT = 8
EPS = 1e-6
DFF = 5632
STAGE = 99
MLA_HEADS_RUN = range(8)
MLA_NSLOT_RUN = NSLOT
FFN_NSLOT_RUN = NSLOT
NSA_NSLOT_RUN = NSLOT
NSA_GROUPS_RUN = (0, 1)
NSA_HEADS_RUN = range(8)
OWN_CHUNKS = ([0, 3, 4, 7, 8, 11, 12, 15], [1, 2, 5, 6, 9, 10, 13, 14])


class Sem:
    def __init__(self, h, is_dma):
        self.h = h
        self.total = 0
        self.is_dma = is_dma
        self.sw = False


class Dep:
    __slots__ = ("w", "r", "excl")

    def __init__(self, excl=False):
        self.w = {}
        self.r = {}
        self.excl = excl


class Q:
    def __init__(self, eng, sem, name, self_wait=True):
        self.eng = eng
        self.sem = sem
        self.name = name
        self.seen = {}
        self.self_wait = self_wait


class FW:
    def __init__(self, nc, es):
        self.nc = nc
        self.es = es
        self.es0 = es
        self.nsem = 0
        self.all_sems = []
        self.free_dsems = {False: [], True: []}
        self.phase_dsems = []
        self.pe = Q(nc.tensor, self.sem("pe"), "pe", self_wait=False)
        self.act = Q(nc.scalar, self.sem("act"), "act")
        self.dve = Q(nc.vector, self.sem("dve"), "dve")
        self.pool = Q(nc.gpsimd, self.sem("pool"), "pool")
        self.sp = Q(nc.sync, self.sem("sp"), "sp")
        self.ninst = 0

    def sem(self, name, is_dma=False):
        self.nsem += 1
        s = Sem(self.es0.enter_context(self.nc.semaphore("m_" + name)), is_dma)
        self.all_sems.append(s)
        return s

    def dsem(self, name, sw=False):
        pool = self.free_dsems[sw]
        if pool:
            s = pool.pop()
        else:
            s = self.sem(name, True)
            s.sw = sw
        self.phase_dsems.append(s)
        return s

    def end_phase(self):
        self.barrier()
        for s in self.phase_dsems:
            self.free_dsems[s.sw].append(s)
        self.phase_dsems = []

    def sb(self, name, shape, dt):
        return self.es.enter_context(self.nc.sbuf_tensor("s_" + name, shape, dt))

    def ps(self, name, shape, dt):
        return self.es.enter_context(self.nc.psum_tensor("p_" + name, shape, dt))

    def _wait(self, q, R, W):
        need = {}
        for d in R:
            for s, v in d.w.items():
                if need.get(s, 0) < v:
                    need[s] = v
            if d.excl:
                for s, v in d.r.items():
                    if s is not q.sem and need.get(s, 0) < v:
                        need[s] = v
        for d in W:
            for s, v in d.w.items():
                if need.get(s, 0) < v:
                    need[s] = v
            for s, v in d.r.items():
                if need.get(s, 0) < v:
                    need[s] = v
        for s, v in need.items():
            if s is q.sem and not q.self_wait:
                continue
            if s.is_dma:
                v = s.total
            if q.seen.get(s, 0) < v:
                q.eng.wait_ge(s.h, v)
                q.seen[s] = v

    def op(self, q, f, R=(), W=()):
        self._wait(q, R, W)
        ins = f(q.eng)
        q.sem.total += 1
        ins.then_inc(q.sem.h, 1)
        v = q.sem.total
        for d in R:
            d.r[q.sem] = v
        for d in W:
            d.w = {q.sem: v}
            d.r = {}
        self.ninst += 1

    def dma(self, q, out, in_, sem, R=(), W=()):
        assert sem.sw == (q is self.pool), "semaphore/queue kind mismatch"
        self._wait(q, R, W)
        ins = q.eng.dma_start(out=out, in_=in_)
        sem.total += 16
        ins.then_inc(sem.h, 16)
        for d in R:
            d.r[sem] = sem.total
        for d in W:
            d.w = {sem: sem.total}
            d.r = {}
        self.ninst += 1

    def dmas(self, q, pairs, sem, R=(), W=()):
        assert sem.sw == (q is self.pool), "semaphore/queue kind mismatch"
        self._wait(q, R, W)
        for (o, i) in pairs:
            ins = q.eng.dma_start(out=o, in_=i)
            sem.total += 16
            ins.then_inc(sem.h, 16)
            self.ninst += 1
        for d in R:
            d.r[sem] = sem.total
        for d in W:
            d.w = {sem: sem.total}
            d.r = {}

    def barrier(self):
        qs = [self.pe, self.act, self.dve, self.pool, self.sp]
        for q in qs:
            for s in self.all_sems:
                if s.total > 0 and q.seen.get(s, 0) < s.total and not (s is q.sem and not q.self_wait):
                    q.eng.wait_ge(s.h, s.total)
                    q.seen[s] = s.total

    def wait_all(self, q, deps):
        self._wait(q, deps, ())


class T:
    def __init__(self, ap, dep=None, sem=None):
        self.ap = ap
        self.d = dep if dep is not None else Dep()
        self.sem = sem

    def __getitem__(self, k):
        return self.ap[k]


def rope_tables(pos):
    inv = (10000.0 ** (-np.arange(0, 64, 2, dtype=np.float32) / np.float32(64))).astype(np.float32)
    ang = (pos.astype(np.float32)[None, :] * inv[:, None]).astype(np.float32)
    c = np.cos(ang).astype(np.float32)
    s = np.sin(ang).astype(np.float32)
    cos_t = np.concatenate([c, c], axis=0)
    sin_t = np.concatenate([-s, s], axis=0)
    return np.ascontiguousarray(cos_t), np.ascontiguousarray(sin_t)


def bcast_rows(ap_row, nparts, n):
    return bass.AP(ap_row.tensor, ap_row.offset, [[0, nparts], [1, n]])


def phase_proj(fw, PS, C, x_dram, ntok, g_dram, w_dram, ncols, side, o, drd):
    nc = fw.nc
    pe, act, dve, pool, sp = fw.pe, fw.act, fw.dve, fw.pool, fw.sp
    ident, ones = C["ident"], C["ones"]
    es2 = ExitStack()
    with es2:
        old_es = fw.es
        fw.es = es2
        wsb = T(fw.sb(f"w{side}", [128, 16, ncols], BF16), sem=fw.dsem(f"w{side}", sw=True))
        fw.dmas(pool, [(wsb[:, kc, :], w_dram[kc * 128:(kc + 1) * 128, :]) for kc in range(16)], wsb.sem, W=[wsb.d])
        gtab = T(fw.sb(f"gtab{side}", [128, D], F32), sem=fw.dsem(f"gtab{side}"))
        fw.dma(sp, gtab[:], bcast_rows(g_dram, 128, D), gtab.sem, W=[gtab.d])
        xt = [T(fw.sb(f"xt{side}{i}", [128, D], F32), sem=fw.dsem(f"xt{side}{i}")) for i in range(2)]
        junk = T(fw.sb(f"junk{side}", [128, D], BF16))
        hb = [T(fw.sb(f"hb{side}{i}", [128, D], BF16)) for i in range(4)]
        hT = [T(fw.sb(f"hT{side}{i}", [128, 16, 512], BF16)) for i in range(2)]
        ssb = [T(fw.sb(f"ss{side}{i}", [128, 2], F32)) for i in range(2)]
        NST = 4
        stg = [T(fw.sb(f"stg{side}{i}", [128, 512], BF16), sem=fw.dsem(f"stg{side}{i}", sw=True)) for i in range(NST)]
        stg_i = [0]

        def stage():
            t = stg[stg_i[0] % NST]
            stg_i[0] += 1
            return t

        if side == "k":
            gsm = T(fw.sb("gkv_sb", [128, 2], F32), sem=fw.dsem("gkv"))
            fw.dma(sp, gsm[:], o["gkv"], gsm.sem, W=[gsm.d])
            NLAT = 2
            costab = [T(fw.sb(f"cos{i}", [64, 512], F32), sem=fw.dsem(f"cos{i}")) for i in range(2)]
            sintab = [T(fw.sb(f"sin{i}", [64, 512], F32), sem=fw.dsem(f"sin{i}")) for i in range(2)]
            rt = [T(fw.sb(f"rt{i}", [64, 512], F32)) for i in range(2)]
        else:
            gsm = T(fw.sb("gq_sb", [128, 4], F32), sem=fw.dsem("gq"))
            fw.dma(sp, gsm[:], o["gq"], gsm.sem, W=[gsm.d])
            NLAT = 4
            gf = T(fw.sb("gf", [48, 512], F32))
            ghi = T(fw.sb("ghi", [48, 512], BF16), sem=fw.dsem("ghi", sw=True))
            glo = T(fw.sb("glo", [48, 512], BF16), sem=fw.dsem("glo", sw=True))
        raw = [T(fw.sb(f"raw{side}{i}", [128, 512], F32)) for i in range(NLAT)]
        sq = [T(fw.sb(f"sq{side}{i}", [128, 512], BF16)) for i in range(NLAT)]
        rsb = T(fw.sb(f"rsb{side}", [128, 512], F32))
        psrot = [0]

        def nextps():
            p = PS[2 + psrot[0] % 6]
            psrot[0] += 1
            return p

        evac_rr = [0]

        def evac_copy(dst_ap, src_ap, R, W):
            evac_rr[0] += 1
            if evac_rr[0] % 2:
                fw.op(act, lambda e: e.copy(out=dst_ap, in_=src_ap), R=R, W=W)
            else:
                fw.op(dve, lambda e: e.tensor_copy(out=dst_ap, in_=src_ap), R=R, W=W)

        NB = ntok // 512

        def norm_part(tb):
            for s in range(4):
                i = tb * 4 + s
                xb = xt[i % 2]
                hbb = hb[s]
                ss = ssb[i % 2]
                fw.dma(sp, xb[:], x_dram[i * 128:(i + 1) * 128, :], xb.sem, W=[xb.d])
                fw.op(act, lambda e: e.memzero(ss[:, 0:1]), W=[ss.d])
                fw.op(act, lambda e: e.activation(out=junk[:], in_=xb[:], func=AF.Square, accum_out=ss[:, 0:1]),
                      R=[xb.d], W=[junk.d, ss.d])
                fw.op(act, lambda e: e.activation(out=ss[:, 1:2], in_=ss[:, 0:1], func=AF.Sqrt, scale=1.0 / D, bias=C["eps"][:, 0:1]),
                      R=[C["eps"].d], W=[ss.d])
                fw.op(dve, lambda e: e.reciprocal(out=ss[:, 1:2], in_=ss[:, 1:2]), W=[ss.d])
                fw.op(dve, lambda e: e.scalar_tensor_tensor(out=hbb[:], in0=xb[:], scalar=ss[:, 1:2], in1=gtab[:],
                                                             op0=ALU.mult, op1=ALU.mult),
                      R=[xb.d, ss.d, gtab.d], W=[hbb.d])

        def transpose_part(tb):
            h = hT[tb % 2]
            for s in range(4):
                hbb = hb[s]
                for half in range(2):
                    pt = PS[half]
                    ptb = pt.ap.bitcast(BF16)
                    for j in range(8):
                        kc = half * 8 + j
                        fw.op(pe, lambda e: e.transpose(out=ptb[:, j * 128:(j + 1) * 128], in_=hbb[:, kc * 128:(kc + 1) * 128],
                                                        identity=ident[:]),
                              R=[hbb.d, ident.d], W=[pt.d])
                    evac_copy(h[:, half * 8:(half + 1) * 8, s * 128:(s + 1) * 128],
                              ptb.rearrange("p (j t) -> p j t", j=8), [pt.d], [h.d])

        norm_part(0)
        transpose_part(0)
        for tb in range(NB):
            h = hT[tb % 2]
            t0 = tb * 512
            if side == "k":
                ct, st_ = costab[tb % 2], sintab[tb % 2]
                fw.dma(sp, ct[:], o["cosk"][:, t0:t0 + 512], ct.sem, W=[ct.d])
                fw.dma(sp, st_[:], o["sink"][:, t0:t0 + 512], st_.sem, W=[st_.d])
            if tb + 1 < NB:
                norm_part(tb + 1)
            def fm_group(c0, m):
                p = nextps()
                for kc in range(16):
                    fw.op(pe, lambda e: e.matmul(p[0:m, :], lhsT=wsb[:, kc, c0:c0 + m], rhs=h[:, kc, :],
                                                 start=(kc == 0), stop=(kc == 15)),
                          R=[wsb.d, h.d], W=[p.d])
                return p

            def store(dst, src_t, m, dd):
                fw.dma(pool, dst, src_t[0:m, :], src_t.sem, R=[src_t.d], W=[dd])

            lat_out = o["ckvT"] if side == "k" else o["cqT"]
            lat_dep = drd["ckvT"] if side == "k" else drd["cqT"]
            for c in range(NLAT):
                p = fm_group(c * 128, 128)
                fw.op(act, lambda e: e.activation(out=sq[c][:], in_=p[:], func=AF.Square), R=[p.d], W=[sq[c].d])
                fw.op(dve, lambda e: e.tensor_copy(out=raw[c][:], in_=p[:]), R=[p.d], W=[raw[c].d])
            p = nextps()
            for c in range(NLAT):
                fw.op(pe, lambda e: e.matmul(p[:], lhsT=ones[:], rhs=sq[c][:], start=(c == 0), stop=(c == NLAT - 1)),
                      R=[ones.d, sq[c].d], W=[p.d])
            fw.op(act, lambda e: e.activation(out=rsb[:], in_=p[:], func=AF.Sqrt, scale=1.0 / (128 * NLAT), bias=C["eps"][:, 0:1]),
                  R=[p.d, C["eps"].d], W=[rsb.d])
            fw.op(dve, lambda e: e.reciprocal(out=rsb[:], in_=rsb[:]), W=[rsb.d])
            for c in range(NLAT):
                sg = stage()
                fw.op(dve, lambda e: e.scalar_tensor_tensor(out=sg[:], in0=raw[c][:], scalar=gsm[:, c:c + 1], in1=rsb[:],
                                                             op0=ALU.mult, op1=ALU.mult),
                      R=[raw[c].d, gsm.d, rsb.d], W=[sg.d])
                store(lat_out[c * 128:(c + 1) * 128, t0:t0 + 512], sg, 128, lat_dep)
            if side == "k":
                px = fm_group(256, 64)
                pw = fm_group(320, 64)
                r1, r2 = rt
                fw.op(dve, lambda e: e.tensor_tensor(out=r1[:], in0=px[0:64, :], in1=ct[:], op=ALU.mult),
                      R=[px.d, ct.d], W=[r1.d])
                fw.op(dve, lambda e: e.tensor_tensor(out=r2[:], in0=pw[0:64, :], in1=st_[:], op=ALU.mult),
                      R=[pw.d, st_.d], W=[r2.d])
                sg = stage()
                fw.op(dve, lambda e: e.tensor_tensor(out=sg[0:64, :], in0=r1[:], in1=r2[:], op=ALU.add),
                      R=[r1.d, r2.d], W=[sg.d])
                store(o["kropeT"][:, t0:t0 + 512], sg, 64, drd["kropeT"])
                for gi, nm in enumerate(["kcT", "vcT", "ksT", "kwT"]):
                    p = fm_group(384 + gi * 128, 128)
                    sg = stage()
                    evac_copy(sg[:], p[:], [p.d], [sg.d])
                    store(o[nm][:, t0:t0 + 512], sg, 128, drd[nm])
                for s in range(4):
                    p = nextps()
                    for kc in range(16):
                        fw.op(pe, lambda e: e.matmul(p[:, 0:256], lhsT=h[:, kc, s * 128:(s + 1) * 128], rhs=wsb[:, kc, 896:1152],
                                                     start=(kc == 0), stop=(kc == 15)),
                              R=[wsb.d, h.d], W=[p.d])
                    sg = stage()
                    evac_copy(sg[:, 0:256], p[:, 0:256], [p.d], [sg.d])
                    fw.dma(pool, o["vsw"][t0 + s * 128:t0 + (s + 1) * 128, :], sg[:, 0:256], sg.sem, R=[sg.d], W=[drd["vsw"]])
            else:
                for gi in range(8):
                    p = fm_group(512 + gi * 128, 128)
                    sg = stage()
                    evac_copy(sg[:], p[:], [p.d], [sg.d])
                    store(o["qnT"][gi * 128:(gi + 1) * 128, t0:t0 + 512], sg, 128, drd["qnT"])
                p = fm_group(1536, 48)
                fw.op(act, lambda e: e.activation(out=gf[:], in_=p[0:48, :], func=AF.Sigmoid), R=[p.d], W=[gf.d])
                fw.op(dve, lambda e: e.tensor_copy(out=ghi[:], in_=gf[:]), R=[gf.d], W=[ghi.d])
                fw.op(dve, lambda e: e.tensor_tensor(out=glo[:], in0=gf[:], in1=ghi[:], op=ALU.subtract),
                      R=[gf.d, ghi.d], W=[glo.d])
                fw.dma(pool, o["gTh"][:, t0:t0 + 512], ghi[:], ghi.sem, R=[ghi.d], W=[drd["gTh"]])
                fw.dma(pool, o["gTl"][:, t0:t0 + 512], glo[:], glo.sem, R=[glo.d], W=[drd["gTl"]])
            if tb + 1 < NB:
                transpose_part(tb + 1)
        fw.es = old_es
        fw.end_phase()


MLA_SCALE = 192.0 ** -0.5


def phase_mla(fw, PS, C, o, drd, heads=range(8), nslot=NSLOT):
    pe, act, dve, pool, sp = fw.pe, fw.act, fw.dve, fw.pool, fw.sp
    ones = C["ones"]
    nkb_all = SEQ // 128
    es2 = ExitStack()
    with es2:
        old_es = fw.es
        fw.es = es2
        krope = T(fw.sb("krope", [128, SEQ], BF16), sem=fw.dsem("krope"))
        fw.op(pool, lambda e: e.memset(krope[64:128, :], 0.0), W=[krope.d])
        fw.dma(sp, krope[0:64, :], o["kropeT"], krope.sem, R=[drd["kropeT"]], W=[krope.d])
        wuq = T(fw.sb("wuq", [128, 4, 2048], BF16), sem=fw.dsem("wuq", sw=True))
        fw.dmas(pool, [(wuq[:, c, :], o["w_uq"][c * 128:(c + 1) * 128, :]) for c in range(4)], wuq.sem, W=[wuq.d])
        wuk = T(fw.sb("wuk", [128, 2, 1024], BF16), sem=fw.dsem("wuk", sw=True))
        fw.dmas(pool, [(wuk[:, c, :], o["w_uk"][c * 128:(c + 1) * 128, :]) for c in range(2)], wuk.sem, W=[wuk.d])
        wuv = T(fw.sb("wuv", [128, 2, 1024], BF16), sem=fw.dsem("wuv", sw=True))
        fw.dmas(pool, [(wuv[:, c, :], o["w_uv"][c * 128:(c + 1) * 128, :]) for c in range(2)], wuv.sem, W=[wuv.d])
        dhi = T(fw.sb("dhi", [128, 512], BF16))
        dlo = T(fw.sb("dlo", [128, 512], BF16))
        masks = []
        for i, nm in enumerate(["cmaskA", "cmaskB"]):
            mt = T(fw.sb(nm, [128, 8, 512], BF16), sem=fw.dsem(nm))
            fw.dma(sp, mt[:], o[nm], mt.sem, W=[mt.d])
            masks.append(mt)
        Kh = [T(fw.sb(f"Kh{i}", [128, SEQ], BF16)) for i in range(2)]
        Vh = [T(fw.sb(f"Vh{i}", [128, nkb_all, 128], BF16)) for i in range(2)]
        ckv = [T(fw.sb(f"ckv{i}", [128, 2, 512], BF16), sem=fw.dsem(f"ckv{i}")) for i in range(2)]
        cq = [T(fw.sb(f"cq{i}", [128, 4, 512], BF16), sem=fw.dsem(f"cq{i}")) for i in range(2)]
        cosq = [T(fw.sb(f"cosq{i}", [64, 512], F32), sem=fw.dsem(f"cosq{i}")) for i in range(2)]
        sinq = [T(fw.sb(f"sinq{i}", [64, 512], F32), sem=fw.dsem(f"sinq{i}")) for i in range(2)]
        qn = [T(fw.sb(f"qn{i}", [128, 512], BF16)) for i in range(2)]
        qr = [T(fw.sb(f"qr{i}", [128, 512], BF16)) for i in range(2)]
        for t_ in qr:
            fw.op(pool, lambda e: e.memset(t_[64:128, :], 0.0), W=[t_.d])
        r1 = [T(fw.sb(f"mr1{i}", [64, 512], F32)) for i in range(2)]
        r2 = [T(fw.sb(f"mr2{i}", [64, 512], F32)) for i in range(2)]
        NPT = 4
        pts = [T(fw.sb(f"pt{i}", [128, 512], BF16)) for i in range(NPT)]
        dacc = [T(fw.sb(f"dacc{i}", [128, 512], F32)) for i in range(4)]
        rec = T(fw.sb("rec", [128, 512], F32))
        ost = [T(fw.sb(f"ost{i}", [128, 512], BF16), sem=fw.dsem(f"ost{i}", sw=True)) for i in range(2)]
        PSO = [PS[3], PS[5]]
        PSD = PS[4]
        PSGEN = [PS[6], PS[7]]
        hpos = {}
        ctr = dict(ckv=0, slot=0, pt=0, ost=0, od=0, gen=0)

        def genbank():
            p = PSGEN[ctr["gen"] % 2]
            ctr["gen"] += 1
            return p

        def kvgen_tasks(h):
            K_, V_ = Kh[hpos[h] % 2], Vh[hpos[h] % 2]
            tasks = []
            for tb in range(SEQ // 512):
                st8 = {}

                def tk(tb=tb, st8=st8):
                    ck = ckv[ctr["ckv"] % 2]
                    ctr["ckv"] += 1
                    st8["ck"] = ck
                    fw.dma(sp, ck[:], o["ckvT"][:, tb * 512:(tb + 1) * 512].rearrange("(c p) t -> p c t", p=128), ck.sem,
                           R=[drd["ckvT"]], W=[ck.d])
                    pg = genbank()
                    for c in range(2):
                        fw.op(pe, lambda e: e.matmul(pg[:], lhsT=wuk[:, c, h * 128:(h + 1) * 128], rhs=ck[:, c, :],
                                                     start=(c == 0), stop=(c == 1)), R=[wuk.d, ck.d], W=[pg.d])
                    fw.op(act, lambda e: e.copy(out=K_[:, tb * 512:(tb + 1) * 512], in_=pg[:]), R=[pg.d], W=[K_.d])

                def tv(tb=tb, st8=st8):
                    ck = st8["ck"]
                    pg = genbank()
                    for s in range(4):
                        for c in range(2):
                            fw.op(pe, lambda e: e.matmul(pg[:, s * 128:(s + 1) * 128], lhsT=ck[:, c, s * 128:(s + 1) * 128],
                                                         rhs=wuv[:, c, h * 128:(h + 1) * 128], start=(c == 0), stop=(c == 1)),
                                  R=[wuv.d, ck.d], W=[pg.d])
                    fw.op(dve, lambda e: e.tensor_copy(out=V_[:, tb * 4:(tb + 1) * 4, :], in_=pg[:].rearrange("p (s d) -> p s d", s=4)),
                          R=[pg.d], W=[V_.d])
                tasks += [tk, tv]
            return tasks

        def qgen_tasks(h, j):
            i = ctr["slot"] % 2
            ctr["slot"] += 1
            cqt, ct, st_, qnt, qrt, r1_, r2_ = cq[i], cosq[i], sinq[i], qn[i], qr[i], r1[i], r2[i]
            q0 = j * 512

            def t0():
                fw.dma(sp, cqt[:], o["cqT"][:, q0:q0 + 512].rearrange("(c p) t -> p c t", p=128), cqt.sem, R=[drd["cqT"]], W=[cqt.d])
                fw.dma(sp, ct[:], o["cosq"][:, q0:q0 + 512], ct.sem, W=[ct.d])
                fw.dma(sp, st_[:], o["sinq"][:, q0:q0 + 512], st_.sem, W=[st_.d])
                pg = genbank()
                for c in range(4):
                    fw.op(pe, lambda e: e.matmul(pg[:], lhsT=wuq[:, c, h * 256:h * 256 + 128], rhs=cqt[:, c, :],
                                                 start=(c == 0), stop=(c == 3)), R=[wuq.d, cqt.d], W=[pg.d])
                fw.op(act, lambda e: e.copy(out=qnt[:], in_=pg[:]), R=[pg.d], W=[qnt.d])

            def t1():
                pg = genbank()
                for c in range(4):
                    fw.op(pe, lambda e: e.matmul(pg[0:64, :], lhsT=wuq[:, c, h * 256 + 128:h * 256 + 192], rhs=cqt[:, c, :],
                                                 start=(c == 0), stop=(c == 3)), R=[wuq.d, cqt.d], W=[pg.d])
                fw.op(dve, lambda e: e.tensor_tensor(out=r1_[:], in0=pg[0:64, :], in1=ct[:], op=ALU.mult), R=[pg.d, ct.d], W=[r1_.d])

            def t2():
                pg = genbank()
                for c in range(4):
                    fw.op(pe, lambda e: e.matmul(pg[0:64, :], lhsT=wuq[:, c, h * 256 + 192:h * 256 + 256], rhs=cqt[:, c, :],
                                                 start=(c == 0), stop=(c == 3)), R=[wuq.d, cqt.d], W=[pg.d])
                fw.op(dve, lambda e: e.tensor_tensor(out=r2_[:], in0=pg[0:64, :], in1=st_[:], op=ALU.mult), R=[pg.d, st_.d], W=[r2_.d])
                fw.op(dve, lambda e: e.tensor_tensor(out=qrt[0:64, :], in0=r1_[:], in1=r2_[:], op=ALU.add), R=[r1_.d, r2_.d], W=[qrt.d])
            return [t0, t1, t2], (qnt, qrt)

        def attend_slot(h, j, bufs, nextq, bg):
            K_, V_ = Kh[hpos[h] % 2], Vh[hpos[h] % 2]
            qnt, qrt = bufs
            mk = masks[j % 2]
            nkb = 8 * (j + 1)
            q0 = j * 512
            po = PSO[ctr["od"] % 2]
            das = (dacc[2 * (ctr["od"] % 2)], dacc[2 * (ctr["od"] % 2) + 1])
            da = das[0]
            ctr["od"] += 1

            def qk(kb):
                ps = PS[kb % 3]
                fw.op(pe, lambda e: e.matmul(ps[:], lhsT=K_[:, kb * 128:(kb + 1) * 128], rhs=qnt[:], start=True, stop=False),
                      R=[K_.d, qnt.d], W=[ps.d])
                fw.op(pe, lambda e: e.matmul(ps[:], lhsT=krope[:, kb * 128:(kb + 1) * 128], rhs=qrt[:], start=False, stop=True),
                      R=[krope.d, qrt.d], W=[ps.d])

            def pv(kb):
                ps = PS[kb % 3]
                pt = pts[ctr["pt"] % NPT]
                ctr["pt"] += 1
                fw.op(act, lambda e: e.activation(out=pt[:], in_=ps[:], func=AF.Exp, scale=MLA_SCALE), R=[ps.d], W=[pt.d])
                w = kb - (nkb - 8)
                if w >= 0:
                    fw.op(dve, lambda e: e.tensor_tensor(out=pt[:], in0=pt[:], in1=mk[:, w, :], op=ALU.mult), R=[mk.d], W=[pt.d])
                fw.op(pe, lambda e: e.matmul(po[:], lhsT=V_[:, kb, :], rhs=pt[:], start=(kb == 0), stop=(kb == nkb - 1)),
                      R=[V_.d, pt.d], W=[po.d])
                dk = das[kb % 2]
                eng_ = dve if kb % 2 == 0 else pool
                if kb < 2:
                    fw.op(eng_, lambda e: e.tensor_copy(out=dk[:], in_=pt[:]), R=[pt.d], W=[dk.d])
                else:
                    fw.op(eng_, lambda e: e.tensor_tensor(out=dk[:], in0=dk[:], in1=pt[:], op=ALU.add), R=[pt.d], W=[dk.d])

            qk(0)
            if nkb > 1:
                qk(1)
            for kb in range(nkb):
                if kb + 2 < nkb:
                    qk(kb + 2)
                pv(kb)
                if nextq and kb % 2 == 1:
                    nextq.pop(0)()
                elif bg and kb % 4 == 3:
                    bg.pop(0)()
            while nextq:
                nextq.pop(0)()
            fw.op(dve, lambda e: e.tensor_tensor(out=da[:], in0=da[:], in1=das[1][:], op=ALU.add), R=[das[1].d], W=[da.d])
            fw.op(dve, lambda e: e.tensor_copy(out=dhi[:], in_=da[:]), R=[da.d], W=[dhi.d])
            fw.op(dve, lambda e: e.tensor_tensor(out=dlo[:], in0=da[:], in1=dhi[:], op=ALU.subtract), R=[da.d, dhi.d], W=[dlo.d])
            fw.op(pe, lambda e: e.matmul(PSD[:], lhsT=ones[:], rhs=dhi[:], start=True, stop=False), R=[ones.d, dhi.d], W=[PSD.d])
            fw.op(pe, lambda e: e.matmul(PSD[:], lhsT=ones[:], rhs=dlo[:], start=False, stop=True), R=[ones.d, dlo.d], W=[PSD.d])
            fw.op(dve, lambda e: e.reciprocal(out=rec[:], in_=PSD[:]), R=[PSD.d], W=[rec.d])
            og = ost[ctr["ost"] % 2]
            ctr["ost"] += 1
            fw.op(dve, lambda e: e.tensor_tensor(out=og[:], in0=po[:], in1=rec[:], op=ALU.mult), R=[po.d, rec.d], W=[og.d])
            fw.dma(pool, o["attnT"][h * 128:(h + 1) * 128, q0:q0 + 512], og[:], og.sem, R=[og.d], W=[drd["attnT"]])

        hs = list(heads)
        hpos.update({h: i for i, h in enumerate(hs)})
        for t in kvgen_tasks(hs[0]):
            t()
        tasks, bufs = qgen_tasks(hs[0], 0)
        for t in tasks:
            t()
        for i, h in enumerate(hs):
            bg = kvgen_tasks(hs[i + 1]) if i + 1 < len(hs) else []
            for j in range(nslot):
                if j + 1 < nslot:
                    nextq, nbufs = qgen_tasks(h, j + 1)
                elif i + 1 < len(hs):
                    nextq, nbufs = qgen_tasks(hs[i + 1], 0)
                else:
                    nextq, nbufs = [], None
                if j == nslot - 1:
                    pass
                attend_slot(h, j, bufs, nextq, bg)
                bufs = nbufs
            while bg:
                bg.pop(0)()
        fw.es = old_es
        fw.end_phase()


def convert_weight_chunks(fw, o, drd):
    sem = fw.dsem("wconv", sw=True)
    fw.phase_dsems.remove(sem)
    chunks = []
    drd["wconv"] = []
    for src_nm, dst_nm, rows in [("w_o", "wo_b", 2048), ("w_gate", "wg_b", 2048), ("w_up", "wu_b", 2048), ("w_down", "wd_b", DFF)]:
        for r in range(0, rows, 512):
            n = min(512, rows - r)

            def issue(src_nm=src_nm, dst_nm=dst_nm, r=r, n=n):
                dep = Dep()
                drd["wconv"].append(dep)
                pairs = [(o[dst_nm][rr:rr + 128, :], o[src_nm][rr:rr + 128, :]) for rr in range(r, r + n, 128)]
                fw.dmas(fw.pool, pairs, sem, W=[dep])
            chunks.append(issue)
    return chunks


def phase_ffn(fw, PS, C, o, drd, nslot=NSLOT):
    pe, act, dve, pool, sp = fw.pe, fw.act, fw.dve, fw.pool, fw.sp
    ident = C["ident"]
    NF = DFF // 128
    es2 = ExitStack()
    with es2:
        old_es = fw.es
        fw.es = es2
        gt_ffn = T(fw.sb("gt_ffn", [128, D], F32), sem=fw.dsem("gt_ffn"))
        fw.dma(sp, gt_ffn[:], bcast_rows(o["ffn_g"], 128, D), gt_ffn.sem, W=[gt_ffn.d])
        gt_fin = T(fw.sb("gt_fin", [128, D], F32), sem=fw.dsem("gt_fin"))
        fw.dma(sp, gt_fin[:], bcast_rows(o["fin_g"], 128, D), gt_fin.sem, W=[gt_fin.d])
        x1 = [T(fw.sb(f"x1_{s}", [128, D], F32), sem=fw.dsem(f"x1_{s}", sw=True)) for s in range(4)]
        ah = T(fw.sb("ah", [128, 16, 512], BF16), sem=fw.dsem("ah"))
        actT = T(fw.sb("actT", [128, NF, 512], BF16))
        NW = 4
        wbuf = [T(fw.sb(f"wb{i}", [128, 8192], BF16), sem=fw.dsem(f"wb{i}")) for i in range(NW)]
        hb = [T(fw.sb(f"fhb{i}", [128, D], BF16)) for i in range(2)]
        junk = T(fw.sb("fjunk", [128, D], BF16))
        ssb = [T(fw.sb(f"fss{i}", [128, 2], F32)) for i in range(2)]
        ostg = [T(fw.sb(f"fo{i}", [128, D // 2], F32), sem=fw.dsem(f"fo{i}", sw=True)) for i in range(2)]
        sg = [T(fw.sb(f"sg{i}", [128, 512], F32)) for i in range(2)]
        ctr = dict(w=0, ss=0, hb=0, sg=0, o=0, ps=0)

        def wload(src_view, n_mid):
            wt = wbuf[ctr["w"] % NW]
            ctr["w"] += 1
            v = wt.ap[:, 0:n_mid * 512].rearrange("p (k c) -> p k c", c=512)
            fw.dma(sp, v, src_view, wt.sem, R=drd["wconv"], W=[wt.d])
            return wt, v

        def rms_rstd(xap, xdep):
            ss = ssb[ctr["ss"] % 2]
            ctr["ss"] += 1
            fw.op(act, lambda e: e.memzero(ss[:, 0:1]), W=[ss.d])
            fw.op(act, lambda e: e.activation(out=junk[:], in_=xap, func=AF.Square, accum_out=ss[:, 0:1]), R=[xdep], W=[junk.d, ss.d])
            fw.op(act, lambda e: e.activation(out=ss[:, 1:2], in_=ss[:, 0:1], func=AF.Sqrt, scale=1.0 / D, bias=C["eps"][:, 0:1]),
                  R=[C["eps"].d], W=[ss.d])
            fw.op(dve, lambda e: e.reciprocal(out=ss[:, 1:2], in_=ss[:, 1:2]), W=[ss.d])
            return ss

        for j in range(nslot):
            q0 = j * 512
            if j == 0:
                fw.dma(sp, ah[:], o["attnT"][:, q0:q0 + 512].rearrange("(k p) t -> p k t", p=128), ah.sem, R=[drd["attnT"]], W=[ah.d])
            if j == 0:
                for s in range(4):
                    fw.dma(pool, x1[s][:], o["xq"][q0 + s * 128:q0 + (s + 1) * 128, :], x1[s].sem, W=[x1[s].d])
            for cb in range(4):
                wt, wv = wload(o["wo_b"][:, cb * 512:(cb + 1) * 512].rearrange("(k p) c -> p k c", p=128), 16)
                for s in range(4):
                    p = PS[ctr["ps"] % 8]
                    ctr["ps"] += 1
                    for kc in range(16):
                        fw.op(pe, lambda e: e.matmul(p[:], lhsT=ah[:, kc, s * 128:(s + 1) * 128], rhs=wv[:, kc, :],
                                                     start=(kc == 0), stop=(kc == 15)), R=[ah.d, wt.d], W=[p.d])
                    fw.op(dve, lambda e: e.tensor_tensor(out=x1[s][:, cb * 512:(cb + 1) * 512], in0=p[:],
                                                          in1=x1[s][:, cb * 512:(cb + 1) * 512], op=ALU.add), R=[p.d], W=[x1[s].d])
            for s in range(4):
                ss = rms_rstd(x1[s][:], x1[s].d)
                hbb = hb[ctr["hb"] % 2]
                ctr["hb"] += 1
                fw.op(dve, lambda e: e.scalar_tensor_tensor(out=hbb[:], in0=x1[s][:], scalar=ss[:, 1:2], in1=gt_ffn[:],
                                                             op0=ALU.mult, op1=ALU.mult), R=[x1[s].d, ss.d, gt_ffn.d], W=[hbb.d])
                for half in range(2):
                    pt = PS[ctr["ps"] % 8]
                    ctr["ps"] += 1
                    ptb = pt.ap.bitcast(BF16)
                    for jj in range(8):
                        kc = half * 8 + jj
                        fw.op(pe, lambda e: e.transpose(out=ptb[:, jj * 128:(jj + 1) * 128], in_=hbb[:, kc * 128:(kc + 1) * 128],
                                                        identity=ident[:]), R=[hbb.d, ident.d], W=[pt.d])
                    dst = ah[:, half * 8:(half + 1) * 8, s * 128:(s + 1) * 128]
                    srcv = ptb.rearrange("p (j t) -> p j t", j=8)
                    if half == 0:
                        fw.op(act, lambda e: e.copy(out=dst, in_=srcv), R=[pt.d], W=[ah.d])
                    else:
                        fw.op(dve, lambda e: e.tensor_copy(out=dst, in_=srcv), R=[pt.d], W=[ah.d])
            for fg in range(NF // 4):
                wgt, wgv = wload(o["wg_b"][:, fg * 512:(fg + 1) * 512].rearrange("(k p) c -> p k c", p=128), 16)
                wut, wuv = wload(o["wu_b"][:, fg * 512:(fg + 1) * 512].rearrange("(k p) c -> p k c", p=128), 16)
                for fi in range(4):
                    f = fg * 4 + fi
                    pg = PS[ctr["ps"] % 8]
                    pu = PS[(ctr["ps"] + 1) % 8]
                    ctr["ps"] += 2
                    for kc in range(16):
                        fw.op(pe, lambda e: e.matmul(pg[:], lhsT=wgv[:, kc, fi * 128:(fi + 1) * 128], rhs=ah[:, kc, :],
                                                     start=(kc == 0), stop=(kc == 15)), R=[wgt.d, ah.d], W=[pg.d])
                    for kc in range(16):
                        fw.op(pe, lambda e: e.matmul(pu[:], lhsT=wuv[:, kc, fi * 128:(fi + 1) * 128], rhs=ah[:, kc, :],
                                                     start=(kc == 0), stop=(kc == 15)), R=[wut.d, ah.d], W=[pu.d])
                    sgt = sg[ctr["sg"] % 2]
                    ctr["sg"] += 1
                    fw.op(act, lambda e: e.activation(out=sgt[:], in_=pg[:], func=AF.Silu), R=[pg.d], W=[sgt.d])
                    fw.op(dve, lambda e: e.tensor_tensor(out=actT[:, f, :], in0=pu[:], in1=sgt[:], op=ALU.mult),
                          R=[pu.d, sgt.d], W=[actT.d])
            for cb in range(4):
                banks = [PS[(cb % 2) * 4 + s] for s in range(4)]
                for f0 in range(0, NF, 16):
                    nf = min(16, NF - f0)
                    wt, wv = wload(o["wd_b"][f0 * 128:(f0 + nf) * 128, cb * 512:(cb + 1) * 512].rearrange("(f p) c -> p f c", p=128), nf)
                    if cb == 0 and f0 == 16 and j + 1 < nslot:
                        fw.dma(sp, ah[:], o["attnT"][:, q0 + 512:q0 + 1024].rearrange("(k p) t -> p k t", p=128), ah.sem, R=[drd["attnT"]], W=[ah.d])
                    for fi in range(nf):
                        f = f0 + fi
                        for s in range(4):
                            p = banks[s]
                            fw.op(pe, lambda e: e.matmul(p[:], lhsT=actT[:, f, s * 128:(s + 1) * 128], rhs=wv[:, fi, :],
                                                         start=(f == 0), stop=(f == NF - 1)), R=[actT.d, wt.d], W=[p.d])
                for s in range(4):
                    p = banks[s]
                    fw.op(dve, lambda e: e.tensor_tensor(out=x1[s][:, cb * 512:(cb + 1) * 512], in0=p[:],
                                                          in1=x1[s][:, cb * 512:(cb + 1) * 512], op=ALU.add), R=[p.d], W=[x1[s].d])
            for s in range(4):
                ss = rms_rstd(x1[s][:], x1[s].d)
                for hf in range(2):
                    og = ostg[ctr["o"] % 2]
                    ctr["o"] += 1
                    cs = slice(hf * 1024, (hf + 1) * 1024)
                    fw.op(dve, lambda e: e.scalar_tensor_tensor(out=og[:], in0=x1[s][:, cs], scalar=ss[:, 1:2], in1=gt_fin[:, cs],
                                                                 op0=ALU.mult, op1=ALU.mult), R=[x1[s].d, ss.d, gt_fin.d], W=[og.d])
                    fw.dma(pool, o["out"][q0 + s * 128:q0 + (s + 1) * 128, cs], og[:], og.sem, R=[og.d], W=[drd["out"]])
                if j + 1 < nslot:
                    fw.dma(pool, x1[s][:], o["xq"][q0 + 512 + s * 128:q0 + 512 + (s + 1) * 128, :], x1[s].sem, W=[x1[s].d])
        fw.es = old_es
        fw.end_phase()


def phase_cmp(fw, PS, C, o, drd, G):
    pe, act, dve, pool, sp = fw.pe, fw.act, fw.dve, fw.pool, fw.sp
    NCMP = SEQ // 16 - 1
    NBC = (NCMP + 127) // 128
    es2 = ExitStack()
    with es2:
        old_es = fw.es
        fw.es = es2
        for X, (srcT, w1n, w2n, posn) in enumerate([("kcT", "w1k", "w2k", "posk"), ("vcT", "w1v", "w2v", "posv")]):
            xt = T(fw.sb(f"cx{X}", [128, SEQ], BF16), sem=fw.dsem(f"cx{X}"))
            fw.dma(sp, xt[:], o[srcT], xt.sem, R=[drd[srcT]], W=[xt.d])
            w1 = T(fw.sb(f"cw1{X}", [128, 32, 128], BF16), sem=fw.dsem(f"cw1{X}", sw=True))
            fw.dmas(pool, [(w1[0:64], o[w1n]), (w1[64:128], o[w1n])], w1.sem, W=[w1.d])
            w2 = T(fw.sb(f"cw2{X}", [128, 64], BF16), sem=fw.dsem(f"cw2{X}", sw=True))
            fw.dma(pool, w2[:], o[w2n], w2.sem, W=[w2.d])
            pos = T(fw.sb(f"cpos{X}", [64, 32], BF16), sem=fw.dsem(f"cpos{X}", sw=True))
            fw.dma(pool, pos[:], o[posn], pos.sem, W=[pos.d])
            bias = T(fw.sb(f"cbias{X}", [128, 1], F32))
            pb = PS[7]
            for l in range(32):
                fw.op(pe, lambda e: e.matmul(pb[:, 0:1], lhsT=w1[0:64, l, :], rhs=pos[:, l:l + 1], start=(l == 0), stop=(l == 31)),
                      R=[w1.d, pos.d], W=[pb.d])
            fw.op(dve, lambda e: e.tensor_copy(out=bias[:], in_=pb[:, 0:1]), R=[pb.d], W=[bias.d])
            for g in range(2):
                ph = PS[g]
                for l in range(32):
                    rhs = xt[g * 64:(g + 1) * 64, l:l + 16 * (NCMP - 1) + 1:16]
                    fw.op(pe, lambda e: e.matmul(ph[:, 0:NCMP], lhsT=w1[g * 64:(g + 1) * 64, l, :], rhs=rhs, start=(l == 0), stop=(l == 31)),
                          R=[w1.d, xt.d], W=[ph.d])
                hs = T(fw.sb(f"chs{X}{g}", [128, 128 * NBC], BF16))
                fw.op(dve, lambda e: e.memset(hs[:], 0.0), W=[hs.d])
                fw.op(act, lambda e: e.activation(out=hs[:, 0:NCMP], in_=ph[:, 0:NCMP], func=AF.Silu, bias=bias[:, 0:1]),
                      R=[ph.d, bias.d], W=[hs.d])
                if X == 0:
                    kc = G["kcmpT"][g]
                    p2 = PS[2 + g]
                    fw.op(pe, lambda e: e.matmul(p2[0:64, 0:128 * NBC], lhsT=w2[:, :], rhs=hs[:, :], start=True, stop=True),
                          R=[w2.d, hs.d], W=[p2.d])
                    fw.op(dve, lambda e: e.tensor_copy(out=kc[0:64, 0:128 * NBC], in_=p2[0:64, 0:128 * NBC]), R=[p2.d], W=[kc.d])
                else:
                    vc = G["vcmp"][g]
                    p2 = PS[4 + g]
                    for nb in range(NBC):
                        fw.op(pe, lambda e: e.matmul(p2[:, nb * 64:(nb + 1) * 64], lhsT=hs[:, nb * 128:(nb + 1) * 128], rhs=w2[:, :],
                                                     start=True, stop=True), R=[w2.d, hs.d], W=[p2.d])
                    fw.op(dve, lambda e: e.tensor_copy(out=vc[:, 0:NBC, 0:64], in_=p2[:, 0:NBC * 64].rearrange("p (n d) -> p n d", d=64)),
                          R=[p2.d], W=[vc.d])
        fw.es = old_es
        fw.end_phase()


def phase_nsa(fw, PS, C, o, drd, G, nslot=NSLOT, groups=(0, 1), heads=range(8), bg=()):
    pe, act, dve, pool, sp = fw.pe, fw.act, fw.dve, fw.pool, fw.sp
    ones, ident = C["ones"], C["ident"]
    nkb_all = SEQ // 128
    es2 = ExitStack()
    with es2:
        old_es = fw.es
        fw.es = es2

        def cload(name, shape, dt, src, q=None, sw=False):
            t = T(fw.sb(name, shape, dt), sem=fw.dsem(name, sw=sw))
            fw.dma(pool if sw else sp, t[:], src, t.sem, W=[t.d])
            return t
        ovl = cload("ovl", [128, 4, 128], BF16, o["ovl"])
        inds = cload("inds4", [128, 64, 128], BF16, o["inds4"])
        oh = cload("oh", [48, 48, 64], BF16, o["oh"])
        bias_s = cload("bias_s", [128, 16 * 64], F32, o["bias_s"])
        bias_w = cload("bias_w", [128, 16 * 12], F32, o["bias_w"])
        bias_c = cload("bias_c", [128, 16 * 8 * 4], F32, o["bias_c"])
        vtab = [cload(f"vtab{i}", [128, 264], F32, o[f"vtab{'AB'[i]}"]) for i in range(2)]
        atab = [cload(f"atab{i}", [128, 264], F32, o[f"atab{'AB'[i]}"]) for i in range(2)]
        cmpneg = [cload(f"cmpneg{i}", [128, 2, 512], BF16, o[f"cmpneg{'AB'[i]}"]) for i in range(2)]
        kw = T(fw.sb("kw_sb", [128, 1536], BF16), sem=fw.dsem("kw_sb"))
        fw.op(pool, lambda e: e.memset(kw[64:128, :], 0.0), W=[kw.d])
        Vw = T(fw.sb("Vw_sb", [128, 12, 128], BF16), sem=fw.dsem("Vw_sb"))
        fw.op(pool, lambda e: e.memset(Vw[:, :, 64:128], 1.0), W=[Vw.d])
        qa = T(fw.sb("qa", [128, 8, 512], BF16), sem=fw.dsem("qa"))
        fw.op(pool, lambda e: e.memset(qa[:], 0.0), W=[qa.d])
        cneg = T(fw.sb("cneg", [128, 8, 512], BF16), sem=fw.dsem("cneg"))
        wneg = T(fw.sb("wneg", [128, 12, 512], BF16), sem=fw.dsem("wneg"))
        masks = T(fw.sb("masks", [128, max(nkb_all - 8, 1), 512], BF16))
        NPT = 6
        pts = [T(fw.sb(f"npt{i}", [128, 512], BF16)) for i in range(NPT)]
        reccs = [T(fw.sb(f"recc{i}", [128, 512], F32)) for i in range(2)]
        oc_sb = T(fw.sb("oc_sb", [64, 8, 512], BF16))
        imp_sb = T(fw.sb("imp_sb", [128, 128], F32))
        work = T(fw.sb("selwork", [128, 128], F32))
        m8 = T(fw.sb("m8", [128, 16], F32))
        sel = T(fw.sb("sel", [128, 128], BF16))
        selT = T(fw.sb("selT", [128, 512], BF16))
        gh = T(fw.sb("gh", [48, 512], BF16), sem=fw.dsem("gh"))
        gl = T(fw.sb("gl", [48, 512], BF16), sem=fw.dsem("gl"))
        gb = [T(fw.sb(f"gb{i}", [64, 512], F32)) for i in range(3)]
        rs = T(fw.sb("nrs", [64, 512], F32))
        tt = T(fw.sb("ntt", [64, 512], F32))
        acc = T(fw.sb("nacc", [64, 512], F32))
        ost = [T(fw.sb(f"nost{i}", [64, 512], BF16), sem=fw.dsem(f"nost{i}", sw=True)) for i in range(2)]
        ctr = dict(pt=0, ss=0, ost=0, acc=0, s=0)
        PS_S = [PS[0], PS[1], PS[2]]
        PS_MISC = PS[7]

        def next_s():
            p = PS_S[ctr["s"] % 3]
            ctr["s"] += 1
            return p

        def next_pt():
            p = pts[ctr["pt"] % NPT]
            ctr["pt"] += 1
            return p

        for g in groups:
            ksT, Vs = G["ksT"], G["Vs"]
            if g != groups[0]:
                G["load_kv"](g)
            kcm, vcm = G["kcmpT"][g], G["vcmp"][g]
            for j in range(nslot):
                q0 = j * 512
                par = j % 2
                nkb = 8 * (j + 1)
                nbc = (j + 2) // 2
                w0 = 4 if j == 0 else 0
                fw.dma(sp, qa[0:64, :, :], o["qnT"][g * 512:(g + 1) * 512, q0:q0 + 512].rearrange("(h d) t -> d h t", d=64), qa.sem,
                       R=[drd["qnT"]], W=[qa.d])
                fw.dma(sp, qa[64:70, :, :], o["qaug" + "AB"[par]][:, g * 8:(g + 1) * 8, :], qa.sem, W=[qa.d])
                fw.dma(sp, cneg[:], o["cneg" + "AB"[par]], cneg.sem, W=[cneg.d])
                fw.dma(sp, wneg[:], o["wneg" + "AB"[par]], wneg.sem, W=[wneg.d])
                kb0 = 8 * j - 4 + w0
                nw = 12 - w0
                fw.dma(sp, kw[0:64, w0 * 128:1536], o["kwT"][g * 64:(g + 1) * 64, kb0 * 128:(kb0 + nw) * 128], kw.sem,
                       R=[drd["kwT"]], W=[kw.d])
                fw.dma(sp, kw[64:70, :], o["kaug"][:, 0:1536], kw.sem, W=[kw.d])
                fw.dma(sp, Vw[:, w0:12, 0:64],
                       o["vsw"][kb0 * 128:(kb0 + nw) * 128, 128 + g * 64:128 + (g + 1) * 64].rearrange("(k p) d -> p k d", p=128),
                       Vw.sem, R=[drd["vsw"]], W=[Vw.d])
                fw.dma(sp, gh[:], o["gTh"][:, q0:q0 + 512], gh.sem, R=[drd["gTh"]], W=[gh.d])
                fw.dma(sp, gl[:], o["gTl"][:, q0:q0 + 512], gl.sem, R=[drd["gTl"]], W=[gl.d])
                for _ in range(3):
                    if bg:
                        bg.pop(0)()
                pimp = PS[5]
                fw.op(dve, lambda e: e.memset(pimp[:], 0.0), W=[pimp.d])
                for hi_, h in enumerate(heads):
                    h16 = g * 8 + h
                    pD, pO = (PS[3], PS[4]) if hi_ % 2 == 0 else (PS[6], PS[7])
                    recc = reccs[hi_ % 2]
                    pcs = []
                    for nb in range(nbc):
                        ps = next_s()
                        wl = nb - (nbc - 2)
                        fw.op(pe, lambda e: e.matmul(ps[:], lhsT=kcm[:, nb * 128:(nb + 1) * 128], rhs=qa[:, h, :], start=True, stop=(wl < 0)),
                              R=[kcm.d, qa.d], W=[ps.d])
                        if wl >= 0:
                            fw.op(pe, lambda e: e.matmul(ps[:], lhsT=ident[:], rhs=cmpneg[par][:, wl, :], start=False, stop=True),
                                  R=[ident.d, cmpneg[par].d], W=[ps.d])
                        bcol = (h16 * 8 + j) * 4 + nb
                        pt = next_pt()
                        fw.op(act, lambda e: e.activation(out=pt[:], in_=ps[:], func=AF.Exp, scale=0.125, bias=bias_c[:, bcol:bcol + 1]),
                              R=[ps.d, bias_c.d], W=[pt.d])
                        pcs.append(pt)
                    for nb in range(nbc):
                        fw.op(pe, lambda e: e.matmul(pD[:], lhsT=ones[:], rhs=pcs[nb][:], start=(nb == 0), stop=(nb == nbc - 1)),
                              R=[ones.d, pcs[nb].d], W=[pD.d])
                    fw.op(act, lambda e: e.activation(out=recc[:], in_=pD[:], func=AF.Ln, bias=C["tiny"][:, 0:1]), R=[pD.d, C["tiny"].d], W=[recc.d])
                    fw.op(act, lambda e: e.activation(out=recc[:], in_=recc[:], func=AF.Exp, scale=-1.0), W=[recc.d])
                    for nb in range(nbc):
                        fw.op(dve, lambda e: e.tensor_tensor(out=pcs[nb][:], in0=pcs[nb][:], in1=recc[:], op=ALU.mult), R=[recc.d], W=[pcs[nb].d])
                    for nb in range(nbc):
                        fw.op(pe, lambda e: e.matmul(pO[:], lhsT=vcm[:, nb, :], rhs=pcs[nb][:], start=(nb == 0), stop=(nb == nbc - 1)),
                              R=[vcm.d, pcs[nb].d], W=[pO.d])
                    for qb in range(4):
                        for nb in range(nbc):
                            fw.op(pe, lambda e: e.matmul(pimp[:, qb * 128:(qb + 1) * 128], lhsT=pcs[nb][:, qb * 128:(qb + 1) * 128],
                                                         rhs=ovl[:, nb, :], start=False, stop=False, skip_group_check=True),
                                  R=[ovl.d, pcs[nb].d], W=[pimp.d])
                    fw.op(act, lambda e: e.copy(out=oc_sb[:, h, :], in_=pO[0:64, :]), R=[pO.d], W=[oc_sb.d])
                pT = PS_MISC
                pTb = pT.ap.bitcast(BF16)
                for qb in range(4):
                    s0 = 134 - 16 * j - 2 * qb
                    fw.op(dve, lambda e: e.tensor_tensor(out=imp_sb[:], in0=pimp[:, qb * 128:(qb + 1) * 128], in1=vtab[par][:, s0:s0 + 128],
                                                          op=ALU.mult), R=[pimp.d, vtab[par].d], W=[imp_sb.d])
                    fw.op(dve, lambda e: e.tensor_tensor(out=imp_sb[:], in0=imp_sb[:], in1=atab[par][:, s0:s0 + 128], op=ALU.add),
                          R=[atab[par].d], W=[imp_sb.d])
                    fw.op(dve, lambda e: e.memset(imp_sb[:, 0:1], 1e4), W=[imp_sb.d])
                    fw.op(dve, lambda e: e.max(out=m8[:, 0:8], in_=imp_sb[:]), R=[imp_sb.d], W=[m8.d])
                    fw.op(dve, lambda e: e.match_replace(out=work[:], in_to_replace=m8[:, 0:8], in_values=imp_sb[:], imm_value=-1e9),
                          R=[imp_sb.d, m8.d], W=[work.d])
                    fw.op(dve, lambda e: e.max(out=m8[:, 8:16], in_=work[:]), R=[work.d], W=[m8.d])
                    fw.op(dve, lambda e: e.tensor_scalar(out=sel[:], in0=imp_sb[:], scalar1=m8[:, 15:16], scalar2=1.0, op0=ALU.is_ge, op1=ALU.subtract),
                          R=[imp_sb.d, m8.d], W=[sel.d])
                    fw.op(pe, lambda e: e.transpose(out=pTb[:, qb * 128:(qb + 1) * 128], in_=sel[:], identity=ident[:]),
                          R=[sel.d, ident.d], W=[pT.d])
                fw.op(act, lambda e: e.copy(out=selT[:], in_=pTb[:, 0:512]), R=[pT.d], W=[selT.d])
                for kb in range(nkb - 8):
                    pm = PS[3 + kb % 2]
                    fw.op(pe, lambda e: e.matmul(pm[:], lhsT=inds[:, kb, :], rhs=selT[:, :], start=True, stop=True),
                          R=[inds.d, selT.d], W=[pm.d])
                    if kb % 2:
                        fw.op(act, lambda e: e.activation(out=masks[:, kb, :], in_=pm[:], func=AF.Identity, scale=2.0 ** -30, bias=C["one"][:, 0:1]),
                              R=[pm.d, C["one"].d], W=[masks.d])
                    else:
                        fw.op(dve, lambda e: e.tensor_scalar(out=masks[:, kb, :], in0=pm[:], scalar1=2.0 ** -30, scalar2=1.0, op0=ALU.mult, op1=ALU.add),
                              R=[pm.d], W=[masks.d])
                for h in heads:
                    h16 = g * 8 + h
                    pS_ = PS[3 + 2 * (ctr["acc"] % 2)]
                    pW_ = PS[4 + 2 * (ctr["acc"] % 2)]
                    ctr["acc"] += 1
                    units = [("w", w) for w in range(w0, 12)] + [("s", kb) for kb in range(nkb)]

                    def qk(u):
                        kind, i = u
                        ps = next_s()
                        if kind == "w":
                            fw.op(pe, lambda e: e.matmul(ps[:], lhsT=kw[:, i * 128:(i + 1) * 128], rhs=qa[:, h, :], start=True, stop=False),
                                  R=[kw.d, qa.d], W=[ps.d])
                            fw.op(pe, lambda e: e.matmul(ps[:], lhsT=ident[:], rhs=wneg[:, i, :], start=False, stop=True),
                                  R=[ident.d, wneg.d], W=[ps.d])
                        else:
                            a = i // 32
                            w = i - (nkb - 8)
                            fw.op(pe, lambda e: e.matmul(ps[:], lhsT=ksT[:, i * 128:(i + 1) * 128], rhs=qa[:, h, :], start=True, stop=(w < 0)),
                                  R=[ksT.d, qa.d], W=[ps.d])
                            if w >= 0:
                                fw.op(pe, lambda e: e.matmul(ps[:], lhsT=inds[:, i, :], rhs=selT[:, :], start=False, stop=False),
                                      R=[inds.d, selT.d], W=[ps.d])
                                fw.op(pe, lambda e: e.matmul(ps[:], lhsT=ident[:], rhs=cneg[:, w, :], start=False, stop=True),
                                      R=[ident.d, cneg.d], W=[ps.d])
                        return ps

                    def pv(u, ps, first, last):
                        kind, i = u
                        pt = next_pt()
                        if kind == "w":
                            bcol = h16 * 12 + i
                            fw.op(act, lambda e: e.activation(out=pt[:], in_=ps[:], func=AF.Exp, scale=0.125, bias=bias_w[:, bcol:bcol + 1]),
                                  R=[ps.d, bias_w.d], W=[pt.d])
                            fw.op(pe, lambda e: e.matmul(pW_[:], lhsT=Vw[:, i, :], rhs=pt[:], start=first, stop=last), R=[Vw.d, pt.d], W=[pW_.d])
                        else:
                            bcol = h16 * 64 + (8 * j - i + 7)
                            fw.op(act, lambda e: e.activation(out=pt[:], in_=ps[:], func=AF.Exp, scale=0.125, bias=bias_s[:, bcol:bcol + 1]),
                                  R=[ps.d, bias_s.d], W=[pt.d])
                            if i < nkb - 8:
                                eng_ = pool if (i % 3 == 2) else dve
                                fw.op(eng_, lambda e: e.tensor_tensor(out=pt[:], in0=pt[:], in1=masks[:, i, :], op=ALU.mult), R=[masks.d], W=[pt.d])
                            fw.op(pe, lambda e: e.matmul(pS_[:], lhsT=Vs[:, i, :], rhs=pt[:], start=first, stop=last), R=[Vs.d, pt.d], W=[pS_.d])

                    nwin = 12 - w0
                    pend = []
                    pend.append(qk(units[0]))
                    if len(units) > 1:
                        pend.append(qk(units[1]))
                    for ui, u in enumerate(units):
                        if ui + 2 < len(units):
                            pend.append(qk(units[ui + 2]))
                        ps = pend.pop(0)
                        if u[0] == "w":
                            pv(u, ps, ui == 0, ui == nwin - 1)
                        else:
                            pv(u, ps, ui == nwin, ui == len(units) - 1)
                    for b in range(3):
                        r = h16 * 3 + b
                        pg = PS_MISC
                        fw.op(pe, lambda e: e.matmul(pg[0:64, :], lhsT=oh[:, r, :], rhs=gh[:], start=True, stop=False), R=[oh.d, gh.d], W=[pg.d])
                        fw.op(pe, lambda e: e.matmul(pg[0:64, :], lhsT=oh[:, r, :], rhs=gl[:], start=False, stop=True), R=[oh.d, gl.d], W=[pg.d])
                        fw.op(act, lambda e: e.copy(out=gb[b][:], in_=pg[0:64, :]), R=[pg.d], W=[gb[b].d])
                    fw.op(dve, lambda e: e.tensor_tensor(out=acc[:], in0=oc_sb[:, h, :], in1=gb[0][:], op=ALU.mult), R=[oc_sb.d, gb[0].d], W=[acc.d])
                    for b, pacc in ((1, pS_), (2, pW_)):
                        fw.op(dve, lambda e: e.reciprocal(out=rs[:], in_=pacc[64:128, :]), R=[pacc.d], W=[rs.d])
                        fw.op(dve, lambda e: e.tensor_tensor(out=rs[:], in0=rs[:], in1=gb[b][:], op=ALU.mult), R=[gb[b].d], W=[rs.d])
                        fw.op(dve, lambda e: e.tensor_tensor(out=tt[:], in0=pacc[0:64, :], in1=rs[:], op=ALU.mult), R=[pacc.d, rs.d], W=[tt.d])
                        if b == 1:
                            fw.op(dve, lambda e: e.tensor_tensor(out=acc[:], in0=acc[:], in1=tt[:], op=ALU.add), R=[tt.d], W=[acc.d])
                        else:
                            og = ost[ctr["ost"] % 2]
                            ctr["ost"] += 1
                            fw.op(dve, lambda e: e.tensor_tensor(out=og[:], in0=acc[:], in1=tt[:], op=ALU.add), R=[acc.d, tt.d], W=[og.d])
                            fw.dma(pool, o["attnT"][1024 + h16 * 64:1024 + (h16 + 1) * 64, q0:q0 + 512], og[:], og.sem, R=[og.d], W=[drd["attnT"]])
        fw.es = old_es
        fw.end_phase()


def build(dbg=(), phases=("f", "k", "q", "c", "n", "m")):
    nc = bass.Bass("TRN2", target_bir_lowering=False)
    es = ExitStack()
    IN = {}

    def din(name, shape, dt=F32):
        IN[name] = nc.dram_tensor(name, list(shape), dt, kind="ExternalInput").ap()
        return IN[name]

    drd = {}

    def dscr(name, shape, dt):
        kind = "ExternalOutput" if name in dbg else "Internal"
        drd[name] = Dep()
        return nc.dram_tensor(name, list(shape), dt, kind=kind).ap()

    xs = din("xs", [SEQ, D])
    xq = din("xq", [NQ, D])
    attn_g = din("attn_g", [1, D])
    wk = din("wk", [D, 1152])
    wq = din("wq", [D, 1584])
    gkv = din("gkv", [128, 2])
    gq = din("gq", [128, 4])
    cosk = din("cosk", [64, SEQ])
    sink = din("sink", [64, SEQ])
    identd = din("identd", [128, 128], BF16)
    o_extra = dict(w_uq=din("w_uq", [512, 2048]), w_uk=din("w_uk", [256, 1024]), w_uv=din("w_uv", [256, 1024]),
                   cosq=din("cosq", [64, NQ]), sinq=din("sinq", [64, NQ]),
                   cmaskA=din("cmaskA", [128, 8, 512], BF16), cmaskB=din("cmaskB", [128, 8, 512], BF16))
    out = nc.dram_tensor("out", [NQ, D], F32, kind="ExternalOutput").ap()
    drd["out"] = Dep()

    o = dict(gkv=gkv, gq=gq, cosk=cosk, sink=sink, xq=xq, out=out)
    o.update(o_extra)
    o.update(dict(w_o=din("w_o", [2048, 2048]), w_gate=din("w_gate", [2048, DFF]), w_up=din("w_up", [2048, DFF]),
                  w_down=din("w_down", [DFF, 2048]), ffn_g=din("ffn_g", [1, D]), fin_g=din("fin_g", [1, D])))
    for nm, shp, dt in [("w1k", [64, 32, 128], F32), ("w1v", [64, 32, 128], F32), ("w2k", [128, 64], F32), ("w2v", [128, 64], F32),
                        ("posk", [64, 32], F32), ("posv", [64, 32], F32), ("kaug", [6, max(SEQ, 1536)], BF16), ("caug", [6, 512], BF16),
                        ("qaugA", [6, 16, 512], BF16), ("qaugB", [6, 16, 512], BF16), ("ovl", [128, 4, 128], BF16),
                        ("inds4", [128, 64, 128], BF16), ("oh", [48, 48, 64], BF16), ("bias_s", [128, 16 * 64], F32),
                        ("bias_w", [128, 16 * 12], F32), ("bias_c", [128, 16 * 8 * 4], F32),
                        ("cnegA", [128, 8, 512], BF16), ("cnegB", [128, 8, 512], BF16),
                        ("wnegA", [128, 12, 512], BF16), ("wnegB", [128, 12, 512], BF16),
                        ("cmpnegA", [128, 2, 512], BF16), ("cmpnegB", [128, 2, 512], BF16),
                        ("vtabA", [128, 264], F32), ("vtabB", [128, 264], F32), ("atabA", [128, 264], F32), ("atabB", [128, 264], F32)]:
        o[nm] = din(nm, shp, dt)
    for nm, shp in [("wo_b", [2048, 2048]), ("wg_b", [2048, DFF]), ("wu_b", [2048, DFF]), ("wd_b", [DFF, 2048])]:
        o[nm] = dscr(nm, shp, BF16)
    for nm, shp in [("ckvT", [256, SEQ]), ("kropeT", [64, SEQ]), ("kcT", [128, SEQ]), ("vcT", [128, SEQ]),
                    ("ksT", [128, SEQ]), ("kwT", [128, SEQ]), ("vsw", [SEQ, 256]), ("cqT", [512, NQ]),
                    ("qnT", [1024, NQ]), ("gTh", [48, NQ]), ("gTl", [48, NQ]), ("attnT", [2048, NQ])]:
        o[nm] = dscr(nm, shp, BF16)

    with es:
        fw = FW(nc, es)
        pe, act, dve, pool, sp = fw.pe, fw.act, fw.dve, fw.pool, fw.sp
        PS = [T(fw.ps(f"ps{i}", [128, 512], F32), dep=Dep(excl=True)) for i in range(8)]
        ident = T(fw.sb("ident", [128, 128], BF16), sem=fw.dsem("ident"))
        ones = T(fw.sb("ones", [128, 128], BF16))
        fw.dma(sp, ident[:], identd, ident.sem, W=[ident.d])
        fw.op(pool, lambda e: e.memset(ones[:], 1.0), W=[ones.d])
        epst = T(fw.sb("epst", [128, 1], F32))
        fw.op(pool, lambda e: e.memset(epst[:], EPS), W=[epst.d])
        tinyt = T(fw.sb("tinyt", [128, 1], F32))
        fw.op(pool, lambda e: e.memset(tinyt[:], 1e-30), W=[tinyt.d])
        onet = T(fw.sb("onet", [128, 1], F32))
        fw.op(pool, lambda e: e.memset(onet[:], 1.0), W=[onet.d])
        C = dict(ident=ident, ones=ones, eps=epst, tiny=tinyt, one=onet)

        if "k" in phases:
            phase_proj(fw, PS, C, xs, SEQ, attn_g, wk, 1152, "k", o, drd)
        if "q" in phases:
            phase_proj(fw, PS, C, xq, NQ, attn_g, wq, 1584, "q", o, drd)

        conv_chunks = convert_weight_chunks(fw, o, drd) if "f" in phases else []
        if "c" in phases or "n" in phases:
            G = dict(kcmpT=[T(fw.sb(f"kcmpT{g}", [128, 512], BF16), sem=fw.dsem(f"kcmpT{g}")) for g in range(2)],
                     vcmp=[T(fw.sb(f"vcmp{g}", [128, 4, 128], BF16)) for g in range(2)])
            for g in range(2):
                fw.op(pool, lambda e: e.memset(G["kcmpT"][g][:, :], 0.0), W=[G["kcmpT"][g].d])
                fw.dma(sp, G["kcmpT"][g][64:70, :], o["caug"], G["kcmpT"][g].sem, W=[G["kcmpT"][g].d])
                fw.op(pool, lambda e: e.memset(G["vcmp"][g][:, :, 0:64], 0.0), W=[G["vcmp"][g].d])
                fw.op(pool, lambda e: e.memset(G["vcmp"][g][:, :, 64:128], 1.0), W=[G["vcmp"][g].d])
        es_kv = ExitStack()
        if "n" in phases:
            fw.es = es_kv
            nkb_all = SEQ // 128
            ksT = T(fw.sb("ksT_sb", [128, SEQ], BF16), sem=fw.dsem("ksT_sb"))
            Vs = T(fw.sb("Vs_sb", [128, nkb_all, 128], BF16), sem=fw.dsem("Vs_sb"))
            G["ksT"], G["Vs"] = ksT, Vs
            fw.es = es
            fw.phase_dsems.remove(ksT.sem)
            fw.phase_dsems.remove(Vs.sem)
            fw.op(pool, lambda e: e.memset(ksT[64:128, :], 0.0), W=[ksT.d])
            fw.op(pool, lambda e: e.memset(Vs[:, :, 64:128], 1.0), W=[Vs.d])
            fw.dma(sp, ksT[64:70, :], o["kaug"][:, 0:SEQ], ksT.sem, W=[ksT.d])

            def load_group_kv(g):
                fw.dma(sp, ksT[0:64, :], o["ksT"][g * 64:(g + 1) * 64, :], ksT.sem, R=[drd["ksT"]], W=[ksT.d])
                fw.dma(sp, Vs[:, :, 0:64], o["vsw"][:, g * 64:(g + 1) * 64].rearrange("(k p) d -> p k d", p=128), Vs.sem,
                       R=[drd["vsw"]], W=[Vs.d])
            G["load_kv"] = load_group_kv
            load_group_kv(NSA_GROUPS_RUN[0])
        if "c" in phases:
            phase_cmp(fw, PS, C, o, drd, G)
        if "n" in phases:
            phase_nsa(fw, PS, C, o, drd, G, nslot=NSA_NSLOT_RUN, groups=NSA_GROUPS_RUN, heads=NSA_HEADS_RUN, bg=conv_chunks)
            es_kv.close()
        for c_ in conv_chunks:
            c_()
        conv_chunks.clear()
        if "m" in phases:
            phase_mla(fw, PS, C, o, drd, heads=MLA_HEADS_RUN, nslot=MLA_NSLOT_RUN)

        if "f" in phases:
            phase_ffn(fw, PS, C, o, drd, nslot=FFN_NSLOT_RUN)

        alld = []
        for v_ in drd.values():
            alld.extend(v_ if isinstance(v_, list) else [v_])
        fw.wait_all(sp, alld)
        fw.barrier()
    return nc


def own_token_index(half):
    return np.concatenate([np.arange(c * 512, (c + 1) * 512) for c in OWN_CHUNKS[half]])


def split3(x):
    x = x.astype(np.float32)
    a = x.astype(ml_dtypes.bfloat16)
    r = (x - a.astype(np.float32)).astype(np.float32)
    b = r.astype(ml_dtypes.bfloat16)
    c = (r - b.astype(np.float32)).astype(np.float32).astype(ml_dtypes.bfloat16)
    return a, b, c


def nsa_tables(half):
    bf = ml_dtypes.bfloat16
    NEGM = np.float32(-1e9)
    slopes = (2.0 ** (-8.0 * np.arange(1, 17, dtype=np.float32) / 16)).astype(np.float32)
    t = {}
    nka = max(SEQ, 1536)
    rk = (np.arange(nka) % 128).astype(np.float32)
    t["kaug"] = np.stack([np.ones(nka, np.float32)] * 3 + [-rk] * 3).astype(bf)
    rn = (16 * (np.arange(512) % 128)).astype(np.float32)
    t["caug"] = np.stack([np.ones(512, np.float32)] * 3 + [-rn] * 3).astype(bf)
    ch = (-8.0 * slopes).astype(np.float32)
    types = (0, 1) if half == 0 else (1, 0)
    for nm, ty in zip("AB", types):
        rq = (512 * ty + np.arange(512)).astype(np.float32)
        prod = (ch[:, None] * rq[None, :]).astype(np.float32)
        p3 = split3(prod)
        c3 = split3(np.broadcast_to(ch[:, None], (16, 512)).copy())
        t["qaug" + nm] = np.stack(list(p3) + list(c3)).astype(bf)
        kpos = (np.arange(8)[None, :, None] * 128 + np.arange(128)[:, None, None])
        qpos = (512 * ty + np.arange(512))[None, None, :]
        t["cneg" + nm] = np.where(kpos <= qpos, np.float32(0), NEGM).astype(bf)
        kposw = ((np.arange(12)[None, :, None] - 4) * 128 + np.arange(128)[:, None, None])
        dist = qpos - kposw
        t["wneg" + nm] = np.where((dist >= 0) & (dist < 512), np.float32(0), NEGM).astype(bf)
        x = np.arange(264)[None, :]
        relj = x - 8 - 8 * (4 * ty // 4) * 1 - 126 if False else (x - 8 - 8 * ty - 126)
        hb = (np.arange(128)[:, None] >= 64).astype(np.int64)
        valid = relj <= hb
        forced = (relj == hb) | (relj == hb - 1)
        t["vtab" + nm] = valid.astype(np.float32)
        t["atab" + nm] = (np.where(valid, 0.0, -1.0) + np.where(forced, 1e4, 0.0)).astype(np.float32)
    for nm, ty, par in (("A", types[0], 0), ("B", types[1], 1)):
        p = np.arange(128)[:, None, None]
        which = np.arange(2)[None, :, None]
        base = np.where(which == 1, 1024 * par, 1024 * par + 2048)
        qq = (512 * ty + np.arange(512))[None, None, :]
        ok = (16 * p + 31) <= (base + qq)
        t["cmpneg" + nm] = np.where(ok, np.float32(0), NEGM).astype(bf)
    n = np.arange(512)[:, None]
    jj = np.arange(128)[None, :]
    ov = np.clip(np.minimum(16 * n + 32, 64 * jj + 64) - np.maximum(16 * n, 64 * jj), 0, None).astype(np.float32) / 16.0
    t["ovl"] = np.ascontiguousarray(ov.reshape(4, 128, 128).transpose(1, 0, 2)).astype(bf)
    p = np.arange(128)[:, None, None]
    m = np.arange(64)[None, :, None]
    k = np.arange(128)[None, None, :]
    t["inds4"] = ((p == 2 * m + (k >= 64)).astype(np.float32) * np.float32(2.0 ** 30)).astype(bf)
    t["oh"] = np.broadcast_to(np.eye(48, dtype=np.float32)[:, :, None], (48, 48, 64)).astype(bf)
    off = np.arange(64) - 7
    bs = (-slopes[:, None] * 128.0 * off[None, :]).astype(np.float32).reshape(1, 16 * 64)
    t["bias_s"] = np.broadcast_to(bs, (128, 16 * 64)).astype(np.float32)
    bw = (-slopes[:, None] * 128.0 * (4 - np.arange(12))[None, :]).astype(np.float32).reshape(1, 16 * 12)
    t["bias_w"] = np.broadcast_to(bw, (128, 16 * 12)).astype(np.float32)
    jv = np.arange(8)[None, :, None]
    nbv = np.arange(4)[None, None, :]
    bc = (-slopes[:, None, None] * (1024.0 * jv - 2048.0 * nbv - 31.0)).astype(np.float32).reshape(1, 16 * 8 * 4)
    t["bias_c"] = np.broadcast_to(bc, (128, 16 * 8 * 4)).astype(np.float32)
    return {k_: np.ascontiguousarray(v) for k_, v in t.items()}


def prep_shared(inp):
    w_in = inp["w_in"][0]
    offs = np.cumsum([0, 512, 256, 64, 1024, 128, 128, 128, 128, 128, 128, 48])
    c_q, c_kv, k_rope, nsa_q, k_c, v_c, k_s, v_s, k_w, v_w, g_raw = [slice(offs[i], offs[i + 1]) for i in range(11)]
    kr = w_in[:, k_rope]
    kr_sw = np.concatenate([kr[:, 32:], kr[:, :32]], axis=1)
    wk = np.concatenate([w_in[:, c_kv], kr, kr_sw, w_in[:, k_c], w_in[:, v_c], w_in[:, k_s], w_in[:, k_w],
                         w_in[:, v_s], w_in[:, v_w]], axis=1)
    wq = np.concatenate([w_in[:, c_q], w_in[:, nsa_q], w_in[:, g_raw]], axis=1)
    cosk, sink = rope_tables(np.arange(SEQ))
    w_uq = inp["w_uq"][0].reshape(512, 8, 192)
    w_uq_ext = np.concatenate([w_uq, w_uq[:, :, 160:192], w_uq[:, :, 128:160]], axis=2).reshape(512, 2048)
    m = dict(
        attn_g=np.ascontiguousarray(inp["attn_norm_g"].reshape(1, D)),
        wk=np.ascontiguousarray(wk), wq=np.ascontiguousarray(wq),
        gkv=np.ascontiguousarray(inp["mla_kv_norm_g"].reshape(2, 128).T),
        gq=np.ascontiguousarray(inp["mla_q_norm_g"].reshape(4, 128).T),
        cosk=cosk, sink=sink,
        w_uq=np.ascontiguousarray(w_uq_ext), w_uk=np.ascontiguousarray(inp["w_uk"][0]), w_uv=np.ascontiguousarray(inp["w_uv"][0]),
        w_o=np.ascontiguousarray(inp["w_o"][0]), w_gate=np.ascontiguousarray(inp["w_gate"][0]),
        w_up=np.ascontiguousarray(inp["w_up"][0]), w_down=np.ascontiguousarray(inp["w_down"][0]),
        ffn_g=np.ascontiguousarray(inp["ffn_norm_g"].reshape(1, D)), fin_g=np.ascontiguousarray(inp["final_norm_g"].reshape(1, D)),
        w1k=np.ascontiguousarray(inp["w_cmp_k1"][0].reshape(32, 64, 128).transpose(1, 0, 2)),
        w1v=np.ascontiguousarray(inp["w_cmp_v1"][0].reshape(32, 64, 128).transpose(1, 0, 2)),
        w2k=np.ascontiguousarray(inp["w_cmp_k2"][0]), w2v=np.ascontiguousarray(inp["w_cmp_v2"][0]),
        posk=np.ascontiguousarray(inp["cmp_pos_k"][0].T), posv=np.ascontiguousarray(inp["cmp_pos_v"][0].T),
        identd=np.eye(128, dtype=np.float32).astype(ml_dtypes.bfloat16),
    )
    return m


_HALF_CACHE = {}


def prep_half(half):
    if half in _HALF_CACHE:
        return _HALF_CACHE[half]
    own = own_token_index(half)
    cosq, sinq = rope_tables(own)
    kpos = (np.arange(8)[None, :, None] * 128 + np.arange(128)[:, None, None])
    qrel = np.arange(512)[None, None, :]
    mE = (kpos <= qrel).astype(np.float32).astype(ml_dtypes.bfloat16)
    mO = (kpos <= qrel + 512).astype(np.float32).astype(ml_dtypes.bfloat16)
    cmA, cmB = (mE, mO) if half == 0 else (mO, mE)
    m = dict(cosq=cosq, sinq=sinq, cmaskA=np.ascontiguousarray(cmA), cmaskB=np.ascontiguousarray(cmB))
    m.update(nsa_tables(half))
    _HALF_CACHE[half] = m
    return m


def prep_inputs(inp, core, shared=None):
    b, half = core // 2, core % 2
    own = own_token_index(half)
    x = inp["x"]
    m = dict(shared if shared is not None else prep_shared(inp))
    m["xs"] = np.ascontiguousarray(x[b])
    m["xq"] = np.ascontiguousarray(x[b][own])
    m.update(prep_half(half))
    return m


def kernel(**inputs):
    inp = {k: np.asarray(v) for k, v in inputs.items()}
    nc = build()
    shared = prep_shared(inp)
    maps = [prep_inputs(inp, c, shared) for c in range(8)]
    res = run_bass_kernel_spmd(nc, maps, core_ids=list(range(8)))
    out = np.empty((4, SEQ, D), np.float32)
    for c in range(8):
        out[c // 2][own_token_index(c % 2)] = np.asarray(res.results[c]["out"], dtype=np.float32)
    return out
```

```python
import numpy as np
import ml_dtypes
from contextlib import ExitStack
import concourse.bass as bass
import concourse.mybir as mybir
from concourse.bass_utils import run_bass_kernel_spmd

F32 = mybir.dt.float32
BF16 = mybir.dt.bfloat16
ALU = mybir.AluOpType
AF = mybir.ActivationFunctionType
AX = mybir.AxisListType

D = 2048
SEQ = 8192
NQ = 4096
NSLOT = 8
EPS = 1e-6
DFF = 5632
STAGE = 99
MLA_HEADS_RUN = range(8)
MLA_NSLOT_RUN = NSLOT
FFN_NSLOT_RUN = NSLOT
NSA_NSLOT_RUN = NSLOT
NSA_GROUPS_RUN = (0, 1)
NSA_HEADS_RUN = range(8)
OWN_CHUNKS = ([0, 3, 4, 7, 8, 11, 12, 15], [1, 2, 5, 6, 9, 10, 13, 14])


class Sem:
    def __init__(self, h, is_dma):
        self.h = h
        self.total = 0
        self.is_dma = is_dma
        self.sw = False


class Dep:
    __slots__ = ("w", "r", "excl")

    def __init__(self, excl=False):
        self.w = {}
        self.r = {}
        self.excl = excl


class Q:
    def __init__(self, eng, sem, name, self_wait=True):
        self.eng = eng
        self.sem = sem
        self.name = name
        self.seen = {}
        self.self_wait = self_wait


class FW:
    def __init__(self, nc, es):
        self.nc = nc
        self.es = es
        self.es0 = es
        self.nsem = 0
        self.all_sems = []
        self.free_dsems = {False: [], True: []}
        self.phase_dsems = []
        self.pe = Q(nc.tensor, self.sem("pe"), "pe", self_wait=False)
        self.act = Q(nc.scalar, self.sem("act"), "act")
        self.dve = Q(nc.vector, self.sem("dve"), "dve")
        self.pool = Q(nc.gpsimd, self.sem("pool"), "pool")
        self.sp = Q(nc.sync, self.sem("sp"), "sp")
        self.ninst = 0

    def sem(self, name, is_dma=False):
        self.nsem += 1
        s = Sem(self.es0.enter_context(self.nc.semaphore("m_" + name)), is_dma)
        self.all_sems.append(s)
        return s

    def dsem(self, name, sw=False):
        pool = self.free_dsems[sw]
        if pool:
            s = pool.pop()
        else:
            s = self.sem(name, True)
            s.sw = sw
        self.phase_dsems.append(s)
        return s

    def end_phase(self):
        self.barrier()
        for s in self.phase_dsems:
            self.free_dsems[s.sw].append(s)
        self.phase_dsems = []

    def sb(self, name, shape, dt):
        return self.es.enter_context(self.nc.sbuf_tensor("s_" + name, shape, dt))

    def ps(self, name, shape, dt):
        return self.es.enter_context(self.nc.psum_tensor("p_" + name, shape, dt))

    def _wait(self, q, R, W):
        need = {}
        for d in R:
            for s, v in d.w.items():
                if need.get(s, 0) < v:
                    need[s] = v
            if d.excl:
                for s, v in d.r.items():
                    if s is not q.sem and need.get(s, 0) < v:
                        need[s] = v
        for d in W:
            for s, v in d.w.items():
                if need.get(s, 0) < v:
                    need[s] = v
            for s, v in d.r.items():
                if need.get(s, 0) < v:
                    need[s] = v
        for s, v in need.items():
            if s is q.sem and not q.self_wait:
                continue
            if s.is_dma:
                v = s.total
            if q.seen.get(s, 0) < v:
                q.eng.wait_ge(s.h, v)
                q.seen[s] = v

    def op(self, q, f, R=(), W=()):
        self._wait(q, R, W)
        ins = f(q.eng)
        q.sem.total += 1
        ins.then_inc(q.sem.h, 1)
        v = q.sem.total
        for d in R:
            d.r[q.sem] = v
        for d in W:
            d.w = {q.sem: v}
            d.r = {}
        self.ninst += 1

    def dma(self, q, out, in_, sem, R=(), W=()):
        assert sem.sw == (q is self.pool), "semaphore/queue kind mismatch"
        self._wait(q, R, W)
        ins = q.eng.dma_start(out=out, in_=in_)
        sem.total += 16
        ins.then_inc(sem.h, 16)
        for d in R:
            d.r[sem] = sem.total
        for d in W:
            d.w = {sem: sem.total}
            d.r = {}
        self.ninst += 1

    def dmas(self, q, pairs, sem, R=(), W=()):
        assert sem.sw == (q is self.pool), "semaphore/queue kind mismatch"
        self._wait(q, R, W)
        for (o, i) in pairs:
            ins = q.eng.dma_start(out=o, in_=i)
            sem.total += 16
            ins.then_inc(sem.h, 16)
            self.ninst += 1
        for d in R:
            d.r[sem] = sem.total
        for d in W:
            d.w = {sem: sem.total}
            d.r = {}

    def barrier(self):
        qs = [self.pe, self.act, self.dve, self.pool, self.sp]
        for q in qs:
            for s in self.all_sems:
                if s.total > 0 and q.seen.get(s, 0) < s.total and not (s is q.sem and not q.self_wait):
                    q.eng.wait_ge(s.h, s.total)
                    q.seen[s] = s.total

    def wait_all(self, q, deps):
        self._wait(q, deps, ())


class T:
    def __init__(self, ap, dep=None, sem=None):
        self.ap = ap
        self.d = dep if dep is not None else Dep()
        self.sem = sem

    def __getitem__(self, k):
        return self.ap[k]


def rope_tables(pos):
    inv = (10000.0 ** (-np.arange(0, 64, 2, dtype=np.float32) / np.float32(64))).astype(np.float32)
    ang = (pos.astype(np.float32)[None, :] * inv[:, None]).astype(np.float32)
    c = np.cos(ang).astype(np.float32)
    s = np.sin(ang).astype(np.float32)
    cos_t = np.concatenate([c, c], axis=0)
    sin_t = np.concatenate([-s, s], axis=0)
    return np.ascontiguousarray(cos_t), np.ascontiguousarray(sin_t)


def bcast_rows(ap_row, nparts, n):
    return bass.AP(ap_row.tensor, ap_row.offset, [[0, nparts], [1, n]])


def phase_proj(fw, PS, C, x_dram, ntok, g_dram, w_dram, ncols, side, o, drd):
    nc = fw.nc
    pe, act, dve, pool, sp = fw.pe, fw.act, fw.dve, fw.pool, fw.sp
    ident, ones = C["ident"], C["ones"]
    es2 = ExitStack()
    with es2:
        old_es = fw.es
        fw.es = es2
        wsb = T(fw.sb(f"w{side}", [128, 16, ncols], BF16), sem=fw.dsem(f"w{side}", sw=True))
        fw.dmas(pool, [(wsb[:, kc, :], w_dram[kc * 128:(kc + 1) * 128, :]) for kc in range(16)], wsb.sem, W=[wsb.d])
        gtab = T(fw.sb(f"gtab{side}", [128, D], F32), sem=fw.dsem(f"gtab{side}"))
        fw.dma(sp, gtab[:], bcast_rows(g_dram, 128, D), gtab.sem, W=[gtab.d])
        xt = [T(fw.sb(f"xt{side}{i}", [128, D], F32), sem=fw.dsem(f"xt{side}{i}")) for i in range(2)]
        junk = T(fw.sb(f"junk{side}", [128, D], BF16))
        hb = [T(fw.sb(f"hb{side}{i}", [128, D], BF16)) for i in range(4)]
        hT = [T(fw.sb(f"hT{side}{i}", [128, 16, 512], BF16)) for i in range(2)]
        ssb = [T(fw.sb(f"ss{side}{i}", [128, 2], F32)) for i in range(2)]
        NST = 4
        stg = [T(fw.sb(f"stg{side}{i}", [128, 512], BF16), sem=fw.dsem(f"stg{side}{i}", sw=True)) for i in range(NST)]
        stg_i = [0]

        def stage():
            t = stg[stg_i[0] % NST]
            stg_i[0] += 1
            return t

        if side == "k":
            gsm = T(fw.sb("gkv_sb", [128, 2], F32), sem=fw.dsem("gkv"))
            fw.dma(sp, gsm[:], o["gkv"], gsm.sem, W=[gsm.d])
            NLAT = 2
            costab = [T(fw.sb(f"cos{i}", [64, 512], F32), sem=fw.dsem(f"cos{i}")) for i in range(2)]
            sintab = [T(fw.sb(f"sin{i}", [64, 512], F32), sem=fw.dsem(f"sin{i}")) for i in range(2)]
            rt = [T(fw.sb(f"rt{i}", [64, 512], F32)) for i in range(2)]
        else:
            gsm = T(fw.sb("gq_sb", [128, 4], F32), sem=fw.dsem("gq"))
            fw.dma(sp, gsm[:], o["gq"], gsm.sem, W=[gsm.d])
            NLAT = 4
            gf = T(fw.sb("gf", [48, 512], F32))
            ghi = T(fw.sb("ghi", [48, 512], BF16), sem=fw.dsem("ghi", sw=True))
            glo = T(fw.sb("glo", [48, 512], BF16), sem=fw.dsem("glo", sw=True))
        raw = [T(fw.sb(f"raw{side}{i}", [128, 512], F32)) for i in range(NLAT)]
        sq = [T(fw.sb(f"sq{side}{i}", [128, 512], BF16)) for i in range(NLAT)]
        rsb = T(fw.sb(f"rsb{side}", [128, 512], F32))
        psrot = [0]

        def nextps():
            p = PS[2 + psrot[0] % 6]
            psrot[0] += 1
            return p

        evac_rr = [0]

        def evac_copy(dst_ap, src_ap, R, W):
            evac_rr[0] += 1
            if evac_rr[0] % 2:
                fw.op(act, lambda e: e.copy(out=dst_ap, in_=src_ap), R=R, W=W)
            else:
                fw.op(dve, lambda e: e.tensor_copy(out=dst_ap, in_=src_ap), R=R, W=W)

        NB = ntok // 512

        def norm_part(tb):
            for s in range(4):
                i = tb * 4 + s
                xb = xt[i % 2]
                hbb = hb[s]
                ss = ssb[i % 2]
                fw.dma(sp, xb[:], x_dram[i * 128:(i + 1) * 128, :], xb.sem, W=[xb.d])
                fw.op(act, lambda e: e.memzero(ss[:, 0:1]), W=[ss.d])
                fw.op(act, lambda e: e.activation(out=junk[:], in_=xb[:], func=AF.Square, accum_out=ss[:, 0:1]),
                      R=[xb.d], W=[junk.d, ss.d])
                fw.op(act, lambda e: e.activation(out=ss[:, 1:2], in_=ss[:, 0:1], func=AF.Sqrt, scale=1.0 / D, bias=C["eps"][:, 0:1]),
                      R=[C["eps"].d], W=[ss.d])
                fw.op(dve, lambda e: e.reciprocal(out=ss[:, 1:2], in_=ss[:, 1:2]), W=[ss.d])
                fw.op(dve, lambda e: e.scalar_tensor_tensor(out=hbb[:], in0=xb[:], scalar=ss[:, 1:2], in1=gtab[:],
                                                             op0=ALU.mult, op1=ALU.mult),
                      R=[xb.d, ss.d, gtab.d], W=[hbb.d])

        def transpose_part(tb):
            h = hT[tb % 2]
            for s in range(4):
                hbb = hb[s]
                for half in range(2):
                    pt = PS[half]
                    ptb = pt.ap.bitcast(BF16)
                    for j in range(8):
                        kc = half * 8 + j
                        fw.op(pe, lambda e: e.transpose(out=ptb[:, j * 128:(j + 1) * 128], in_=hbb[:, kc * 128:(kc + 1) * 128],
                                                        identity=ident[:]),
                              R=[hbb.d, ident.d], W=[pt.d])
                    evac_copy(h[:, half * 8:(half + 1) * 8, s * 128:(s + 1) * 128],
                              ptb.rearrange("p (j t) -> p j t", j=8), [pt.d], [h.d])

        norm_part(0)
        transpose_part(0)
        for tb in range(NB):
            h = hT[tb % 2]
            t0 = tb * 512
            if side == "k":
                ct, st_ = costab[tb % 2], sintab[tb % 2]
                fw.dma(sp, ct[:], o["cosk"][:, t0:t0 + 512], ct.sem, W=[ct.d])
                fw.dma(sp, st_[:], o["sink"][:, t0:t0 + 512], st_.sem, W=[st_.d])
            if tb + 1 < NB:
                norm_part(tb + 1)
            def fm_group(c0, m):
                p = nextps()
                for kc in range(16):
                    fw.op(pe, lambda e: e.matmul(p[0:m, :], lhsT=wsb[:, kc, c0:c0 + m], rhs=h[:, kc, :],
                                                 start=(kc == 0), stop=(kc == 15)),
                          R=[wsb.d, h.d], W=[p.d])
                return p

            def store(dst, src_t, m, dd):
                fw.dma(pool, dst, src_t[0:m, :], src_t.sem, R=[src_t.d], W=[dd])

            lat_out = o["ckvT"] if side == "k" else o["cqT"]
            lat_dep = drd["ckvT"] if side == "k" else drd["cqT"]
            for c in range(NLAT):
                p = fm_group(c * 128, 128)
                fw.op(act, lambda e: e.activation(out=sq[c][:], in_=p[:], func=AF.Square), R=[p.d], W=[sq[c].d])
                fw.op(dve, lambda e: e.tensor_copy(out=raw[c][:], in_=p[:]), R=[p.d], W=[raw[c].d])
            p = nextps()
            for c in range(NLAT):
                fw.op(pe, lambda e: e.matmul(p[:], lhsT=ones[:], rhs=sq[c][:], start=(c == 0), stop=(c == NLAT - 1)),
                      R=[ones.d, sq[c].d], W=[p.d])
            fw.op(act, lambda e: e.activation(out=rsb[:], in_=p[:], func=AF.Sqrt, scale=1.0 / (128 * NLAT), bias=C["eps"][:, 0:1]),
                  R=[p.d, C["eps"].d], W=[rsb.d])
            fw.op(dve, lambda e: e.reciprocal(out=rsb[:], in_=rsb[:]), W=[rsb.d])
            for c in range(NLAT):
                sg = stage()
                fw.op(dve, lambda e: e.scalar_tensor_tensor(out=sg[:], in0=raw[c][:], scalar=gsm[:, c:c + 1], in1=rsb[:],
                                                             op0=ALU.mult, op1=ALU.mult),
                      R=[raw[c].d, gsm.d, rsb.d], W=[sg.d])
                store(lat_out[c * 128:(c + 1) * 128, t0:t0 + 512], sg, 128, lat_dep)
            if side == "k":
                px = fm_group(256, 64)
                pw = fm_group(320, 64)
                r1, r2 = rt
                fw.op(dve, lambda e: e.tensor_tensor(out=r1[:], in0=px[0:64, :], in1=ct[:], op=ALU.mult),
                      R=[px.d, ct.d], W=[r1.d])
                fw.op(dve, lambda e: e.tensor_tensor(out=r2[:], in0=pw[0:64, :], in1=st_[:], op=ALU.mult),
                      R=[pw.d, st_.d], W=[r2.d])
                sg = stage()
                fw.op(dve, lambda e: e.tensor_tensor(out=sg[0:64, :], in0=r1[:], in1=r2[:], op=ALU.add),
                      R=[r1.d, r2.d], W=[sg.d])
                store(o["kropeT"][:, t0:t0 + 512], sg, 64, drd["kropeT"])
                for gi, nm in enumerate(["kcT", "vcT", "ksT", "kwT"]):
                    p = fm_group(384 + gi * 128, 128)
                    sg = stage()
                    evac_copy(sg[:], p[:], [p.d], [sg.d])
                    store(o[nm][:, t0:t0 + 512], sg, 128, drd[nm])
                for s in range(4):
                    p = nextps()
                    for kc in range(16):
                        fw.op(pe, lambda e: e.matmul(p[:, 0:256], lhsT=h[:, kc, s * 128:(s + 1) * 128], rhs=wsb[:, kc, 896:1152],
                                                     start=(kc == 0), stop=(kc == 15)),
                              R=[wsb.d, h.d], W=[p.d])
                    sg = stage()
                    evac_copy(sg[:, 0:256], p[:, 0:256], [p.d], [sg.d])
                    fw.dma(pool, o["vsw"][t0 + s * 128:t0 + (s + 1) * 128, :], sg[:, 0:256], sg.sem, R=[sg.d], W=[drd["vsw"]])
            else:
                for gi in range(8):
                    p = fm_group(512 + gi * 128, 128)
                    sg = stage()
                    evac_copy(sg[:], p[:], [p.d], [sg.d])
                    store(o["qnT"][gi * 128:(gi + 1) * 128, t0:t0 + 512], sg, 128, drd["qnT"])
                p = fm_group(1536, 48)
                fw.op(act, lambda e: e.activation(out=gf[:], in_=p[0:48, :], func=AF.Sigmoid), R=[p.d], W=[gf.d])
                fw.op(dve, lambda e: e.tensor_copy(out=ghi[:], in_=gf[:]), R=[gf.d], W=[ghi.d])
                fw.op(dve, lambda e: e.tensor_tensor(out=glo[:], in0=gf[:], in1=ghi[:], op=ALU.subtract),
                      R=[gf.d, ghi.d], W=[glo.d])
                fw.dma(pool, o["gTh"][:, t0:t0 + 512], ghi[:], ghi.sem, R=[ghi.d], W=[drd["gTh"]])
                fw.dma(pool, o["gTl"][:, t0:t0 + 512], glo[:], glo.sem, R=[glo.d], W=[drd["gTl"]])
            if tb + 1 < NB:
                transpose_part(tb + 1)
        fw.es = old_es
        fw.end_phase()


MLA_SCALE = 192.0 ** -0.5


def phase_mla(fw, PS, C, o, drd, heads=range(8), nslot=NSLOT):
    pe, act, dve, pool, sp = fw.pe, fw.act, fw.dve, fw.pool, fw.sp
    ones = C["ones"]
    nkb_all = SEQ // 128
    es2 = ExitStack()
    with es2:
        old_es = fw.es
        fw.es = es2
        krope = T(fw.sb("krope", [128, SEQ], BF16), sem=fw.dsem("krope"))
        fw.op(pool, lambda e: e.memset(krope[64:128, :], 0.0), W=[krope.d])
        fw.dma(sp, krope[0:64, :], o["kropeT"], krope.sem, R=[drd["kropeT"]], W=[krope.d])
        wuq = T(fw.sb("wuq", [128, 4, 2048], BF16), sem=fw.dsem("wuq", sw=True))
        fw.dmas(pool, [(wuq[:, c, :], o["w_uq"][c * 128:(c + 1) * 128, :]) for c in range(4)], wuq.sem, W=[wuq.d])
        wuk = T(fw.sb("wuk", [128, 2, 1024], BF16), sem=fw.dsem("wuk", sw=True))
        fw.dmas(pool, [(wuk[:, c, :], o["w_uk"][c * 128:(c + 1) * 128, :]) for c in range(2)], wuk.sem, W=[wuk.d])
        wuv = T(fw.sb("wuv", [128, 2, 1024], BF16), sem=fw.dsem("wuv", sw=True))
        fw.dmas(pool, [(wuv[:, c, :], o["w_uv"][c * 128:(c + 1) * 128, :]) for c in range(2)], wuv.sem, W=[wuv.d])
        dhi = T(fw.sb("dhi", [128, 512], BF16))
        dlo = T(fw.sb("dlo", [128, 512], BF16))
        masks = []
        for i, nm in enumerate(["cmaskA", "cmaskB"]):
            mt = T(fw.sb(nm, [128, 8, 512], BF16), sem=fw.dsem(nm))
            fw.dma(sp, mt[:], o[nm], mt.sem, W=[mt.d])
            masks.append(mt)
        Kh = [T(fw.sb(f"Kh{i}", [128, SEQ], BF16)) for i in range(2)]
        Vh = [T(fw.sb(f"Vh{i}", [128, nkb_all, 128], BF16)) for i in range(2)]
        ckv = [T(fw.sb(f"ckv{i}", [128, 2, 512], BF16), sem=fw.dsem(f"ckv{i}")) for i in range(2)]
        cq = [T(fw.sb(f"cq{i}", [128, 4, 512], BF16), sem=fw.dsem(f"cq{i}")) for i in range(2)]
        cosq = [T(fw.sb(f"cosq{i}", [64, 512], F32), sem=fw.dsem(f"cosq{i}")) for i in range(2)]
        sinq = [T(fw.sb(f"sinq{i}", [64, 512], F32), sem=fw.dsem(f"sinq{i}")) for i in range(2)]
        qn = [T(fw.sb(f"qn{i}", [128, 512], BF16)) for i in range(2)]
        qr = [T(fw.sb(f"qr{i}", [128, 512], BF16)) for i in range(2)]
        for t_ in qr:
            fw.op(pool, lambda e: e.memset(t_[64:128, :], 0.0), W=[t_.d])
        r1 = [T(fw.sb(f"mr1{i}", [64, 512], F32)) for i in range(2)]
        r2 = [T(fw.sb(f"mr2{i}", [64, 512], F32)) for i in range(2)]
        NPT = 4
        pts = [T(fw.sb(f"pt{i}", [128, 512], BF16)) for i in range(NPT)]
        dacc = [T(fw.sb(f"dacc{i}", [128, 512], F32)) for i in range(4)]
        rec = T(fw.sb("rec", [128, 512], F32))
        ost = [T(fw.sb(f"ost{i}", [128, 512], BF16), sem=fw.dsem(f"ost{i}", sw=True)) for i in range(2)]
        PSO = [PS[3], PS[5]]
        PSDS = [PS[4], PS[6]]
        PSGEN = [PS[7], PS[7]]
        hpos = {}
        ctr = dict(ckv=0, slot=0, pt=0, ost=0, od=0, gen=0)

        def genbank():
            p = PSGEN[ctr["gen"] % 2]
            ctr["gen"] += 1
            return p

        def kvgen_tasks(h):
            K_, V_ = Kh[hpos[h] % 2], Vh[hpos[h] % 2]
            tasks = []
            for tb in range(SEQ // 512):
                st8 = {}

                def tk(tb=tb, st8=st8):
                    ck = ckv[ctr["ckv"] % 2]
                    ctr["ckv"] += 1
                    st8["ck"] = ck
                    fw.dma(sp, ck[:], o["ckvT"][:, tb * 512:(tb + 1) * 512].rearrange("(c p) t -> p c t", p=128), ck.sem,
                           R=[drd["ckvT"]], W=[ck.d])
                    pg = genbank()
                    for c in range(2):
                        fw.op(pe, lambda e: e.matmul(pg[:], lhsT=wuk[:, c, h * 128:(h + 1) * 128], rhs=ck[:, c, :],
                                                     start=(c == 0), stop=(c == 1)), R=[wuk.d, ck.d], W=[pg.d])
                    fw.op(act, lambda e: e.copy(out=K_[:, tb * 512:(tb + 1) * 512], in_=pg[:]), R=[pg.d], W=[K_.d])

                def tv(tb=tb, st8=st8):
                    ck = st8["ck"]
                    pg = genbank()
                    for s in range(4):
                        for c in range(2):
                            fw.op(pe, lambda e: e.matmul(pg[:, s * 128:(s + 1) * 128], lhsT=ck[:, c, s * 128:(s + 1) * 128],
                                                         rhs=wuv[:, c, h * 128:(h + 1) * 128], start=(c == 0), stop=(c == 1)),
                                  R=[wuv.d, ck.d], W=[pg.d])
                    fw.op(dve, lambda e: e.tensor_copy(out=V_[:, tb * 4:(tb + 1) * 4, :], in_=pg[:].rearrange("p (s d) -> p s d", s=4)),
                          R=[pg.d], W=[V_.d])
                tasks += [tk, tv]
            return tasks

        def qgen_tasks(h, j):
            i = ctr["slot"] % 2
            ctr["slot"] += 1
            cqt, ct, st_, qnt, qrt, r1_, r2_ = cq[i], cosq[i], sinq[i], qn[i], qr[i], r1[i], r2[i]
            q0 = j * 512

            def t0():
                fw.dma(sp, cqt[:], o["cqT"][:, q0:q0 + 512].rearrange("(c p) t -> p c t", p=128), cqt.sem, R=[drd["cqT"]], W=[cqt.d])
                fw.dma(sp, ct[:], o["cosq"][:, q0:q0 + 512], ct.sem, W=[ct.d])
                fw.dma(sp, st_[:], o["sinq"][:, q0:q0 + 512], st_.sem, W=[st_.d])
                pg = genbank()
                for c in range(4):
                    fw.op(pe, lambda e: e.matmul(pg[:], lhsT=wuq[:, c, h * 256:h * 256 + 128], rhs=cqt[:, c, :],
                                                 start=(c == 0), stop=(c == 3)), R=[wuq.d, cqt.d], W=[pg.d])
                fw.op(act, lambda e: e.copy(out=qnt[:], in_=pg[:]), R=[pg.d], W=[qnt.d])

            def t1():
                pg = genbank()
                for c in range(4):
                    fw.op(pe, lambda e: e.matmul(pg[0:64, :], lhsT=wuq[:, c, h * 256 + 128:h * 256 + 192], rhs=cqt[:, c, :],
                                                 start=(c == 0), stop=(c == 3)), R=[wuq.d, cqt.d], W=[pg.d])
                fw.op(dve, lambda e: e.tensor_tensor(out=r1_[:], in0=pg[0:64, :], in1=ct[:], op=ALU.mult), R=[pg.d, ct.d], W=[r1_.d])

            def t2():
                pg = genbank()
                for c in range(4):
                    fw.op(pe, lambda e: e.matmul(pg[0:64, :], lhsT=wuq[:, c, h * 256 + 192:h * 256 + 256], rhs=cqt[:, c, :],
                                                 start=(c == 0), stop=(c == 3)), R=[wuq.d, cqt.d], W=[pg.d])
                fw.op(dve, lambda e: e.tensor_tensor(out=r2_[:], in0=pg[0:64, :], in1=st_[:], op=ALU.mult), R=[pg.d, st_.d], W=[r2_.d])
                fw.op(dve, lambda e: e.tensor_tensor(out=qrt[0:64, :], in0=r1_[:], in1=r2_[:], op=ALU.add), R=[r1_.d, r2_.d], W=[qrt.d])
            return [t0, t1, t2], (qnt, qrt)

        def attend_slot(h, j, bufs, nextq, bg):
            K_, V_ = Kh[hpos[h] % 2], Vh[hpos[h] % 2]
            qnt, qrt = bufs
            mk = masks[j % 2]
            nkb = 8 * (j + 1)
            q0 = j * 512
            po = PSO[ctr["od"] % 2]
            pd = PSDS[ctr["od"] % 2]
            das = (dacc[2 * (ctr["od"] % 2)], dacc[2 * (ctr["od"] % 2) + 1])
            da = das[0]
            ctr["od"] += 1

            def qk(kb):
                ps = PS[kb % 3]
                fw.op(pe, lambda e: e.matmul(ps[:], lhsT=K_[:, kb * 128:(kb + 1) * 128], rhs=qnt[:], start=True, stop=False),
                      R=[K_.d, qnt.d], W=[ps.d])
                fw.op(pe, lambda e: e.matmul(ps[:], lhsT=krope[:, kb * 128:(kb + 1) * 128], rhs=qrt[:], start=False, stop=True),
                      R=[krope.d, qrt.d], W=[ps.d])

            def pv(kb):
                ps = PS[kb % 3]
                pt = pts[ctr["pt"] % NPT]
                ctr["pt"] += 1
                fw.op(act, lambda e: e.activation(out=pt[:], in_=ps[:], func=AF.Exp, scale=MLA_SCALE), R=[ps.d], W=[pt.d])
                w = kb - (nkb - 8)
                if w >= 0:
                    fw.op(dve, lambda e: e.tensor_tensor(out=pt[:], in0=pt[:], in1=mk[:, w, :], op=ALU.mult), R=[mk.d], W=[pt.d])
                fw.op(pe, lambda e: e.matmul(po[:], lhsT=V_[:, kb, :], rhs=pt[:], start=(kb == 0), stop=(kb == nkb - 1)),
                      R=[V_.d, pt.d], W=[po.d])
                if kb % 2 == 0:
                    dk = das[(kb // 2) % 2]
                    if kb < 4:
                        fw.op(dve, lambda e: e.tensor_copy(out=dk[:], in_=pt[:]), R=[pt.d], W=[dk.d])
                    else:
                        fw.op(dve, lambda e: e.tensor_tensor(out=dk[:], in0=dk[:], in1=pt[:], op=ALU.add), R=[pt.d], W=[dk.d])
                else:
                    fw.op(pe, lambda e: e.matmul(pd[:], lhsT=ones[:], rhs=pt[:], start=(kb == 1), stop=False), R=[ones.d, pt.d], W=[pd.d])

            qk(0)
            if nkb > 1:
                qk(1)
            for kb in range(nkb):
                if kb + 2 < nkb:
                    qk(kb + 2)
                pv(kb)
                if nextq and kb % 2 == 1:
                    nextq.pop(0)()
                elif bg and kb % 4 == 3:
                    bg.pop(0)()
            while nextq:
                nextq.pop(0)()
            fw.op(dve, lambda e: e.tensor_tensor(out=da[:], in0=da[:], in1=das[1][:], op=ALU.add), R=[das[1].d], W=[da.d])
            fw.op(dve, lambda e: e.tensor_copy(out=dhi[:], in_=da[:]), R=[da.d], W=[dhi.d])
            fw.op(dve, lambda e: e.tensor_tensor(out=dlo[:], in0=da[:], in1=dhi[:], op=ALU.subtract), R=[da.d, dhi.d], W=[dlo.d])
            fw.op(pe, lambda e: e.matmul(pd[:], lhsT=ones[:], rhs=dhi[:], start=False, stop=False), R=[ones.d, dhi.d], W=[pd.d])
            fw.op(pe, lambda e: e.matmul(pd[:], lhsT=ones[:], rhs=dlo[:], start=False, stop=True), R=[ones.d, dlo.d], W=[pd.d])
            fw.op(dve, lambda e: e.reciprocal(out=rec[:], in_=pd[:]), R=[pd.d], W=[rec.d])
            og = ost[ctr["ost"] % 2]
            ctr["ost"] += 1
            fw.op(dve, lambda e: e.tensor_tensor(out=og[:], in0=po[:], in1=rec[:], op=ALU.mult), R=[po.d, rec.d], W=[og.d])
            fw.dma(pool, o["attnT"][h * 128:(h + 1) * 128, q0:q0 + 512], og[:], og.sem, R=[og.d], W=[drd["attnT"]])

        hs = list(heads)
        hpos.update({h: i for i, h in enumerate(hs)})
        for t in kvgen_tasks(hs[0]):
            t()
        tasks, bufs = qgen_tasks(hs[0], 0)
        for t in tasks:
            t()
        for i, h in enumerate(hs):
            bg = kvgen_tasks(hs[i + 1]) if i + 1 < len(hs) else []
            for j in range(nslot):
                if j + 1 < nslot:
                    nextq, nbufs = qgen_tasks(h, j + 1)
                elif i + 1 < len(hs):
                    nextq, nbufs = qgen_tasks(hs[i + 1], 0)
                else:
                    nextq, nbufs = [], None
                if j == nslot - 1:
                    pass
                attend_slot(h, j, bufs, nextq, bg)
                bufs = nbufs
            while bg:
                bg.pop(0)()
        fw.es = old_es
        fw.end_phase()


def convert_weight_chunks(fw, o, drd):
    sem = fw.dsem("wconv", sw=True)
    fw.phase_dsems.remove(sem)
    chunks = []
    drd["wconv"] = []
    for src_nm, dst_nm, rows in [("w_o", "wo_b", 2048), ("w_gate", "wg_b", 2048), ("w_up", "wu_b", 2048), ("w_down", "wd_b", DFF)]:
        for r in range(0, rows, 512):
            n = min(512, rows - r)

            def issue(src_nm=src_nm, dst_nm=dst_nm, r=r, n=n):
                dep = Dep()
                drd["wconv"].append(dep)
                pairs = [(o[dst_nm][rr:rr + 128, :], o[src_nm][rr:rr + 128, :]) for rr in range(r, r + n, 128)]
                fw.dmas(fw.pool, pairs, sem, W=[dep])
            chunks.append(issue)
    return chunks


def phase_ffn(fw, PS, C, o, drd, nslot=NSLOT):
    pe, act, dve, pool, sp = fw.pe, fw.act, fw.dve, fw.pool, fw.sp
    ident = C["ident"]
    NF = DFF // 128
    es2 = ExitStack()
    with es2:
        old_es = fw.es
        fw.es = es2
        gt_ffn = T(fw.sb("gt_ffn", [128, D], F32), sem=fw.dsem("gt_ffn"))
        fw.dma(sp, gt_ffn[:], bcast_rows(o["ffn_g"], 128, D), gt_ffn.sem, W=[gt_ffn.d])
        gt_fin = T(fw.sb("gt_fin", [128, D], F32), sem=fw.dsem("gt_fin"))
        fw.dma(sp, gt_fin[:], bcast_rows(o["fin_g"], 128, D), gt_fin.sem, W=[gt_fin.d])
        x1 = [T(fw.sb(f"x1_{s}", [128, D], F32), sem=fw.dsem(f"x1_{s}", sw=True)) for s in range(4)]
        ah = T(fw.sb("ah", [128, 16, 512], BF16), sem=fw.dsem("ah"))
        actT = T(fw.sb("actT", [128, NF, 512], BF16))
        NW = 4
        wbuf = [T(fw.sb(f"wb{i}", [128, 8192], BF16), sem=fw.dsem(f"wb{i}")) for i in range(NW)]
        hb = [T(fw.sb(f"fhb{i}", [128, D], BF16)) for i in range(2)]
        junk = T(fw.sb("fjunk", [128, D], BF16))
        ssb = [T(fw.sb(f"fss{i}", [128, 2], F32)) for i in range(2)]
        ostg = [T(fw.sb(f"fo{i}", [128, D // 2], F32), sem=fw.dsem(f"fo{i}", sw=True)) for i in range(2)]
        sg = [T(fw.sb(f"sg{i}", [128, 512], F32)) for i in range(2)]
        ctr = dict(w=0, ss=0, hb=0, sg=0, o=0, ps=0)

        def wload(src_view, n_mid):
            wt = wbuf[ctr["w"] % NW]
            ctr["w"] += 1
            v = wt.ap[:, 0:n_mid * 512].rearrange("p (k c) -> p k c", c=512)
            fw.dma(sp, v, src_view, wt.sem, R=drd["wconv"], W=[wt.d])
            return wt, v

        def rms_rstd(xap, xdep):
            ss = ssb[ctr["ss"] % 2]
            ctr["ss"] += 1
            fw.op(act, lambda e: e.memzero(ss[:, 0:1]), W=[ss.d])
            fw.op(act, lambda e: e.activation(out=junk[:], in_=xap, func=AF.Square, accum_out=ss[:, 0:1]), R=[xdep], W=[junk.d, ss.d])
            fw.op(act, lambda e: e.activation(out=ss[:, 1:2], in_=ss[:, 0:1], func=AF.Sqrt, scale=1.0 / D, bias=C["eps"][:, 0:1]),
                  R=[C["eps"].d], W=[ss.d])
            fw.op(dve, lambda e: e.reciprocal(out=ss[:, 1:2], in_=ss[:, 1:2]), W=[ss.d])
            return ss

        for j in range(nslot):
            q0 = j * 512
            if j == 0:
                fw.dma(sp, ah[:], o["attnT"][:, q0:q0 + 512].rearrange("(k p) t -> p k t", p=128), ah.sem, R=[drd["attnT"]], W=[ah.d])
            if j == 0:
                for s in range(4):
                    fw.dma(pool, x1[s][:], o["xq"][q0 + s * 128:q0 + (s + 1) * 128, :], x1[s].sem, W=[x1[s].d])
            for cb in range(4):
                wt, wv = wload(o["wo_b"][:, cb * 512:(cb + 1) * 512].rearrange("(k p) c -> p k c", p=128), 16)
                for s in range(4):
                    p = PS[ctr["ps"] % 8]
                    ctr["ps"] += 1
                    for kc in range(16):
                        fw.op(pe, lambda e: e.matmul(p[:], lhsT=ah[:, kc, s * 128:(s + 1) * 128], rhs=wv[:, kc, :],
                                                     start=(kc == 0), stop=(kc == 15)), R=[ah.d, wt.d], W=[p.d])
                    fw.op(dve, lambda e: e.tensor_tensor(out=x1[s][:, cb * 512:(cb + 1) * 512], in0=p[:],
                                                          in1=x1[s][:, cb * 512:(cb + 1) * 512], op=ALU.add), R=[p.d], W=[x1[s].d])
            for s in range(4):
                ss = rms_rstd(x1[s][:], x1[s].d)
                hbb = hb[ctr["hb"] % 2]
                ctr["hb"] += 1
                fw.op(dve, lambda e: e.scalar_tensor_tensor(out=hbb[:], in0=x1[s][:], scalar=ss[:, 1:2], in1=gt_ffn[:],
                                                             op0=ALU.mult, op1=ALU.mult), R=[x1[s].d, ss.d, gt_ffn.d], W=[hbb.d])
                for half in range(2):
                    pt = PS[ctr["ps"] % 8]
                    ctr["ps"] += 1
                    ptb = pt.ap.bitcast(BF16)
                    for jj in range(8):
                        kc = half * 8 + jj
                        fw.op(pe, lambda e: e.transpose(out=ptb[:, jj * 128:(jj + 1) * 128], in_=hbb[:, kc * 128:(kc + 1) * 128],
                                                        identity=ident[:]), R=[hbb.d, ident.d], W=[pt.d])
                    dst = ah[:, half * 8:(half + 1) * 8, s * 128:(s + 1) * 128]
                    srcv = ptb.rearrange("p (j t) -> p j t", j=8)
                    if half == 0:
                        fw.op(act, lambda e: e.copy(out=dst, in_=srcv), R=[pt.d], W=[ah.d])
                    else:
                        fw.op(dve, lambda e: e.tensor_copy(out=dst, in_=srcv), R=[pt.d], W=[ah.d])
            for fg in range(NF // 4):
                wgt, wgv = wload(o["wg_b"][:, fg * 512:(fg + 1) * 512].rearrange("(k p) c -> p k c", p=128), 16)
                wut, wuv = wload(o["wu_b"][:, fg * 512:(fg + 1) * 512].rearrange("(k p) c -> p k c", p=128), 16)
                for fi in range(4):
                    f = fg * 4 + fi
                    pg = PS[ctr["ps"] % 8]
                    pu = PS[(ctr["ps"] + 1) % 8]
                    ctr["ps"] += 2
                    for kc in range(16):
                        fw.op(pe, lambda e: e.matmul(pg[:], lhsT=wgv[:, kc, fi * 128:(fi + 1) * 128], rhs=ah[:, kc, :],
                                                     start=(kc == 0), stop=(kc == 15)), R=[wgt.d, ah.d], W=[pg.d])
                    for kc in range(16):
                        fw.op(pe, lambda e: e.matmul(pu[:], lhsT=wuv[:, kc, fi * 128:(fi + 1) * 128], rhs=ah[:, kc, :],
                                                     start=(kc == 0), stop=(kc == 15)), R=[wut.d, ah.d], W=[pu.d])
                    sgt = sg[ctr["sg"] % 2]
                    ctr["sg"] += 1
                    fw.op(act, lambda e: e.activation(out=sgt[:], in_=pg[:], func=AF.Silu), R=[pg.d], W=[sgt.d])
                    fw.op(dve, lambda e: e.tensor_tensor(out=actT[:, f, :], in0=pu[:], in1=sgt[:], op=ALU.mult),
                          R=[pu.d, sgt.d], W=[actT.d])
            for cb in range(4):
                banks = [PS[(cb % 2) * 4 + s] for s in range(4)]
                for f0 in range(0, NF, 16):
                    nf = min(16, NF - f0)
                    wt, wv = wload(o["wd_b"][f0 * 128:(f0 + nf) * 128, cb * 512:(cb + 1) * 512].rearrange("(f p) c -> p f c", p=128), nf)
                    if cb == 0 and f0 == 16 and j + 1 < nslot:
                        fw.dma(sp, ah[:], o["attnT"][:, q0 + 512:q0 + 1024].rearrange("(k p) t -> p k t", p=128), ah.sem, R=[drd["attnT"]], W=[ah.d])
                    for fi in range(nf):
                        f = f0 + fi
                        for s in range(4):
                            p = banks[s]
                            fw.op(pe, lambda e: e.matmul(p[:], lhsT=actT[:, f, s * 128:(s + 1) * 128], rhs=wv[:, fi, :],
                                                         start=(f == 0), stop=(f == NF - 1)), R=[actT.d, wt.d], W=[p.d])
                for s in range(4):
                    p = banks[s]
                    fw.op(dve, lambda e: e.tensor_tensor(out=x1[s][:, cb * 512:(cb + 1) * 512], in0=p[:],
                                                          in1=x1[s][:, cb * 512:(cb + 1) * 512], op=ALU.add), R=[p.d], W=[x1[s].d])
            for s in range(4):
                ss = rms_rstd(x1[s][:], x1[s].d)
                for hf in range(2):
                    og = ostg[ctr["o"] % 2]
                    ctr["o"] += 1
                    cs = slice(hf * 1024, (hf + 1) * 1024)
                    fw.op(dve, lambda e: e.scalar_tensor_tensor(out=og[:], in0=x1[s][:, cs], scalar=ss[:, 1:2], in1=gt_fin[:, cs],
                                                                 op0=ALU.mult, op1=ALU.mult), R=[x1[s].d, ss.d, gt_fin.d], W=[og.d])
                    fw.dma(pool, o["out"][q0 + s * 128:q0 + (s + 1) * 128, cs], og[:], og.sem, R=[og.d], W=[drd["out"]])
                if j + 1 < nslot:
                    fw.dma(pool, x1[s][:], o["xq"][q0 + 512 + s * 128:q0 + 512 + (s + 1) * 128, :], x1[s].sem, W=[x1[s].d])
        fw.es = old_es
        fw.end_phase()


def phase_cmp(fw, PS, C, o, drd, G):
    pe, act, dve, pool, sp = fw.pe, fw.act, fw.dve, fw.pool, fw.sp
    NCMP = SEQ // 16 - 1
    NBC = (NCMP + 127) // 128
    es2 = ExitStack()
    with es2:
        old_es = fw.es
        fw.es = es2
        for X, (srcT, w1n, w2n, posn) in enumerate([("kcT", "w1k", "w2k", "posk"), ("vcT", "w1v", "w2v", "posv")]):
            xt = T(fw.sb(f"cx{X}", [128, SEQ], BF16), sem=fw.dsem(f"cx{X}"))
            fw.dma(sp, xt[:], o[srcT], xt.sem, R=[drd[srcT]], W=[xt.d])
            w1 = T(fw.sb(f"cw1{X}", [128, 32, 128], BF16), sem=fw.dsem(f"cw1{X}", sw=True))
            fw.dmas(pool, [(w1[0:64], o[w1n]), (w1[64:128], o[w1n])], w1.sem, W=[w1.d])
            w2 = T(fw.sb(f"cw2{X}", [128, 64], BF16), sem=fw.dsem(f"cw2{X}", sw=True))
            fw.dma(pool, w2[:], o[w2n], w2.sem, W=[w2.d])
            pos = T(fw.sb(f"cpos{X}", [64, 32], BF16), sem=fw.dsem(f"cpos{X}", sw=True))
            fw.dma(pool, pos[:], o[posn], pos.sem, W=[pos.d])
            bias = T(fw.sb(f"cbias{X}", [128, 1], F32))
            pb = PS[7]
            for l in range(32):
                fw.op(pe, lambda e: e.matmul(pb[:, 0:1], lhsT=w1[0:64, l, :], rhs=pos[:, l:l + 1], start=(l == 0), stop=(l == 31)),
                      R=[w1.d, pos.d], W=[pb.d])
            fw.op(dve, lambda e: e.tensor_copy(out=bias[:], in_=pb[:, 0:1]), R=[pb.d], W=[bias.d])
            for g in range(2):
                ph = PS[g]
                for l in range(32):
                    rhs = xt[g * 64:(g + 1) * 64, l:l + 16 * (NCMP - 1) + 1:16]
                    fw.op(pe, lambda e: e.matmul(ph[:, 0:NCMP], lhsT=w1[g * 64:(g + 1) * 64, l, :], rhs=rhs, start=(l == 0), stop=(l == 31)),
                          R=[w1.d, xt.d], W=[ph.d])
                hs = T(fw.sb(f"chs{X}{g}", [128, 128 * NBC], BF16))
                fw.op(dve, lambda e: e.memset(hs[:], 0.0), W=[hs.d])
                fw.op(act, lambda e: e.activation(out=hs[:, 0:NCMP], in_=ph[:, 0:NCMP], func=AF.Silu, bias=bias[:, 0:1]),
                      R=[ph.d, bias.d], W=[hs.d])
                if X == 0:
                    kc = G["kcmpT"][g]
                    p2 = PS[2 + g]
                    fw.op(pe, lambda e: e.matmul(p2[0:64, 0:128 * NBC], lhsT=w2[:, :], rhs=hs[:, :], start=True, stop=True),
                          R=[w2.d, hs.d], W=[p2.d])
                    fw.op(dve, lambda e: e.tensor_copy(out=kc[0:64, 0:128 * NBC], in_=p2[0:64, 0:128 * NBC]), R=[p2.d], W=[kc.d])
                else:
                    vc = G["vcmp"][g]
                    p2 = PS[4 + g]
                    for nb in range(NBC):
                        fw.op(pe, lambda e: e.matmul(p2[:, nb * 64:(nb + 1) * 64], lhsT=hs[:, nb * 128:(nb + 1) * 128], rhs=w2[:, :],
                                                     start=True, stop=True), R=[w2.d, hs.d], W=[p2.d])
                    fw.op(dve, lambda e: e.tensor_copy(out=vc[:, 0:NBC, 0:64], in_=p2[:, 0:NBC * 64].rearrange("p (n d) -> p n d", d=64)),
                          R=[p2.d], W=[vc.d])
        fw.es = old_es
        fw.end_phase()


def phase_nsa(fw, PS, C, o, drd, G, nslot=NSLOT, groups=(0, 1), heads=range(8), bg=()):
    pe, act, dve, pool, sp = fw.pe, fw.act, fw.dve, fw.pool, fw.sp
    ones, ident = C["ones"], C["ident"]
    nkb_all = SEQ // 128
    es2 = ExitStack()
    with es2:
        old_es = fw.es
        fw.es = es2

        def cload(name, shape, dt, src, q=None, sw=False):
            t = T(fw.sb(name, shape, dt), sem=fw.dsem(name, sw=sw))
            fw.dma(pool if sw else sp, t[:], src, t.sem, W=[t.d])
            return t
        ovl = cload("ovl", [128, 4, 128], BF16, o["ovl"])
        inds = cload("inds4", [128, 64, 128], BF16, o["inds4"])
        oh = cload("oh", [48, 48, 64], BF16, o["oh"])
        bias_s = cload("bias_s", [128, 16 * 64], F32, o["bias_s"])
        bias_w = cload("bias_w", [128, 16 * 12], F32, o["bias_w"])
        bias_c = cload("bias_c", [128, 16 * 8 * 4], F32, o["bias_c"])
        vtab = [cload(f"vtab{i}", [128, 264], F32, o[f"vtab{'AB'[i]}"]) for i in range(2)]
        atab = [cload(f"atab{i}", [128, 264], F32, o[f"atab{'AB'[i]}"]) for i in range(2)]
        cmpneg = [cload(f"cmpneg{i}", [128, 2, 512], BF16, o[f"cmpneg{'AB'[i]}"]) for i in range(2)]
        kw = T(fw.sb("kw_sb", [128, 1536], BF16), sem=fw.dsem("kw_sb"))
        fw.op(pool, lambda e: e.memset(kw[64:128, :], 0.0), W=[kw.d])
        Vw = T(fw.sb("Vw_sb", [128, 12, 128], BF16), sem=fw.dsem("Vw_sb"))
        fw.op(pool, lambda e: e.memset(Vw[:, :, 64:128], 1.0), W=[Vw.d])
        qa = T(fw.sb("qa", [128, 8, 512], BF16), sem=fw.dsem("qa"))
        fw.op(pool, lambda e: e.memset(qa[:], 0.0), W=[qa.d])
        cneg = T(fw.sb("cneg", [128, 8, 512], BF16), sem=fw.dsem("cneg"))
        wneg = T(fw.sb("wneg", [128, 12, 512], BF16), sem=fw.dsem("wneg"))
        masks = T(fw.sb("masks", [128, max(nkb_all - 8, 1), 512], BF16))
        NPT = 6
        pts = [T(fw.sb(f"npt{i}", [128, 512], BF16)) for i in range(NPT)]
        reccs = [T(fw.sb(f"recc{i}", [128, 512], F32)) for i in range(2)]
        oc_sb = T(fw.sb("oc_sb", [64, 8, 512], BF16))
        imp_sb = T(fw.sb("imp_sb", [128, 128], F32))
        work = T(fw.sb("selwork", [128, 128], F32))
        m8 = T(fw.sb("m8", [128, 16], F32))
        sel = T(fw.sb("sel", [128, 128], BF16))
        selT = T(fw.sb("selT", [128, 512], BF16))
        gh = T(fw.sb("gh", [48, 512], BF16), sem=fw.dsem("gh"))
        gl = T(fw.sb("gl", [48, 512], BF16), sem=fw.dsem("gl"))
        gb = [T(fw.sb(f"gb{i}", [64, 512], F32)) for i in range(3)]
        rs = T(fw.sb("nrs", [64, 512], F32))
        tt = T(fw.sb("ntt", [64, 512], F32))
        acc = T(fw.sb("nacc", [64, 512], F32))
        ost = [T(fw.sb(f"nost{i}", [64, 512], BF16), sem=fw.dsem(f"nost{i}", sw=True)) for i in range(2)]
        ctr = dict(pt=0, ss=0, ost=0, acc=0, s=0)
        PS_S = [PS[0], PS[1], PS[2]]
        PS_MISC = PS[7]

        def next_s():
            p = PS_S[ctr["s"] % 3]
            ctr["s"] += 1
            return p

        def next_pt():
            p = pts[ctr["pt"] % NPT]
            ctr["pt"] += 1
            return p

        for g in groups:
            ksT, Vs = G["ksT"], G["Vs"]
            if g != groups[0]:
                G["load_kv"](g)
            kcm, vcm = G["kcmpT"][g], G["vcmp"][g]
            for j in range(nslot):
                q0 = j * 512
                par = j % 2
                nkb = 8 * (j + 1)
                nbc = (j + 2) // 2
                w0 = 4 if j == 0 else 0
                fw.dma(sp, qa[0:64, :, :], o["qnT"][g * 512:(g + 1) * 512, q0:q0 + 512].rearrange("(h d) t -> d h t", d=64), qa.sem,
                       R=[drd["qnT"]], W=[qa.d])
                fw.dma(sp, qa[64:70, :, :], o["qaug" + "AB"[par]][:, g * 8:(g + 1) * 8, :], qa.sem, W=[qa.d])
                fw.dma(sp, cneg[:], o["cneg" + "AB"[par]], cneg.sem, W=[cneg.d])
                fw.dma(sp, wneg[:], o["wneg" + "AB"[par]], wneg.sem, W=[wneg.d])
                kb0 = 8 * j - 4 + w0
                nw = 12 - w0
                fw.dma(sp, kw[0:64, w0 * 128:1536], o["kwT"][g * 64:(g + 1) * 64, kb0 * 128:(kb0 + nw) * 128], kw.sem,
                       R=[drd["kwT"]], W=[kw.d])
                fw.dma(sp, kw[64:70, :], o["kaug"][:, 0:1536], kw.sem, W=[kw.d])
                fw.dma(sp, Vw[:, w0:12, 0:64],
                       o["vsw"][kb0 * 128:(kb0 + nw) * 128, 128 + g * 64:128 + (g + 1) * 64].rearrange("(k p) d -> p k d", p=128),
                       Vw.sem, R=[drd["vsw"]], W=[Vw.d])
                fw.dma(sp, gh[:], o["gTh"][:, q0:q0 + 512], gh.sem, R=[drd["gTh"]], W=[gh.d])
                fw.dma(sp, gl[:], o["gTl"][:, q0:q0 + 512], gl.sem, R=[drd["gTl"]], W=[gl.d])
                for _ in range(3):
                    if bg:
                        bg.pop(0)()
                pimp = PS[5]
                fw.op(dve, lambda e: e.memset(pimp[:], 0.0), W=[pimp.d])
                for hi_, h in enumerate(heads):
                    h16 = g * 8 + h
                    pD, pO = (PS[3], PS[4]) if hi_ % 2 == 0 else (PS[6], PS[7])
                    recc = reccs[hi_ % 2]
                    pcs = []
                    for nb in range(nbc):
                        ps = next_s()
                        wl = nb - (nbc - 2)
                        fw.op(pe, lambda e: e.matmul(ps[:], lhsT=kcm[:, nb * 128:(nb + 1) * 128], rhs=qa[:, h, :], start=True, stop=(wl < 0)),
                              R=[kcm.d, qa.d], W=[ps.d])
                        if wl >= 0:
                            fw.op(pe, lambda e: e.matmul(ps[:], lhsT=ident[:], rhs=cmpneg[par][:, wl, :], start=False, stop=True),
                                  R=[ident.d, cmpneg[par].d], W=[ps.d])
                        bcol = (h16 * 8 + j) * 4 + nb
                        pt = next_pt()
                        fw.op(act, lambda e: e.activation(out=pt[:], in_=ps[:], func=AF.Exp, scale=0.125, bias=bias_c[:, bcol:bcol + 1]),
                              R=[ps.d, bias_c.d], W=[pt.d])
                        pcs.append(pt)
                    for nb in range(nbc):
                        fw.op(pe, lambda e: e.matmul(pD[:], lhsT=ones[:], rhs=pcs[nb][:], start=(nb == 0), stop=(nb == nbc - 1)),
                              R=[ones.d, pcs[nb].d], W=[pD.d])
                    fw.op(act, lambda e: e.activation(out=recc[:], in_=pD[:], func=AF.Ln, bias=C["tiny"][:, 0:1]), R=[pD.d, C["tiny"].d], W=[recc.d])
                    fw.op(act, lambda e: e.activation(out=recc[:], in_=recc[:], func=AF.Exp, scale=-1.0), W=[recc.d])
                    for nb in range(nbc):
                        fw.op(dve, lambda e: e.tensor_tensor(out=pcs[nb][:], in0=pcs[nb][:], in1=recc[:], op=ALU.mult), R=[recc.d], W=[pcs[nb].d])
                    for nb in range(nbc):
                        fw.op(pe, lambda e: e.matmul(pO[:], lhsT=vcm[:, nb, :], rhs=pcs[nb][:], start=(nb == 0), stop=(nb == nbc - 1)),
                              R=[vcm.d, pcs[nb].d], W=[pO.d])
                    for qb in range(4):
                        for nb in range(nbc):
                            fw.op(pe, lambda e: e.matmul(pimp[:, qb * 128:(qb + 1) * 128], lhsT=pcs[nb][:, qb * 128:(qb + 1) * 128],
                                                         rhs=ovl[:, nb, :], start=False, stop=False, skip_group_check=True),
                                  R=[ovl.d, pcs[nb].d], W=[pimp.d])
                    fw.op(act, lambda e: e.copy(out=oc_sb[:, h, :], in_=pO[0:64, :]), R=[pO.d], W=[oc_sb.d])
                pT = PS_MISC
                pTb = pT.ap.bitcast(BF16)
                for qb in range(4):
                    s0 = 134 - 16 * j - 2 * qb
                    fw.op(dve, lambda e: e.tensor_tensor(out=imp_sb[:], in0=pimp[:, qb * 128:(qb + 1) * 128], in1=vtab[par][:, s0:s0 + 128],
                                                          op=ALU.mult), R=[pimp.d, vtab[par].d], W=[imp_sb.d])
                    fw.op(dve, lambda e: e.tensor_tensor(out=imp_sb[:], in0=imp_sb[:], in1=atab[par][:, s0:s0 + 128], op=ALU.add),
                          R=[atab[par].d], W=[imp_sb.d])
                    fw.op(dve, lambda e: e.memset(imp_sb[:, 0:1], 1e4), W=[imp_sb.d])
                    fw.op(dve, lambda e: e.max(out=m8[:, 0:8], in_=imp_sb[:]), R=[imp_sb.d], W=[m8.d])
                    fw.op(dve, lambda e: e.match_replace(out=work[:], in_to_replace=m8[:, 0:8], in_values=imp_sb[:], imm_value=-1e9),
                          R=[imp_sb.d, m8.d], W=[work.d])
                    fw.op(dve, lambda e: e.max(out=m8[:, 8:16], in_=work[:]), R=[work.d], W=[m8.d])
                    fw.op(dve, lambda e: e.tensor_scalar(out=sel[:], in0=imp_sb[:], scalar1=m8[:, 15:16], scalar2=1.0, op0=ALU.is_ge, op1=ALU.subtract),
                          R=[imp_sb.d, m8.d], W=[sel.d])
                    fw.op(pe, lambda e: e.transpose(out=pTb[:, qb * 128:(qb + 1) * 128], in_=sel[:], identity=ident[:]),
                          R=[sel.d, ident.d], W=[pT.d])
                fw.op(act, lambda e: e.copy(out=selT[:], in_=pTb[:, 0:512]), R=[pT.d], W=[selT.d])
                for kb in range(nkb - 8):
                    pm = PS[3 + kb % 2]
                    fw.op(pe, lambda e: e.matmul(pm[:], lhsT=inds[:, kb, :], rhs=selT[:, :], start=True, stop=True),
                          R=[inds.d, selT.d], W=[pm.d])
                    if kb % 2:
                        fw.op(act, lambda e: e.activation(out=masks[:, kb, :], in_=pm[:], func=AF.Identity, scale=2.0 ** -30, bias=C["one"][:, 0:1]),
                              R=[pm.d, C["one"].d], W=[masks.d])
                    else:
                        fw.op(dve, lambda e: e.tensor_scalar(out=masks[:, kb, :], in0=pm[:], scalar1=2.0 ** -30, scalar2=1.0, op0=ALU.mult, op1=ALU.add),
                              R=[pm.d], W=[masks.d])
                for h in heads:
                    h16 = g * 8 + h
                    pS_ = PS[3 + 2 * (ctr["acc"] % 2)]
                    pW_ = PS[4 + 2 * (ctr["acc"] % 2)]
                    ctr["acc"] += 1
                    units = [("w", w) for w in range(w0, 12)] + [("s", kb) for kb in range(nkb)]

                    def qk(u):
                        kind, i = u
                        ps = next_s()
                        if kind == "w":
                            fw.op(pe, lambda e: e.matmul(ps[:], lhsT=kw[:, i * 128:(i + 1) * 128], rhs=qa[:, h, :], start=True, stop=False),
                                  R=[kw.d, qa.d], W=[ps.d])
                            fw.op(pe, lambda e: e.matmul(ps[:], lhsT=ident[:], rhs=wneg[:, i, :], start=False, stop=True),
                                  R=[ident.d, wneg.d], W=[ps.d])
                        else:
                            a = i // 32
                            w = i - (nkb - 8)
                            fw.op(pe, lambda e: e.matmul(ps[:], lhsT=ksT[:, i * 128:(i + 1) * 128], rhs=qa[:, h, :], start=True, stop=(w < 0)),
                                  R=[ksT.d, qa.d], W=[ps.d])
                            if w >= 0:
                                fw.op(pe, lambda e: e.matmul(ps[:], lhsT=inds[:, i, :], rhs=selT[:, :], start=False, stop=False),
                                      R=[inds.d, selT.d], W=[ps.d])
                                fw.op(pe, lambda e: e.matmul(ps[:], lhsT=ident[:], rhs=cneg[:, w, :], start=False, stop=True),
                                      R=[ident.d, cneg.d], W=[ps.d])
                        return ps

                    def pv(u, ps, first, last):
                        kind, i = u
                        pt = next_pt()
                        if kind == "w":
                            bcol = h16 * 12 + i
                            fw.op(act, lambda e: e.activation(out=pt[:], in_=ps[:], func=AF.Exp, scale=0.125, bias=bias_w[:, bcol:bcol + 1]),
                                  R=[ps.d, bias_w.d], W=[pt.d])
                            fw.op(pe, lambda e: e.matmul(pW_[:], lhsT=Vw[:, i, :], rhs=pt[:], start=first, stop=last), R=[Vw.d, pt.d], W=[pW_.d])
                        else:
                            bcol = h16 * 64 + (8 * j - i + 7)
                            fw.op(act, lambda e: e.activation(out=pt[:], in_=ps[:], func=AF.Exp, scale=0.125, bias=bias_s[:, bcol:bcol + 1]),
                                  R=[ps.d, bias_s.d], W=[pt.d])
                            if i < nkb - 8:
                                fw.op(dve, lambda e: e.tensor_tensor(out=pt[:], in0=pt[:], in1=masks[:, i, :], op=ALU.mult), R=[masks.d], W=[pt.d])
                            fw.op(pe, lambda e: e.matmul(pS_[:], lhsT=Vs[:, i, :], rhs=pt[:], start=first, stop=last), R=[Vs.d, pt.d], W=[pS_.d])

                    nwin = 12 - w0
                    pend = []
                    pend.append(qk(units[0]))
                    if len(units) > 1:
                        pend.append(qk(units[1]))
                    for ui, u in enumerate(units):
                        if ui + 2 < len(units):
                            pend.append(qk(units[ui + 2]))
                        ps = pend.pop(0)
                        if u[0] == "w":
                            pv(u, ps, ui == 0, ui == nwin - 1)
                        else:
                            pv(u, ps, ui == nwin, ui == len(units) - 1)
                    for b in range(3):
                        r = h16 * 3 + b
                        pg = PS_MISC
                        fw.op(pe, lambda e: e.matmul(pg[0:64, :], lhsT=oh[:, r, :], rhs=gh[:], start=True, stop=False), R=[oh.d, gh.d], W=[pg.d])
                        fw.op(pe, lambda e: e.matmul(pg[0:64, :], lhsT=oh[:, r, :], rhs=gl[:], start=False, stop=True), R=[oh.d, gl.d], W=[pg.d])
                        fw.op(act, lambda e: e.copy(out=gb[b][:], in_=pg[0:64, :]), R=[pg.d], W=[gb[b].d])
                    fw.op(dve, lambda e: e.tensor_tensor(out=acc[:], in0=oc_sb[:, h, :], in1=gb[0][:], op=ALU.mult), R=[oc_sb.d, gb[0].d], W=[acc.d])
                    for b, pacc in ((1, pS_), (2, pW_)):
                        fw.op(dve, lambda e: e.reciprocal(out=rs[:], in_=pacc[64:128, :]), R=[pacc.d], W=[rs.d])
                        fw.op(dve, lambda e: e.tensor_tensor(out=rs[:], in0=rs[:], in1=gb[b][:], op=ALU.mult), R=[gb[b].d], W=[rs.d])
                        fw.op(dve, lambda e: e.tensor_tensor(out=tt[:], in0=pacc[0:64, :], in1=rs[:], op=ALU.mult), R=[pacc.d, rs.d], W=[tt.d])
                        if b == 1:
                            fw.op(dve, lambda e: e.tensor_tensor(out=acc[:], in0=acc[:], in1=tt[:], op=ALU.add), R=[tt.d], W=[acc.d])
                        else:
                            og = ost[ctr["ost"] % 2]
                            ctr["ost"] += 1
                            fw.op(dve, lambda e: e.tensor_tensor(out=og[:], in0=acc[:], in1=tt[:], op=ALU.add), R=[acc.d, tt.d], W=[og.d])
                            fw.dma(pool, o["attnT"][1024 + h16 * 64:1024 + (h16 + 1) * 64, q0:q0 + 512], og[:], og.sem, R=[og.d], W=[drd["attnT"]])
        fw.es = old_es
        fw.end_phase()


def build(dbg=(), phases=("f", "k", "q", "c", "n", "m")):
    nc = bass.Bass("TRN2", target_bir_lowering=False)
    es = ExitStack()
    IN = {}

    def din(name, shape, dt=F32):
        IN[name] = nc.dram_tensor(name, list(shape), dt, kind="ExternalInput").ap()
        return IN[name]

    drd = {}

    def dscr(name, shape, dt):
        kind = "ExternalOutput" if name in dbg else "Internal"
        drd[name] = Dep()
        return nc.dram_tensor(name, list(shape), dt, kind=kind).ap()

    xs = din("xs", [SEQ, D])
    xq = din("xq", [NQ, D])
    attn_g = din("attn_g", [1, D])
    wk = din("wk", [D, 1152])
    wq = din("wq", [D, 1584])
    gkv = din("gkv", [128, 2])
    gq = din("gq", [128, 4])
    cosk = din("cosk", [64, SEQ])
    sink = din("sink", [64, SEQ])
    identd = din("identd", [128, 128], BF16)
    o_extra = dict(w_uq=din("w_uq", [512, 2048]), w_uk=din("w_uk", [256, 1024]), w_uv=din("w_uv", [256, 1024]),
                   cosq=din("cosq", [64, NQ]), sinq=din("sinq", [64, NQ]),
                   cmaskA=din("cmaskA", [128, 8, 512], BF16), cmaskB=din("cmaskB", [128, 8, 512], BF16))
    out = nc.dram_tensor("out", [NQ, D], F32, kind="ExternalOutput").ap()
    drd["out"] = Dep()

    o = dict(gkv=gkv, gq=gq, cosk=cosk, sink=sink, xq=xq, out=out)
    o.update(o_extra)
    o.update(dict(w_o=din("w_o", [2048, 2048]), w_gate=din("w_gate", [2048, DFF]), w_up=din("w_up", [2048, DFF]),
                  w_down=din("w_down", [DFF, 2048]), ffn_g=din("ffn_g", [1, D]), fin_g=din("fin_g", [1, D])))
    for nm, shp, dt in [("w1k", [64, 32, 128], F32), ("w1v", [64, 32, 128], F32), ("w2k", [128, 64], F32), ("w2v", [128, 64], F32),
                        ("posk", [64, 32], F32), ("posv", [64, 32], F32), ("kaug", [6, max(SEQ, 1536)], BF16), ("caug", [6, 512], BF16),
                        ("qaugA", [6, 16, 512], BF16), ("qaugB", [6, 16, 512], BF16), ("ovl", [128, 4, 128], BF16),
                        ("inds4", [128, 64, 128], BF16), ("oh", [48, 48, 64], BF16), ("bias_s", [128, 16 * 64], F32),
                        ("bias_w", [128, 16 * 12], F32), ("bias_c", [128, 16 * 8 * 4], F32),
                        ("cnegA", [128, 8, 512], BF16), ("cnegB", [128, 8, 512], BF16),
                        ("wnegA", [128, 12, 512], BF16), ("wnegB", [128, 12, 512], BF16),
                        ("cmpnegA", [128, 2, 512], BF16), ("cmpnegB", [128, 2, 512], BF16),
                        ("vtabA", [128, 264], F32), ("vtabB", [128, 264], F32), ("atabA", [128, 264], F32), ("atabB", [128, 264], F32)]:
        o[nm] = din(nm, shp, dt)
    for nm, shp in [("wo_b", [2048, 2048]), ("wg_b", [2048, DFF]), ("wu_b", [2048, DFF]), ("wd_b", [DFF, 2048])]:
        o[nm] = dscr(nm, shp, BF16)
    for nm, shp in [("ckvT", [256, SEQ]), ("kropeT", [64, SEQ]), ("kcT", [128, SEQ]), ("vcT", [128, SEQ]),
                    ("ksT", [128, SEQ]), ("kwT", [128, SEQ]), ("vsw", [SEQ, 256]), ("cqT", [512, NQ]),
                    ("qnT", [1024, NQ]), ("gTh", [48, NQ]), ("gTl", [48, NQ]), ("attnT", [2048, NQ])]:
        o[nm] = dscr(nm, shp, BF16)

    with es:
        fw = FW(nc, es)
        pe, act, dve, pool, sp = fw.pe, fw.act, fw.dve, fw.pool, fw.sp
        PS = [T(fw.ps(f"ps{i}", [128, 512], F32), dep=Dep(excl=True)) for i in range(8)]
        ident = T(fw.sb("ident", [128, 128], BF16), sem=fw.dsem("ident"))
        ones = T(fw.sb("ones", [128, 128], BF16))
        fw.dma(sp, ident[:], identd, ident.sem, W=[ident.d])
        fw.op(pool, lambda e: e.memset(ones[:], 1.0), W=[ones.d])
        epst = T(fw.sb("epst", [128, 1], F32))
        fw.op(pool, lambda e: e.memset(epst[:], EPS), W=[epst.d])
        tinyt = T(fw.sb("tinyt", [128, 1], F32))
        fw.op(pool, lambda e: e.memset(tinyt[:], 1e-30), W=[tinyt.d])
        onet = T(fw.sb("onet", [128, 1], F32))
        fw.op(pool, lambda e: e.memset(onet[:], 1.0), W=[onet.d])
        C = dict(ident=ident, ones=ones, eps=epst, tiny=tinyt, one=onet)

        if "k" in phases:
            phase_proj(fw, PS, C, xs, SEQ, attn_g, wk, 1152, "k", o, drd)
        if "q" in phases:
            phase_proj(fw, PS, C, xq, NQ, attn_g, wq, 1584, "q", o, drd)

        conv_chunks = convert_weight_chunks(fw, o, drd) if "f" in phases else []
        if "c" in phases or "n" in phases:
            G = dict(kcmpT=[T(fw.sb(f"kcmpT{g}", [128, 512], BF16), sem=fw.dsem(f"kcmpT{g}")) for g in range(2)],
                     vcmp=[T(fw.sb(f"vcmp{g}", [128, 4, 128], BF16)) for g in range(2)])
            for g in range(2):
                fw.op(pool, lambda e: e.memset(G["kcmpT"][g][:, :], 0.0), W=[G["kcmpT"][g].d])
                fw.dma(sp, G["kcmpT"][g][64:70, :], o["caug"], G["kcmpT"][g].sem, W=[G["kcmpT"][g].d])
                fw.op(pool, lambda e: e.memset(G["vcmp"][g][:, :, 0:64], 0.0), W=[G["vcmp"][g].d])
                fw.op(pool, lambda e: e.memset(G["vcmp"][g][:, :, 64:128], 1.0), W=[G["vcmp"][g].d])
        es_kv = ExitStack()
        if "n" in phases:
            fw.es = es_kv
            nkb_all = SEQ // 128
            ksT = T(fw.sb("ksT_sb", [128, SEQ], BF16), sem=fw.dsem("ksT_sb"))
            Vs = T(fw.sb("Vs_sb", [128, nkb_all, 128], BF16), sem=fw.dsem("Vs_sb"))
            G["ksT"], G["Vs"] = ksT, Vs
            fw.es = es
            fw.phase_dsems.remove(ksT.sem)
            fw.phase_dsems.remove(Vs.sem)
            fw.op(pool, lambda e: e.memset(ksT[64:128, :], 0.0), W=[ksT.d])
            fw.op(pool, lambda e: e.memset(Vs[:, :, 64:128], 1.0), W=[Vs.d])
            fw.dma(sp, ksT[64:70, :], o["kaug"][:, 0:SEQ], ksT.sem, W=[ksT.d])

            def load_group_kv(g):
                fw.dma(sp, ksT[0:64, :], o["ksT"][g * 64:(g + 1) * 64, :], ksT.sem, R=[drd["ksT"]], W=[ksT.d])
                fw.dma(sp, Vs[:, :, 0:64], o["vsw"][:, g * 64:(g + 1) * 64].rearrange("(k p) d -> p k d", p=128), Vs.sem,
                       R=[drd["vsw"]], W=[Vs.d])
            G["load_kv"] = load_group_kv
            load_group_kv(NSA_GROUPS_RUN[0])
        if "c" in phases:
            phase_cmp(fw, PS, C, o, drd, G)
        if "n" in phases:
            phase_nsa(fw, PS, C, o, drd, G, nslot=NSA_NSLOT_RUN, groups=NSA_GROUPS_RUN, heads=NSA_HEADS_RUN, bg=conv_chunks)
            es_kv.close()
        for c_ in conv_chunks:
            c_()
        conv_chunks.clear()
        if "m" in phases:
            phase_mla(fw, PS, C, o, drd, heads=MLA_HEADS_RUN, nslot=MLA_NSLOT_RUN)

        if "f" in phases:
            phase_ffn(fw, PS, C, o, drd, nslot=FFN_NSLOT_RUN)

        alld = []
        for v_ in drd.values():
            alld.extend(v_ if isinstance(v_, list) else [v_])
        fw.wait_all(sp, alld)
        fw.barrier()
    return nc


def own_token_index(half):
    return np.concatenate([np.arange(c * 512, (c + 1) * 512) for c in OWN_CHUNKS[half]])


def split3(x):
    x = x.astype(np.float32)
    a = x.astype(ml_dtypes.bfloat16)
    r = (x - a.astype(np.float32)).astype(np.float32)
    b = r.astype(ml_dtypes.bfloat16)
    c = (r - b.astype(np.float32)).astype(np.float32).astype(ml_dtypes.bfloat16)
    return a, b, c


def nsa_tables(half):
    bf = ml_dtypes.bfloat16
    NEGM = np.float32(-1e9)
    slopes = (2.0 ** (-8.0 * np.arange(1, 17, dtype=np.float32) / 16)).astype(np.float32)
    t = {}
    nka = max(SEQ, 1536)
    rk = (np.arange(nka) % 128).astype(np.float32)
    t["kaug"] = np.stack([np.ones(nka, np.float32)] * 3 + [-rk] * 3).astype(bf)
    rn = (16 * (np.arange(512) % 128)).astype(np.float32)
    t["caug"] = np.stack([np.ones(512, np.float32)] * 3 + [-rn] * 3).astype(bf)
    ch = (-8.0 * slopes).astype(np.float32)
    types = (0, 1) if half == 0 else (1, 0)
    for nm, ty in zip("AB", types):
        rq = (512 * ty + np.arange(512)).astype(np.float32)
        prod = (ch[:, None] * rq[None, :]).astype(np.float32)
        p3 = split3(prod)
        c3 = split3(np.broadcast_to(ch[:, None], (16, 512)).copy())
        t["qaug" + nm] = np.stack(list(p3) + list(c3)).astype(bf)
        kpos = (np.arange(8)[None, :, None] * 128 + np.arange(128)[:, None, None])
        qpos = (512 * ty + np.arange(512))[None, None, :]
        t["cneg" + nm] = np.where(kpos <= qpos, np.float32(0), NEGM).astype(bf)
        kposw = ((np.arange(12)[None, :, None] - 4) * 128 + np.arange(128)[:, None, None])
        dist = qpos - kposw
        t["wneg" + nm] = np.where((dist >= 0) & (dist < 512), np.float32(0), NEGM).astype(bf)
        x = np.arange(264)[None, :]
        relj = x - 8 - 8 * (4 * ty // 4) * 1 - 126 if False else (x - 8 - 8 * ty - 126)
        hb = (np.arange(128)[:, None] >= 64).astype(np.int64)
        valid = relj <= hb
        forced = (relj == hb) | (relj == hb - 1)
        t["vtab" + nm] = valid.astype(np.float32)
        t["atab" + nm] = (np.where(valid, 0.0, -1.0) + np.where(forced, 1e4, 0.0)).astype(np.float32)
    for nm, ty, par in (("A", types[0], 0), ("B", types[1], 1)):
        p = np.arange(128)[:, None, None]
        which = np.arange(2)[None, :, None]
        base = np.where(which == 1, 1024 * par, 1024 * par + 2048)
        qq = (512 * ty + np.arange(512))[None, None, :]
        ok = (16 * p + 31) <= (base + qq)
        t["cmpneg" + nm] = np.where(ok, np.float32(0), NEGM).astype(bf)
    n = np.arange(512)[:, None]
    jj = np.arange(128)[None, :]
    ov = np.clip(np.minimum(16 * n + 32, 64 * jj + 64) - np.maximum(16 * n, 64 * jj), 0, None).astype(np.float32) / 16.0
    t["ovl"] = np.ascontiguousarray(ov.reshape(4, 128, 128).transpose(1, 0, 2)).astype(bf)
    p = np.arange(128)[:, None, None]
    m = np.arange(64)[None, :, None]
    k = np.arange(128)[None, None, :]
    t["inds4"] = ((p == 2 * m + (k >= 64)).astype(np.float32) * np.float32(2.0 ** 30)).astype(bf)
    t["oh"] = np.broadcast_to(np.eye(48, dtype=np.float32)[:, :, None], (48, 48, 64)).astype(bf)
    off = np.arange(64) - 7
    bs = (-slopes[:, None] * 128.0 * off[None, :]).astype(np.float32).reshape(1, 16 * 64)
    t["bias_s"] = np.broadcast_to(bs, (128, 16 * 64)).astype(np.float32)
    bw = (-slopes[:, None] * 128.0 * (4 - np.arange(12))[None, :]).astype(np.float32).reshape(1, 16 * 12)
    t["bias_w"] = np.broadcast_to(bw, (128, 16 * 12)).astype(np.float32)
    jv = np.arange(8)[None, :, None]
    nbv = np.arange(4)[None, None, :]
    bc = (-slopes[:, None, None] * (1024.0 * jv - 2048.0 * nbv - 31.0)).astype(np.float32).reshape(1, 16 * 8 * 4)
    t["bias_c"] = np.broadcast_to(bc, (128, 16 * 8 * 4)).astype(np.float32)
    return {k_: np.ascontiguousarray(v) for k_, v in t.items()}


def prep_shared(inp):
    w_in = inp["w_in"][0]
    offs = np.cumsum([0, 512, 256, 64, 1024, 128, 128, 128, 128, 128, 128, 48])
    c_q, c_kv, k_rope, nsa_q, k_c, v_c, k_s, v_s, k_w, v_w, g_raw = [slice(offs[i], offs[i + 1]) for i in range(11)]
    kr = w_in[:, k_rope]
    kr_sw = np.concatenate([kr[:, 32:], kr[:, :32]], axis=1)
    wk = np.concatenate([w_in[:, c_kv], kr, kr_sw, w_in[:, k_c], w_in[:, v_c], w_in[:, k_s], w_in[:, k_w],
                         w_in[:, v_s], w_in[:, v_w]], axis=1)
    wq = np.concatenate([w_in[:, c_q], w_in[:, nsa_q], w_in[:, g_raw]], axis=1)
    cosk, sink = rope_tables(np.arange(SEQ))
    w_uq = inp["w_uq"][0].reshape(512, 8, 192)
    w_uq_ext = np.concatenate([w_uq, w_uq[:, :, 160:192], w_uq[:, :, 128:160]], axis=2).reshape(512, 2048)
    m = dict(
        attn_g=np.ascontiguousarray(inp["attn_norm_g"].reshape(1, D)),
        wk=np.ascontiguousarray(wk), wq=np.ascontiguousarray(wq),
        gkv=np.ascontiguousarray(inp["mla_kv_norm_g"].reshape(2, 128).T),
        gq=np.ascontiguousarray(inp["mla_q_norm_g"].reshape(4, 128).T),
        cosk=cosk, sink=sink,
        w_uq=np.ascontiguousarray(w_uq_ext), w_uk=np.ascontiguousarray(inp["w_uk"][0]), w_uv=np.ascontiguousarray(inp["w_uv"][0]),
        w_o=np.ascontiguousarray(inp["w_o"][0]), w_gate=np.ascontiguousarray(inp["w_gate"][0]),
        w_up=np.ascontiguousarray(inp["w_up"][0]), w_down=np.ascontiguousarray(inp["w_down"][0]),
        ffn_g=np.ascontiguousarray(inp["ffn_norm_g"].reshape(1, D)), fin_g=np.ascontiguousarray(inp["final_norm_g"].reshape(1, D)),
        w1k=np.ascontiguousarray(inp["w_cmp_k1"][0].reshape(32, 64, 128).transpose(1, 0, 2)),
        w1v=np.ascontiguousarray(inp["w_cmp_v1"][0].reshape(32, 64, 128).transpose(1, 0, 2)),
        w2k=np.ascontiguousarray(inp["w_cmp_k2"][0]), w2v=np.ascontiguousarray(inp["w_cmp_v2"][0]),
        posk=np.ascontiguousarray(inp["cmp_pos_k"][0].T), posv=np.ascontiguousarray(inp["cmp_pos_v"][0].T),
        identd=np.eye(128, dtype=np.float32).astype(ml_dtypes.bfloat16),
    )
    return m


_HALF_CACHE = {}


def prep_half(half):
    if half in _HALF_CACHE:
        return _HALF_CACHE[half]
    own = own_token_index(half)
    cosq, sinq = rope_tables(own)
    kpos = (np.arange(8)[None, :, None] * 128 + np.arange(128)[:, None, None])
    qrel = np.arange(512)[None, None, :]
    mE = (kpos <= qrel).astype(np.float32).astype(ml_dtypes.bfloat16)
    mO = (kpos <= qrel + 512).astype(np.float32).astype(ml_dtypes.bfloat16)
    cmA, cmB = (mE, mO) if half == 0 else (mO, mE)
    m = dict(cosq=cosq, sinq=sinq, cmaskA=np.ascontiguousarray(cmA), cmaskB=np.ascontiguousarray(cmB))
    m.update(nsa_tables(half))
    _HALF_CACHE[half] = m
    return m


def prep_inputs(inp, core, shared=None):
    b, half = core // 2, core % 2
    own = own_token_index(half)
    x = inp["x"]
    m = dict(shared if shared is not None else prep_shared(inp))
    m["xs"] = np.ascontiguousarray(x[b])
    m["xq"] = np.ascontiguousarray(x[b][own])
    m.update(prep_half(half))
    return m


def kernel(**inputs):
    inp = {k: np.asarray(v) for k, v in inputs.items()}
    nc = build()
    shared = prep_shared(inp)
    maps = [prep_inputs(inp, c, shared) for c in range(8)]
    res = run_bass_kernel_spmd(nc, maps, core_ids=list(range(8)))
    out = np.empty((4, SEQ, D), np.float32)
    for c in range(8):
        out[c // 2][own_token_index(c % 2)] = np.asarray(res.results[c]["out"], dtype=np.float32)
    return out
```

```python
import numpy as np
import ml_dtypes
from contextlib import ExitStack
import concourse.bass as bass
import concourse.mybir as mybir
from concourse.bass_utils import run_bass_kernel_spmd

F32 = mybir.dt.float32
BF16 = mybir.dt.bfloat16
ALU = mybir.AluOpType
AF = mybir.ActivationFunctionType
AX = mybir.AxisListType

D = 2048
SEQ = 8192
NQ = 4096
NSLOT = 8
EPS = 1e-6
DFF = 5632
STAGE = 99
MLA_HEADS_RUN = range(8)
MLA_NSLOT_RUN = NSLOT
FFN_NSLOT_RUN = NSLOT
NSA_NSLOT_RUN = NSLOT
NSA_GROUPS_RUN = (0, 1)
NSA_HEADS_RUN = range(8)
OWN_CHUNKS = ([0, 3, 4, 7, 8, 11, 12, 15], [1, 2, 5, 6, 9, 10, 13, 14])


class Sem:
    def __init__(self, h, is_dma):
        self.h = h
        self.total = 0
        self.is_dma = is_dma
        self.sw = False


class Dep:
    __slots__ = ("w", "r", "excl")

    def __init__(self, excl=False):
        self.w = {}
        self.r = {}
        self.excl = excl


class Q:
    def __init__(self, eng, sem, name, self_wait=True):
        self.eng = eng
        self.sem = sem
        self.name = name
        self.seen = {}
        self.self_wait = self_wait


class FW:
    def __init__(self, nc, es):
        self.nc = nc
        self.es = es
        self.es0 = es
        self.nsem = 0
        self.all_sems = []
        self.free_dsems = {False: [], True: []}
        self.phase_dsems = []
        self.pe = Q(nc.tensor, self.sem("pe"), "pe", self_wait=False)
        self.act = Q(nc.scalar, self.sem("act"), "act")
        self.dve = Q(nc.vector, self.sem("dve"), "dve")
        self.pool = Q(nc.gpsimd, self.sem("pool"), "pool")
        self.sp = Q(nc.sync, self.sem("sp"), "sp")
        self.ninst = 0

    def sem(self, name, is_dma=False):
        self.nsem += 1
        s = Sem(self.es0.enter_context(self.nc.semaphore("m_" + name)), is_dma)
        self.all_sems.append(s)
        return s

    def dsem(self, name, sw=False):
        pool = self.free_dsems[sw]
        if pool:
            s = pool.pop()
        else:
            s = self.sem(name, True)
            s.sw = sw
        self.phase_dsems.append(s)
        return s

    def end_phase(self):
        self.barrier()
        for s in self.phase_dsems:
            self.free_dsems[s.sw].append(s)
        self.phase_dsems = []

    def sb(self, name, shape, dt):
        return self.es.enter_context(self.nc.sbuf_tensor("s_" + name, shape, dt))

    def ps(self, name, shape, dt):
        return self.es.enter_context(self.nc.psum_tensor("p_" + name, shape, dt))

    def _wait(self, q, R, W):
        need = {}
        for d in R:
            for s, v in d.w.items():
                if need.get(s, 0) < v:
                    need[s] = v
            if d.excl:
                for s, v in d.r.items():
                    if s is not q.sem and need.get(s, 0) < v:
                        need[s] = v
        for d in W:
            for s, v in d.w.items():
                if need.get(s, 0) < v:
                    need[s] = v
            for s, v in d.r.items():
                if need.get(s, 0) < v:
                    need[s] = v
        for s, v in need.items():
            if s is q.sem and not q.self_wait:
                continue
            if s.is_dma:
                v = s.total
            if q.seen.get(s, 0) < v:
                q.eng.wait_ge(s.h, v)
                q.seen[s] = v

    def op(self, q, f, R=(), W=()):
        self._wait(q, R, W)
        ins = f(q.eng)
        q.sem.total += 1
        ins.then_inc(q.sem.h, 1)
        v = q.sem.total
        for d in R:
            d.r[q.sem] = v
        for d in W:
            d.w = {q.sem: v}
            d.r = {}
        self.ninst += 1

    def dma(self, q, out, in_, sem, R=(), W=()):
        assert sem.sw == (q is self.pool), "semaphore/queue kind mismatch"
        self._wait(q, R, W)
        ins = q.eng.dma_start(out=out, in_=in_)
        sem.total += 16
        ins.then_inc(sem.h, 16)
        for d in R:
            d.r[sem] = sem.total
        for d in W:
            d.w = {sem: sem.total}
            d.r = {}
        self.ninst += 1

    def dmas(self, q, pairs, sem, R=(), W=()):
        assert sem.sw == (q is self.pool), "semaphore/queue kind mismatch"
        self._wait(q, R, W)
        for (o, i) in pairs:
            ins = q.eng.dma_start(out=o, in_=i)
            sem.total += 16
            ins.then_inc(sem.h, 16)
            self.ninst += 1
        for d in R:
            d.r[sem] = sem.total
        for d in W:
            d.w = {sem: sem.total}
            d.r = {}

    def barrier(self):
        qs = [self.pe, self.act, self.dve, self.pool, self.sp]
        for q in qs:
            for s in self.all_sems:
                if s.total > 0 and q.seen.get(s, 0) < s.total and not (s is q.sem and not q.self_wait):
                    q.eng.wait_ge(s.h, s.total)
                    q.seen[s] = s.total

    def wait_all(self, q, deps):
        self._wait(q, deps, ())


class T:
    def __init__(self, ap, dep=None, sem=None):
        self.ap = ap
        self.d = dep if dep is not None else Dep()
        self.sem = sem

    def __getitem__(self, k):
        return self.ap[k]


def rope_tables(pos):
    inv = (10000.0 ** (-np.arange(0, 64, 2, dtype=np.float32) / np.float32(64))).astype(np.float32)
    ang = (pos.astype(np.float32)[None, :] * inv[:, None]).astype(np.float32)
    c = np.cos(ang).astype(np.float32)
    s = np.sin(ang).astype(np.float32)
    cos_t = np.concatenate([c, c], axis=0)
    sin_t = np.concatenate([-s, s], axis=0)
    return np.ascontiguousarray(cos_t), np.ascontiguousarray(sin_t)


def bcast_rows(ap_row, nparts, n):
    return bass.AP(ap_row.tensor, ap_row.offset, [[0, nparts], [1, n]])


def phase_proj(fw, PS, C, x_dram, ntok, g_dram, w_dram, ncols, side, o, drd):
    nc = fw.nc
    pe, act, dve, pool, sp = fw.pe, fw.act, fw.dve, fw.pool, fw.sp
    ident, ones = C["ident"], C["ones"]
    es2 = ExitStack()
    with es2:
        old_es = fw.es
        fw.es = es2
        wsb = T(fw.sb(f"w{side}", [128, 16, ncols], BF16), sem=fw.dsem(f"w{side}", sw=True))
        fw.dmas(pool, [(wsb[:, kc, :], w_dram[kc * 128:(kc + 1) * 128, :]) for kc in range(16)], wsb.sem, W=[wsb.d])
        gtab = T(fw.sb(f"gtab{side}", [128, D], F32), sem=fw.dsem(f"gtab{side}"))
        fw.dma(sp, gtab[:], bcast_rows(g_dram, 128, D), gtab.sem, W=[gtab.d])
        xt = [T(fw.sb(f"xt{side}{i}", [128, D], F32), sem=fw.dsem(f"xt{side}{i}")) for i in range(2)]
        junk = T(fw.sb(f"junk{side}", [128, D], BF16))
        hb = [T(fw.sb(f"hb{side}{i}", [128, D], BF16)) for i in range(4)]
        hT = [T(fw.sb(f"hT{side}{i}", [128, 16, 512], BF16)) for i in range(2)]
        ssb = [T(fw.sb(f"ss{side}{i}", [128, 2], F32)) for i in range(2)]
        NST = 4
        stg = [T(fw.sb(f"stg{side}{i}", [128, 512], BF16), sem=fw.dsem(f"stg{side}{i}", sw=True)) for i in range(NST)]
        stg_i = [0]

        def stage():
            t = stg[stg_i[0] % NST]
            stg_i[0] += 1
            return t

        if side == "k":
            gsm = T(fw.sb("gkv_sb", [128, 2], F32), sem=fw.dsem("gkv"))
            fw.dma(sp, gsm[:], o["gkv"], gsm.sem, W=[gsm.d])
            NLAT = 2
            costab = [T(fw.sb(f"cos{i}", [64, 512], F32), sem=fw.dsem(f"cos{i}")) for i in range(2)]
            sintab = [T(fw.sb(f"sin{i}", [64, 512], F32), sem=fw.dsem(f"sin{i}")) for i in range(2)]
            rt = [T(fw.sb(f"rt{i}", [64, 512], F32)) for i in range(2)]
        else:
            gsm = T(fw.sb("gq_sb", [128, 4], F32), sem=fw.dsem("gq"))
            fw.dma(sp, gsm[:], o["gq"], gsm.sem, W=[gsm.d])
            NLAT = 4
            gf = T(fw.sb("gf", [48, 512], F32))
            ghi = T(fw.sb("ghi", [48, 512], BF16), sem=fw.dsem("ghi", sw=True))
            glo = T(fw.sb("glo", [48, 512], BF16), sem=fw.dsem("glo", sw=True))
        raw = [T(fw.sb(f"raw{side}{i}", [128, 512], F32)) for i in range(NLAT)]
        sq = [T(fw.sb(f"sq{side}{i}", [128, 512], BF16)) for i in range(NLAT)]
        rsb = T(fw.sb(f"rsb{side}", [128, 512], F32))
        psrot = [0]

        def nextps():
            p = PS[2 + psrot[0] % 6]
            psrot[0] += 1
            return p

        evac_rr = [0]

        def evac_copy(dst_ap, src_ap, R, W):
            evac_rr[0] += 1
            if evac_rr[0] % 2:
                fw.op(act, lambda e: e.copy(out=dst_ap, in_=src_ap), R=R, W=W)
            else:
                fw.op(dve, lambda e: e.tensor_copy(out=dst_ap, in_=src_ap), R=R, W=W)

        NB = ntok // 512

        def norm_part(tb):
            for s in range(4):
                i = tb * 4 + s
                xb = xt[i % 2]
                hbb = hb[s]
                ss = ssb[i % 2]
                fw.dma(sp, xb[:], x_dram[i * 128:(i + 1) * 128, :], xb.sem, W=[xb.d])
                fw.op(act, lambda e: e.memzero(ss[:, 0:1]), W=[ss.d])
                fw.op(act, lambda e: e.activation(out=junk[:], in_=xb[:], func=AF.Square, accum_out=ss[:, 0:1]),
                      R=[xb.d], W=[junk.d, ss.d])
                fw.op(act, lambda e: e.activation(out=ss[:, 1:2], in_=ss[:, 0:1], func=AF.Sqrt, scale=1.0 / D, bias=C["eps"][:, 0:1]),
                      R=[C["eps"].d], W=[ss.d])
                fw.op(dve, lambda e: e.reciprocal(out=ss[:, 1:2], in_=ss[:, 1:2]), W=[ss.d])
                fw.op(dve, lambda e: e.scalar_tensor_tensor(out=hbb[:], in0=xb[:], scalar=ss[:, 1:2], in1=gtab[:],
                                                             op0=ALU.mult, op1=ALU.mult),
                      R=[xb.d, ss.d, gtab.d], W=[hbb.d])

        def transpose_part(tb):
            h = hT[tb % 2]
            for s in range(4):
                hbb = hb[s]
                for half in range(2):
                    pt = PS[half]
                    ptb = pt.ap.bitcast(BF16)
                    for j in range(8):
                        kc = half * 8 + j
                        fw.op(pe, lambda e: e.transpose(out=ptb[:, j * 128:(j + 1) * 128], in_=hbb[:, kc * 128:(kc + 1) * 128],
                                                        identity=ident[:]),
                              R=[hbb.d, ident.d], W=[pt.d])
                    evac_copy(h[:, half * 8:(half + 1) * 8, s * 128:(s + 1) * 128],
                              ptb.rearrange("p (j t) -> p j t", j=8), [pt.d], [h.d])

        norm_part(0)
        transpose_part(0)
        for tb in range(NB):
            h = hT[tb % 2]
            t0 = tb * 512
            if side == "k":
                ct, st_ = costab[tb % 2], sintab[tb % 2]
                fw.dma(sp, ct[:], o["cosk"][:, t0:t0 + 512], ct.sem, W=[ct.d])
                fw.dma(sp, st_[:], o["sink"][:, t0:t0 + 512], st_.sem, W=[st_.d])
            if tb + 1 < NB:
                norm_part(tb + 1)
            def fm_group(c0, m):
                p = nextps()
                for kc in range(16):
                    fw.op(pe, lambda e: e.matmul(p[0:m, :], lhsT=wsb[:, kc, c0:c0 + m], rhs=h[:, kc, :],
                                                 start=(kc == 0), stop=(kc == 15)),
                          R=[wsb.d, h.d], W=[p.d])
                return p

            def store(dst, src_t, m, dd):
                fw.dma(pool, dst, src_t[0:m, :], src_t.sem, R=[src_t.d], W=[dd])

            lat_out = o["ckvT"] if side == "k" else o["cqT"]
            lat_dep = drd["ckvT"] if side == "k" else drd["cqT"]
            for c in range(NLAT):
                p = fm_group(c * 128, 128)
                fw.op(act, lambda e: e.activation(out=sq[c][:], in_=p[:], func=AF.Square), R=[p.d], W=[sq[c].d])
                fw.op(dve, lambda e: e.tensor_copy(out=raw[c][:], in_=p[:]), R=[p.d], W=[raw[c].d])
            p = nextps()
            for c in range(NLAT):
                fw.op(pe, lambda e: e.matmul(p[:], lhsT=ones[:], rhs=sq[c][:], start=(c == 0), stop=(c == NLAT - 1)),
                      R=[ones.d, sq[c].d], W=[p.d])
            fw.op(act, lambda e: e.activation(out=rsb[:], in_=p[:], func=AF.Sqrt, scale=1.0 / (128 * NLAT), bias=C["eps"][:, 0:1]),
                  R=[p.d, C["eps"].d], W=[rsb.d])
            fw.op(dve, lambda e: e.reciprocal(out=rsb[:], in_=rsb[:]), W=[rsb.d])
            for c in range(NLAT):
                sg = stage()
                fw.op(dve, lambda e: e.scalar_tensor_tensor(out=sg[:], in0=raw[c][:], scalar=gsm[:, c:c + 1], in1=rsb[:],
                                                             op0=ALU.mult, op1=ALU.mult),
                      R=[raw[c].d, gsm.d, rsb.d], W=[sg.d])
                store(lat_out[c * 128:(c + 1) * 128, t0:t0 + 512], sg, 128, lat_dep)
            if side == "k":
                px = fm_group(256, 64)
                pw = fm_group(320, 64)
                r1, r2 = rt
                fw.op(dve, lambda e: e.tensor_tensor(out=r1[:], in0=px[0:64, :], in1=ct[:], op=ALU.mult),
                      R=[px.d, ct.d], W=[r1.d])
                fw.op(dve, lambda e: e.tensor_tensor(out=r2[:], in0=pw[0:64, :], in1=st_[:], op=ALU.mult),
                      R=[pw.d, st_.d], W=[r2.d])
                sg = stage()
                fw.op(dve, lambda e: e.tensor_tensor(out=sg[0:64, :], in0=r1[:], in1=r2[:], op=ALU.add),
                      R=[r1.d, r2.d], W=[sg.d])
                store(o["kropeT"][:, t0:t0 + 512], sg, 64, drd["kropeT"])
                for gi, nm in enumerate(["kcT", "vcT", "ksT", "kwT"]):
                    p = fm_group(384 + gi * 128, 128)
                    sg = stage()
                    evac_copy(sg[:], p[:], [p.d], [sg.d])
                    store(o[nm][:, t0:t0 + 512], sg, 128, drd[nm])
                for s in range(4):
                    p = nextps()
                    for kc in range(16):
                        fw.op(pe, lambda e: e.matmul(p[:, 0:256], lhsT=h[:, kc, s * 128:(s + 1) * 128], rhs=wsb[:, kc, 896:1152],
                                                     start=(kc == 0), stop=(kc == 15)),
                              R=[wsb.d, h.d], W=[p.d])
                    sg = stage()
                    evac_copy(sg[:, 0:256], p[:, 0:256], [p.d], [sg.d])
                    fw.dma(pool, o["vsw"][t0 + s * 128:t0 + (s + 1) * 128, :], sg[:, 0:256], sg.sem, R=[sg.d], W=[drd["vsw"]])
            else:
                for gi in range(8):
                    p = fm_group(512 + gi * 128, 128)
                    sg = stage()
                    evac_copy(sg[:], p[:], [p.d], [sg.d])
                    store(o["qnT"][gi * 128:(gi + 1) * 128, t0:t0 + 512], sg, 128, drd["qnT"])
                p = fm_group(1536, 48)
                fw.op(act, lambda e: e.activation(out=gf[:], in_=p[0:48, :], func=AF.Sigmoid), R=[p.d], W=[gf.d])
                fw.op(dve, lambda e: e.tensor_copy(out=ghi[:], in_=gf[:]), R=[gf.d], W=[ghi.d])
                fw.op(dve, lambda e: e.tensor_tensor(out=glo[:], in0=gf[:], in1=ghi[:], op=ALU.subtract),
                      R=[gf.d, ghi.d], W=[glo.d])
                fw.dma(pool, o["gTh"][:, t0:t0 + 512], ghi[:], ghi.sem, R=[ghi.d], W=[drd["gTh"]])
                fw.dma(pool, o["gTl"][:, t0:t0 + 512], glo[:], glo.sem, R=[glo.d], W=[drd["gTl"]])
            if tb + 1 < NB:
                transpose_part(tb + 1)
        fw.es = old_es
        fw.end_phase()


MLA_SCALE = 192.0 ** -0.5


def phase_mla(fw, PS, C, o, drd, heads=range(8), nslot=NSLOT):
    pe, act, dve, pool, sp = fw.pe, fw.act, fw.dve, fw.pool, fw.sp
    ones = C["ones"]
    nkb_all = SEQ // 128
    es2 = ExitStack()
    with es2:
        old_es = fw.es
        fw.es = es2
        krope = T(fw.sb("krope", [128, SEQ], BF16), sem=fw.dsem("krope"))
        fw.op(pool, lambda e: e.memset(krope[64:128, :], 0.0), W=[krope.d])
        fw.dma(sp, krope[0:64, :], o["kropeT"], krope.sem, R=[drd["kropeT"]], W=[krope.d])
        wuq = T(fw.sb("wuq", [128, 4, 2048], BF16), sem=fw.dsem("wuq", sw=True))
        fw.dmas(pool, [(wuq[:, c, :], o["w_uq"][c * 128:(c + 1) * 128, :]) for c in range(4)], wuq.sem, W=[wuq.d])
        wuk = T(fw.sb("wuk", [128, 2, 1024], BF16), sem=fw.dsem("wuk", sw=True))
        fw.dmas(pool, [(wuk[:, c, :], o["w_uk"][c * 128:(c + 1) * 128, :]) for c in range(2)], wuk.sem, W=[wuk.d])
        wuv = T(fw.sb("wuv", [128, 2, 1024], BF16), sem=fw.dsem("wuv", sw=True))
        fw.dmas(pool, [(wuv[:, c, :], o["w_uv"][c * 128:(c + 1) * 128, :]) for c in range(2)], wuv.sem, W=[wuv.d])
        dhi = T(fw.sb("dhi", [128, 512], BF16))
        dlo = T(fw.sb("dlo", [128, 512], BF16))
        masks = []
        for i, nm in enumerate(["cmaskA", "cmaskB"]):
            mt = T(fw.sb(nm, [128, 8, 512], BF16), sem=fw.dsem(nm))
            fw.dma(sp, mt[:], o[nm], mt.sem, W=[mt.d])
            masks.append(mt)
        Kh = [T(fw.sb(f"Kh{i}", [128, SEQ], BF16)) for i in range(2)]
        Vh = [T(fw.sb(f"Vh{i}", [128, nkb_all, 128], BF16)) for i in range(2)]
        ckv = [T(fw.sb(f"ckv{i}", [128, 2, 512], BF16), sem=fw.dsem(f"ckv{i}")) for i in range(2)]
        cq = [T(fw.sb(f"cq{i}", [128, 4, 512], BF16), sem=fw.dsem(f"cq{i}")) for i in range(2)]
        cosq = [T(fw.sb(f"cosq{i}", [64, 512], F32), sem=fw.dsem(f"cosq{i}")) for i in range(2)]
        sinq = [T(fw.sb(f"sinq{i}", [64, 512], F32), sem=fw.dsem(f"sinq{i}")) for i in range(2)]
        qn = [T(fw.sb(f"qn{i}", [128, 512], BF16)) for i in range(2)]
        qr = [T(fw.sb(f"qr{i}", [128, 512], BF16)) for i in range(2)]
        for t_ in qr:
            fw.op(pool, lambda e: e.memset(t_[64:128, :], 0.0), W=[t_.d])
        r1 = [T(fw.sb(f"mr1{i}", [64, 512], F32)) for i in range(2)]
        r2 = [T(fw.sb(f"mr2{i}", [64, 512], F32)) for i in range(2)]
        NPT = 4
        pts = [T(fw.sb(f"pt{i}", [128, 512], BF16)) for i in range(NPT)]
        dacc = [T(fw.sb(f"dacc{i}", [128, 512], F32)) for i in range(4)]
        rec = T(fw.sb("rec", [128, 512], F32))
        ost = [T(fw.sb(f"ost{i}", [128, 512], BF16), sem=fw.dsem(f"ost{i}", sw=True)) for i in range(2)]
        PSO = [PS[3], PS[5]]
        PSDS = [PS[4], PS[6]]
        PSGEN = [PS[7], PS[7]]
        hpos = {}
        ctr = dict(ckv=0, slot=0, pt=0, ost=0, od=0, gen=0)

        def genbank():
            p = PSGEN[ctr["gen"] % 2]
            ctr["gen"] += 1
            return p

        def kvgen_tasks(h):
            K_, V_ = Kh[hpos[h] % 2], Vh[hpos[h] % 2]
            tasks = []
            for tb in range(SEQ // 512):
                st8 = {}

                def tk(tb=tb, st8=st8):
                    ck = ckv[ctr["ckv"] % 2]
                    ctr["ckv"] += 1
                    st8["ck"] = ck
                    fw.dma(sp, ck[:], o["ckvT"][:, tb * 512:(tb + 1) * 512].rearrange("(c p) t -> p c t", p=128), ck.sem,
                           R=[drd["ckvT"]], W=[ck.d])
                    pg = genbank()
                    for c in range(2):
                        fw.op(pe, lambda e: e.matmul(pg[:], lhsT=wuk[:, c, h * 128:(h + 1) * 128], rhs=ck[:, c, :],
                                                     start=(c == 0), stop=(c == 1)), R=[wuk.d, ck.d], W=[pg.d])
                    fw.op(act, lambda e: e.copy(out=K_[:, tb * 512:(tb + 1) * 512], in_=pg[:]), R=[pg.d], W=[K_.d])

                def tv(tb=tb, st8=st8):
                    ck = st8["ck"]
                    pg = genbank()
                    for s in range(4):
                        for c in range(2):
                            fw.op(pe, lambda e: e.matmul(pg[:, s * 128:(s + 1) * 128], lhsT=ck[:, c, s * 128:(s + 1) * 128],
                                                         rhs=wuv[:, c, h * 128:(h + 1) * 128], start=(c == 0), stop=(c == 1)),
                                  R=[wuv.d, ck.d], W=[pg.d])
                    fw.op(dve, lambda e: e.tensor_copy(out=V_[:, tb * 4:(tb + 1) * 4, :], in_=pg[:].rearrange("p (s d) -> p s d", s=4)),
                          R=[pg.d], W=[V_.d])
                tasks += [tk, tv]
            return tasks

        def qgen_tasks(h, j):
            i = ctr["slot"] % 2
            ctr["slot"] += 1
            cqt, ct, st_, qnt, qrt, r1_, r2_ = cq[i], cosq[i], sinq[i], qn[i], qr[i], r1[i], r2[i]
            q0 = j * 512

            def t0():
                fw.dma(sp, cqt[:], o["cqT"][:, q0:q0 + 512].rearrange("(c p) t -> p c t", p=128), cqt.sem, R=[drd["cqT"]], W=[cqt.d])
                fw.dma(sp, ct[:], o["cosq"][:, q0:q0 + 512], ct.sem, W=[ct.d])
                fw.dma(sp, st_[:], o["sinq"][:, q0:q0 + 512], st_.sem, W=[st_.d])
                pg = genbank()
                for c in range(4):
                    fw.op(pe, lambda e: e.matmul(pg[:], lhsT=wuq[:, c, h * 256:h * 256 + 128], rhs=cqt[:, c, :],
                                                 start=(c == 0), stop=(c == 3)), R=[wuq.d, cqt.d], W=[pg.d])
                fw.op(act, lambda e: e.copy(out=qnt[:], in_=pg[:]), R=[pg.d], W=[qnt.d])

            def t1():
                pg = genbank()
                for c in range(4):
                    fw.op(pe, lambda e: e.matmul(pg[0:64, :], lhsT=wuq[:, c, h * 256 + 128:h * 256 + 192], rhs=cqt[:, c, :],
                                                 start=(c == 0), stop=(c == 3)), R=[wuq.d, cqt.d], W=[pg.d])
                fw.op(dve, lambda e: e.tensor_tensor(out=r1_[:], in0=pg[0:64, :], in1=ct[:], op=ALU.mult), R=[pg.d, ct.d], W=[r1_.d])

            def t2():
                pg = genbank()
                for c in range(4):
                    fw.op(pe, lambda e: e.matmul(pg[0:64, :], lhsT=wuq[:, c, h * 256 + 192:h * 256 + 256], rhs=cqt[:, c, :],
                                                 start=(c == 0), stop=(c == 3)), R=[wuq.d, cqt.d], W=[pg.d])
                fw.op(dve, lambda e: e.tensor_tensor(out=r2_[:], in0=pg[0:64, :], in1=st_[:], op=ALU.mult), R=[pg.d, st_.d], W=[r2_.d])
                fw.op(dve, lambda e: e.tensor_tensor(out=qrt[0:64, :], in0=r1_[:], in1=r2_[:], op=ALU.add), R=[r1_.d, r2_.d], W=[qrt.d])
            return [t0, t1, t2], (qnt, qrt)

        def attend_slot(h, j, bufs, nextq, bg):
            K_, V_ = Kh[hpos[h] % 2], Vh[hpos[h] % 2]
            qnt, qrt = bufs
            mk = masks[j % 2]
            nkb = 8 * (j + 1)
            q0 = j * 512
            po = PSO[ctr["od"] % 2]
            pd = PSDS[ctr["od"] % 2]
            das = (dacc[2 * (ctr["od"] % 2)], dacc[2 * (ctr["od"] % 2) + 1])
            da = das[0]
            ctr["od"] += 1

            def qk(kb):
                ps = PS[kb % 3]
                fw.op(pe, lambda e: e.matmul(ps[:], lhsT=K_[:, kb * 128:(kb + 1) * 128], rhs=qnt[:], start=True, stop=False),
                      R=[K_.d, qnt.d], W=[ps.d])
                fw.op(pe, lambda e: e.matmul(ps[:], lhsT=krope[:, kb * 128:(kb + 1) * 128], rhs=qrt[:], start=False, stop=True),
                      R=[krope.d, qrt.d], W=[ps.d])

            def pv(kb):
                ps = PS[kb % 3]
                pt = pts[ctr["pt"] % NPT]
                ctr["pt"] += 1
                fw.op(act, lambda e: e.activation(out=pt[:], in_=ps[:], func=AF.Exp, scale=MLA_SCALE), R=[ps.d], W=[pt.d])
                w = kb - (nkb - 8)
                if w >= 0:
                    fw.op(dve, lambda e: e.tensor_tensor(out=pt[:], in0=pt[:], in1=mk[:, w, :], op=ALU.mult), R=[mk.d], W=[pt.d])
                fw.op(pe, lambda e: e.matmul(po[:], lhsT=V_[:, kb, :], rhs=pt[:], start=(kb == 0), stop=(kb == nkb - 1)),
                      R=[V_.d, pt.d], W=[po.d])
                if kb % 2 == 0:
                    dk = das[(kb // 2) % 2]
                    if kb < 4:
                        fw.op(dve, lambda e: e.tensor_copy(out=dk[:], in_=pt[:]), R=[pt.d], W=[dk.d])
                    else:
                        fw.op(dve, lambda e: e.tensor_tensor(out=dk[:], in0=dk[:], in1=pt[:], op=ALU.add), R=[pt.d], W=[dk.d])
                else:
                    fw.op(pe, lambda e: e.matmul(pd[:], lhsT=ones[:], rhs=pt[:], start=(kb == 1), stop=False), R=[ones.d, pt.d], W=[pd.d])

            qk(0)
            if nkb > 1:
                qk(1)
            for kb in range(nkb):
                if kb + 2 < nkb:
                    qk(kb + 2)
                pv(kb)
                if nextq and kb % 2 == 1:
                    nextq.pop(0)()
                elif bg and kb % 4 == 3:
                    bg.pop(0)()
            while nextq:
                nextq.pop(0)()
            fw.op(dve, lambda e: e.tensor_tensor(out=da[:], in0=da[:], in1=das[1][:], op=ALU.add), R=[das[1].d], W=[da.d])
            fw.op(dve, lambda e: e.tensor_copy(out=dhi[:], in_=da[:]), R=[da.d], W=[dhi.d])
            fw.op(dve, lambda e: e.tensor_tensor(out=dlo[:], in0=da[:], in1=dhi[:], op=ALU.subtract), R=[da.d, dhi.d], W=[dlo.d])
            fw.op(pe, lambda e: e.matmul(pd[:], lhsT=ones[:], rhs=dhi[:], start=False, stop=False), R=[ones.d, dhi.d], W=[pd.d])
            fw.op(pe, lambda e: e.matmul(pd[:], lhsT=ones[:], rhs=dlo[:], start=False, stop=True), R=[ones.d, dlo.d], W=[pd.d])
            fw.op(dve, lambda e: e.reciprocal(out=rec[:], in_=pd[:]), R=[pd.d], W=[rec.d])
            og = ost[ctr["ost"] % 2]
            ctr["ost"] += 1
            fw.op(dve, lambda e: e.tensor_tensor(out=og[:], in0=po[:], in1=rec[:], op=ALU.mult), R=[po.d, rec.d], W=[og.d])
            fw.dma(pool, o["attnT"][h * 128:(h + 1) * 128, q0:q0 + 512], og[:], og.sem, R=[og.d], W=[drd["attnT"]])

        hs = list(heads)
        hpos.update({h: i for i, h in enumerate(hs)})
        for t in kvgen_tasks(hs[0]):
            t()
        tasks, bufs = qgen_tasks(hs[0], 0)
        for t in tasks:
            t()
        for i, h in enumerate(hs):
            bg = kvgen_tasks(hs[i + 1]) if i + 1 < len(hs) else []
            for j in range(nslot):
                if j + 1 < nslot:
                    nextq, nbufs = qgen_tasks(h, j + 1)
                elif i + 1 < len(hs):
                    nextq, nbufs = qgen_tasks(hs[i + 1], 0)
                else:
                    nextq, nbufs = [], None
                if j == nslot - 1:
                    pass
                attend_slot(h, j, bufs, nextq, bg)
                bufs = nbufs
            while bg:
                bg.pop(0)()
        fw.es = old_es
        fw.end_phase()


def convert_weight_chunks(fw, o, drd):
    sem = fw.dsem("wconv", sw=True)
    fw.phase_dsems.remove(sem)
    chunks = []
    drd["wconv"] = []
    for src_nm, dst_nm, rows in [("w_o", "wo_b", 2048), ("w_gate", "wg_b", 2048), ("w_up", "wu_b", 2048), ("w_down", "wd_b", DFF)]:
        for r in range(0, rows, 512):
            n = min(512, rows - r)

            def issue(src_nm=src_nm, dst_nm=dst_nm, r=r, n=n):
                dep = Dep()
                drd["wconv"].append(dep)
                pairs = [(o[dst_nm][rr:rr + 128, :], o[src_nm][rr:rr + 128, :]) for rr in range(r, r + n, 128)]
                fw.dmas(fw.pool, pairs, sem, W=[dep])
            chunks.append(issue)
    return chunks


def phase_ffn(fw, PS, C, o, drd, nslot=NSLOT):
    pe, act, dve, pool, sp = fw.pe, fw.act, fw.dve, fw.pool, fw.sp
    ident = C["ident"]
    NF = DFF // 128
    es2 = ExitStack()
    with es2:
        old_es = fw.es
        fw.es = es2
        gt_ffn = T(fw.sb("gt_ffn", [128, D], F32), sem=fw.dsem("gt_ffn"))
        fw.dma(sp, gt_ffn[:], bcast_rows(o["ffn_g"], 128, D), gt_ffn.sem, W=[gt_ffn.d])
        gt_fin = T(fw.sb("gt_fin", [128, D], F32), sem=fw.dsem("gt_fin"))
        fw.dma(sp, gt_fin[:], bcast_rows(o["fin_g"], 128, D), gt_fin.sem, W=[gt_fin.d])
        x1 = [T(fw.sb(f"x1_{s}", [128, D], F32), sem=fw.dsem(f"x1_{s}", sw=True)) for s in range(4)]
        ah = T(fw.sb("ah", [128, 16, 512], BF16), sem=fw.dsem("ah"))
        actT = T(fw.sb("actT", [128, NF, 512], BF16))
        NW = 4
        wbuf = [T(fw.sb(f"wb{i}", [128, 8192], BF16), sem=fw.dsem(f"wb{i}")) for i in range(NW)]
        hb = [T(fw.sb(f"fhb{i}", [128, D], BF16)) for i in range(2)]
        junk = T(fw.sb("fjunk", [128, D], BF16))
        ssb = [T(fw.sb(f"fss{i}", [128, 2], F32)) for i in range(2)]
        ostg = [T(fw.sb(f"fo{i}", [128, D // 2], F32), sem=fw.dsem(f"fo{i}", sw=True)) for i in range(2)]
        sg = [T(fw.sb(f"sg{i}", [128, 512], F32)) for i in range(2)]
        ctr = dict(w=0, ss=0, hb=0, sg=0, o=0, ps=0)

        def wload(src_view, n_mid):
            wt = wbuf[ctr["w"] % NW]
            ctr["w"] += 1
            v = wt.ap[:, 0:n_mid * 512].rearrange("p (k c) -> p k c", c=512)
            fw.dma(sp, v, src_view, wt.sem, R=drd["wconv"], W=[wt.d])
            return wt, v

        def rms_rstd(xap, xdep):
            ss = ssb[ctr["ss"] % 2]
            ctr["ss"] += 1
            fw.op(act, lambda e: e.memzero(ss[:, 0:1]), W=[ss.d])
            fw.op(act, lambda e: e.activation(out=junk[:], in_=xap, func=AF.Square, accum_out=ss[:, 0:1]), R=[xdep], W=[junk.d, ss.d])
            fw.op(act, lambda e: e.activation(out=ss[:, 1:2], in_=ss[:, 0:1], func=AF.Sqrt, scale=1.0 / D, bias=C["eps"][:, 0:1]),
                  R=[C["eps"].d], W=[ss.d])
            fw.op(dve, lambda e: e.reciprocal(out=ss[:, 1:2], in_=ss[:, 1:2]), W=[ss.d])
            return ss

        for j in range(nslot):
            q0 = j * 512
            if j == 0:
                fw.dma(sp, ah[:], o["attnT"][:, q0:q0 + 512].rearrange("(k p) t -> p k t", p=128), ah.sem, R=[drd["attnT"]], W=[ah.d])
            if j == 0:
                for s in range(4):
                    fw.dma(pool, x1[s][:], o["xq"][q0 + s * 128:q0 + (s + 1) * 128, :], x1[s].sem, W=[x1[s].d])
            for cb in range(4):
                wt, wv = wload(o["wo_b"][:, cb * 512:(cb + 1) * 512].rearrange("(k p) c -> p k c", p=128), 16)
                for s in range(4):
                    p = PS[ctr["ps"] % 8]
                    ctr["ps"] += 1
                    for kc in range(16):
                        fw.op(pe, lambda e: e.matmul(p[:], lhsT=ah[:, kc, s * 128:(s + 1) * 128], rhs=wv[:, kc, :],
                                                     start=(kc == 0), stop=(kc == 15)), R=[ah.d, wt.d], W=[p.d])
                    fw.op(dve, lambda e: e.tensor_tensor(out=x1[s][:, cb * 512:(cb + 1) * 512], in0=p[:],
                                                          in1=x1[s][:, cb * 512:(cb + 1) * 512], op=ALU.add), R=[p.d], W=[x1[s].d])
            for s in range(4):
                ss = rms_rstd(x1[s][:], x1[s].d)
                hbb = hb[ctr["hb"] % 2]
                ctr["hb"] += 1
                fw.op(dve, lambda e: e.scalar_tensor_tensor(out=hbb[:], in0=x1[s][:], scalar=ss[:, 1:2], in1=gt_ffn[:],
                                                             op0=ALU.mult, op1=ALU.mult), R=[x1[s].d, ss.d, gt_ffn.d], W=[hbb.d])
                for half in range(2):
                    pt = PS[ctr["ps"] % 8]
                    ctr["ps"] += 1
                    ptb = pt.ap.bitcast(BF16)
                    for jj in range(8):
                        kc = half * 8 + jj
                        fw.op(pe, lambda e: e.transpose(out=ptb[:, jj * 128:(jj + 1) * 128], in_=hbb[:, kc * 128:(kc + 1) * 128],
                                                        identity=ident[:]), R=[hbb.d, ident.d], W=[pt.d])
                    dst = ah[:, half * 8:(half + 1) * 8, s * 128:(s + 1) * 128]
                    srcv = ptb.rearrange("p (j t) -> p j t", j=8)
                    if half == 0:
                        fw.op(act, lambda e: e.copy(out=dst, in_=srcv), R=[pt.d], W=[ah.d])
                    else:
                        fw.op(dve, lambda e: e.tensor_copy(out=dst, in_=srcv), R=[pt.d], W=[ah.d])
            for fg in range(NF // 4):
                wgt, wgv = wload(o["wg_b"][:, fg * 512:(fg + 1) * 512].rearrange("(k p) c -> p k c", p=128), 16)
                wut, wuv = wload(o["wu_b"][:, fg * 512:(fg + 1) * 512].rearrange("(k p) c -> p k c", p=128), 16)
                for fi in range(4):
                    f = fg * 4 + fi
                    pg = PS[ctr["ps"] % 8]
                    pu = PS[(ctr["ps"] + 1) % 8]
                    ctr["ps"] += 2
                    for kc in range(16):
                        fw.op(pe, lambda e: e.matmul(pg[:], lhsT=wgv[:, kc, fi * 128:(fi + 1) * 128], rhs=ah[:, kc, :],
                                                     start=(kc == 0), stop=(kc == 15)), R=[wgt.d, ah.d], W=[pg.d])
                    for kc in range(16):
                        fw.op(pe, lambda e: e.matmul(pu[:], lhsT=wuv[:, kc, fi * 128:(fi + 1) * 128], rhs=ah[:, kc, :],
                                                     start=(kc == 0), stop=(kc == 15)), R=[wut.d, ah.d], W=[pu.d])
                    sgt = sg[ctr["sg"] % 2]
                    ctr["sg"] += 1
                    fw.op(act, lambda e: e.activation(out=sgt[:], in_=pg[:], func=AF.Silu), R=[pg.d], W=[sgt.d])
                    fw.op(dve, lambda e: e.tensor_tensor(out=actT[:, f, :], in0=pu[:], in1=sgt[:], op=ALU.mult),
                          R=[pu.d, sgt.d], W=[actT.d])
            for cb in range(4):
                banks = [PS[(cb % 2) * 4 + s] for s in range(4)]
                for f0 in range(0, NF, 16):
                    nf = min(16, NF - f0)
                    wt, wv = wload(o["wd_b"][f0 * 128:(f0 + nf) * 128, cb * 512:(cb + 1) * 512].rearrange("(f p) c -> p f c", p=128), nf)
                    if cb == 0 and f0 == 16 and j + 1 < nslot:
                        fw.dma(sp, ah[:], o["attnT"][:, q0 + 512:q0 + 1024].rearrange("(k p) t -> p k t", p=128), ah.sem, R=[drd["attnT"]], W=[ah.d])
                    for fi in range(nf):
                        f = f0 + fi
                        for s in range(4):
                            p = banks[s]
                            fw.op(pe, lambda e: e.matmul(p[:], lhsT=actT[:, f, s * 128:(s + 1) * 128], rhs=wv[:, fi, :],
                                                         start=(f == 0), stop=(f == NF - 1)), R=[actT.d, wt.d], W=[p.d])
                for s in range(4):
                    p = banks[s]
                    fw.op(dve, lambda e: e.tensor_tensor(out=x1[s][:, cb * 512:(cb + 1) * 512], in0=p[:],
                                                          in1=x1[s][:, cb * 512:(cb + 1) * 512], op=ALU.add), R=[p.d], W=[x1[s].d])
            for s in range(4):
                ss = rms_rstd(x1[s][:], x1[s].d)
                for hf in range(2):
                    og = ostg[ctr["o"] % 2]
                    ctr["o"] += 1
                    cs = slice(hf * 1024, (hf + 1) * 1024)
                    fw.op(dve, lambda e: e.scalar_tensor_tensor(out=og[:], in0=x1[s][:, cs], scalar=ss[:, 1:2], in1=gt_fin[:, cs],
                                                                 op0=ALU.mult, op1=ALU.mult), R=[x1[s].d, ss.d, gt_fin.d], W=[og.d])
                    fw.dma(pool, o["out"][q0 + s * 128:q0 + (s + 1) * 128, cs], og[:], og.sem, R=[og.d], W=[drd["out"]])
                if j + 1 < nslot:
                    fw.dma(pool, x1[s][:], o["xq"][q0 + 512 + s * 128:q0 + 512 + (s + 1) * 128, :], x1[s].sem, W=[x1[s].d])
        fw.es = old_es
        fw.end_phase()


def phase_cmp(fw, PS, C, o, drd, G):
    pe, act, dve, pool, sp = fw.pe, fw.act, fw.dve, fw.pool, fw.sp
    NCMP = SEQ // 16 - 1
    NBC = (NCMP + 127) // 128
    es2 = ExitStack()
    with es2:
        old_es = fw.es
        fw.es = es2
        for X, (srcT, w1n, w2n, posn) in enumerate([("kcT", "w1k", "w2k", "posk"), ("vcT", "w1v", "w2v", "posv")]):
            xt = T(fw.sb(f"cx{X}", [128, SEQ], BF16), sem=fw.dsem(f"cx{X}"))
            fw.dma(sp, xt[:], o[srcT], xt.sem, R=[drd[srcT]], W=[xt.d])
            w1 = T(fw.sb(f"cw1{X}", [128, 32, 128], BF16), sem=fw.dsem(f"cw1{X}", sw=True))
            fw.dmas(pool, [(w1[0:64], o[w1n]), (w1[64:128], o[w1n])], w1.sem, W=[w1.d])
            w2 = T(fw.sb(f"cw2{X}", [128, 64], BF16), sem=fw.dsem(f"cw2{X}", sw=True))
            fw.dma(pool, w2[:], o[w2n], w2.sem, W=[w2.d])
            pos = T(fw.sb(f"cpos{X}", [64, 32], BF16), sem=fw.dsem(f"cpos{X}", sw=True))
            fw.dma(pool, pos[:], o[posn], pos.sem, W=[pos.d])
            bias = T(fw.sb(f"cbias{X}", [128, 1], F32))
            pb = PS[7]
            for l in range(32):
                fw.op(pe, lambda e: e.matmul(pb[:, 0:1], lhsT=w1[0:64, l, :], rhs=pos[:, l:l + 1], start=(l == 0), stop=(l == 31)),
                      R=[w1.d, pos.d], W=[pb.d])
            fw.op(dve, lambda e: e.tensor_copy(out=bias[:], in_=pb[:, 0:1]), R=[pb.d], W=[bias.d])
            for g in range(2):
                ph = PS[g]
                for l in range(32):
                    rhs = xt[g * 64:(g + 1) * 64, l:l + 16 * (NCMP - 1) + 1:16]
                    fw.op(pe, lambda e: e.matmul(ph[:, 0:NCMP], lhsT=w1[g * 64:(g + 1) * 64, l, :], rhs=rhs, start=(l == 0), stop=(l == 31)),
                          R=[w1.d, xt.d], W=[ph.d])
                hs = T(fw.sb(f"chs{X}{g}", [128, 128 * NBC], BF16))
                fw.op(dve, lambda e: e.memset(hs[:], 0.0), W=[hs.d])
                fw.op(act, lambda e: e.activation(out=hs[:, 0:NCMP], in_=ph[:, 0:NCMP], func=AF.Silu, bias=bias[:, 0:1]),
                      R=[ph.d, bias.d], W=[hs.d])
                if X == 0:
                    kc = G["kcmpT"][g]
                    p2 = PS[2 + g]
                    fw.op(pe, lambda e: e.matmul(p2[0:64, 0:128 * NBC], lhsT=w2[:, :], rhs=hs[:, :], start=True, stop=True),
                          R=[w2.d, hs.d], W=[p2.d])
                    fw.op(dve, lambda e: e.tensor_copy(out=kc[0:64, 0:128 * NBC], in_=p2[0:64, 0:128 * NBC]), R=[p2.d], W=[kc.d])
                else:
                    vc = G["vcmp"][g]
                    p2 = PS[4 + g]
                    for nb in range(NBC):
                        fw.op(pe, lambda e: e.matmul(p2[:, nb * 64:(nb + 1) * 64], lhsT=hs[:, nb * 128:(nb + 1) * 128], rhs=w2[:, :],
                                                     start=True, stop=True), R=[w2.d, hs.d], W=[p2.d])
                    fw.op(dve, lambda e: e.tensor_copy(out=vc[:, 0:NBC, 0:64], in_=p2[:, 0:NBC * 64].rearrange("p (n d) -> p n d", d=64)),
                          R=[p2.d], W=[vc.d])
        fw.es = old_es
        fw.end_phase()


def phase_nsa(fw, PS, C, o, drd, G, nslot=NSLOT, groups=(0, 1), heads=range(8), bg=()):
    pe, act, dve, pool, sp = fw.pe, fw.act, fw.dve, fw.pool, fw.sp
    ones, ident = C["ones"], C["ident"]
    nkb_all = SEQ // 128
    es2 = ExitStack()
    with es2:
        old_es = fw.es
        fw.es = es2

        def cload(name, shape, dt, src, q=None, sw=False):
            t = T(fw.sb(name, shape, dt), sem=fw.dsem(name, sw=sw))
            fw.dma(pool if sw else sp, t[:], src, t.sem, W=[t.d])
            return t
        ovl = cload("ovl", [128, 4, 128], BF16, o["ovl"])
        inds = cload("inds4", [128, 64, 128], BF16, o["inds4"])
        oh = cload("oh", [48, 48, 64], BF16, o["oh"])
        bias_s = cload("bias_s", [128, 16 * 64], F32, o["bias_s"])
        bias_w = cload("bias_w", [128, 16 * 12], F32, o["bias_w"])
        bias_c = cload("bias_c", [128, 16 * 8 * 4], F32, o["bias_c"])
        vtab = [cload(f"vtab{i}", [128, 264], F32, o[f"vtab{'AB'[i]}"]) for i in range(2)]
        atab = [cload(f"atab{i}", [128, 264], F32, o[f"atab{'AB'[i]}"]) for i in range(2)]
        cmpneg = [cload(f"cmpneg{i}", [128, 2, 512], BF16, o[f"cmpneg{'AB'[i]}"]) for i in range(2)]
        kw = T(fw.sb("kw_sb", [128, 1536], BF16), sem=fw.dsem("kw_sb"))
        fw.op(pool, lambda e: e.memset(kw[64:128, :], 0.0), W=[kw.d])
        Vw = T(fw.sb("Vw_sb", [128, 12, 128], BF16), sem=fw.dsem("Vw_sb"))
        fw.op(pool, lambda e: e.memset(Vw[:, :, 64:128], 1.0), W=[Vw.d])
        qa = T(fw.sb("qa", [128, 8, 512], BF16), sem=fw.dsem("qa"))
        fw.op(pool, lambda e: e.memset(qa[:], 0.0), W=[qa.d])
        cneg = T(fw.sb("cneg", [128, 8, 512], BF16), sem=fw.dsem("cneg"))
        wneg = T(fw.sb("wneg", [128, 12, 512], BF16), sem=fw.dsem("wneg"))
        NPT = 8
        pts = [T(fw.sb(f"npt{i}", [128, 512], BF16)) for i in range(NPT)]
        reccs = [T(fw.sb(f"recc{i}", [128, 512], F32)) for i in range(2)]
        oc_sb = T(fw.sb("oc_sb", [64, 8, 512], F32))
        imp_sb = T(fw.sb("imp_sb", [128, 128], F32))
        work = T(fw.sb("selwork", [128, 128], F32))
        m8 = T(fw.sb("m8", [128, 16], F32))
        sel = T(fw.sb("sel", [128, 128], BF16))
        selT = T(fw.sb("selT", [128, 512], BF16))
        gh = T(fw.sb("gh", [48, 512], BF16), sem=fw.dsem("gh"))
        gl = T(fw.sb("gl", [48, 512], BF16), sem=fw.dsem("gl"))
        gb = [T(fw.sb(f"gb{i}", [64, 512], F32)) for i in range(3)]
        rs = T(fw.sb("nrs", [64, 512], F32))
        tt = T(fw.sb("ntt", [64, 512], F32))
        acc = T(fw.sb("nacc", [64, 512], F32))
        ost = [T(fw.sb(f"nost{i}", [64, 512], BF16), sem=fw.dsem(f"nost{i}", sw=True)) for i in range(2)]
        ctr = dict(pt=0, ss=0, ost=0, acc=0, s=0)
        PS_S = [PS[0], PS[1], PS[2]]
        PS_MISC = PS[7]

        def next_s():
            p = PS_S[ctr["s"] % 3]
            ctr["s"] += 1
            return p

        def next_pt():
            p = pts[ctr["pt"] % NPT]
            ctr["pt"] += 1
            return p

        for g in groups:
            ksT, Vs = G["ksT"][g], G["Vs"][g]
            kcm, vcm = G["kcmpT"][g], G["vcmp"][g]
            for j in range(nslot):
                q0 = j * 512
                par = j % 2
                nkb = 8 * (j + 1)
                nbc = (j + 2) // 2
                w0 = 4 if j == 0 else 0
                fw.dma(sp, qa[0:64, :, :], o["qnT"][g * 512:(g + 1) * 512, q0:q0 + 512].rearrange("(h d) t -> d h t", d=64), qa.sem,
                       R=[drd["qnT"]], W=[qa.d])
                fw.dma(sp, qa[64:70, :, :], o["qaug" + "AB"[par]][:, g * 8:(g + 1) * 8, :], qa.sem, W=[qa.d])
                fw.dma(sp, cneg[:], o["cneg" + "AB"[par]], cneg.sem, W=[cneg.d])
                fw.dma(sp, wneg[:], o["wneg" + "AB"[par]], wneg.sem, W=[wneg.d])
                kb0 = 8 * j - 4 + w0
                nw = 12 - w0
                fw.dma(sp, kw[0:64, w0 * 128:1536], o["kwT"][g * 64:(g + 1) * 64, kb0 * 128:(kb0 + nw) * 128], kw.sem,
                       R=[drd["kwT"]], W=[kw.d])
                fw.dma(sp, kw[64:70, :], o["kaug"][:, 0:1536], kw.sem, W=[kw.d])
                fw.dma(sp, Vw[:, w0:12, 0:64],
                       o["vsw"][kb0 * 128:(kb0 + nw) * 128, 128 + g * 64:128 + (g + 1) * 64].rearrange("(k p) d -> p k d", p=128),
                       Vw.sem, R=[drd["vsw"]], W=[Vw.d])
                fw.dma(sp, gh[:], o["gTh"][:, q0:q0 + 512], gh.sem, R=[drd["gTh"]], W=[gh.d])
                fw.dma(sp, gl[:], o["gTl"][:, q0:q0 + 512], gl.sem, R=[drd["gTl"]], W=[gl.d])
                for _ in range(3):
                    if bg:
                        bg.pop(0)()
                pimp = PS[5]
                fw.op(dve, lambda e: e.memset(pimp[:], 0.0), W=[pimp.d])
                for hi_, h in enumerate(heads):
                    h16 = g * 8 + h
                    pD, pO = (PS[3], PS[4]) if hi_ % 2 == 0 else (PS[6], PS[7])
                    recc = reccs[hi_ % 2]
                    pcs = []
                    for nb in range(nbc):
                        ps = next_s()
                        wl = nb - (nbc - 2)
                        fw.op(pe, lambda e: e.matmul(ps[:], lhsT=kcm[:, nb * 128:(nb + 1) * 128], rhs=qa[:, h, :], start=True, stop=(wl < 0)),
                              R=[kcm.d, qa.d], W=[ps.d])
                        if wl >= 0:
                            fw.op(pe, lambda e: e.matmul(ps[:], lhsT=ident[:], rhs=cmpneg[par][:, wl, :], start=False, stop=True),
                                  R=[ident.d, cmpneg[par].d], W=[ps.d])
                        bcol = (h16 * 8 + j) * 4 + nb
                        pt = next_pt()
                        fw.op(act, lambda e: e.activation(out=pt[:], in_=ps[:], func=AF.Exp, scale=0.125, bias=bias_c[:, bcol:bcol + 1]),
                              R=[ps.d, bias_c.d], W=[pt.d])
                        pcs.append(pt)
                    for nb in range(nbc):
                        fw.op(pe, lambda e: e.matmul(pD[:], lhsT=ones[:], rhs=pcs[nb][:], start=(nb == 0), stop=(nb == nbc - 1)),
                              R=[ones.d, pcs[nb].d], W=[pD.d])
                    fw.op(act, lambda e: e.activation(out=recc[:], in_=pD[:], func=AF.Ln, bias=C["tiny"][:, 0:1]), R=[pD.d, C["tiny"].d], W=[recc.d])
                    fw.op(act, lambda e: e.activation(out=recc[:], in_=recc[:], func=AF.Exp, scale=-1.0), W=[recc.d])
                    for nb in range(nbc):
                        fw.op(dve, lambda e: e.tensor_tensor(out=pcs[nb][:], in0=pcs[nb][:], in1=recc[:], op=ALU.mult), R=[recc.d], W=[pcs[nb].d])
                    for nb in range(nbc):
                        fw.op(pe, lambda e: e.matmul(pO[:], lhsT=vcm[:, nb, :], rhs=pcs[nb][:], start=(nb == 0), stop=(nb == nbc - 1)),
                              R=[vcm.d, pcs[nb].d], W=[pO.d])
                    for qb in range(4):
                        for nb in range(nbc):
                            fw.op(pe, lambda e: e.matmul(pimp[:, qb * 128:(qb + 1) * 128], lhsT=pcs[nb][:, qb * 128:(qb + 1) * 128],
                                                         rhs=ovl[:, nb, :], start=False, stop=False, skip_group_check=True),
                                  R=[ovl.d, pcs[nb].d], W=[pimp.d])
                    fw.op(act, lambda e: e.copy(out=oc_sb[:, h, :], in_=pO[0:64, :]), R=[pO.d], W=[oc_sb.d])
                pT = PS_MISC
                pTb = pT.ap.bitcast(BF16)
                for qb in range(4):
                    s0 = 134 - 16 * j - 2 * qb
                    fw.op(dve, lambda e: e.tensor_tensor(out=imp_sb[:], in0=pimp[:, qb * 128:(qb + 1) * 128], in1=vtab[par][:, s0:s0 + 128],
                                                          op=ALU.mult), R=[pimp.d, vtab[par].d], W=[imp_sb.d])
                    fw.op(dve, lambda e: e.tensor_tensor(out=imp_sb[:], in0=imp_sb[:], in1=atab[par][:, s0:s0 + 128], op=ALU.add),
                          R=[atab[par].d], W=[imp_sb.d])
                    fw.op(dve, lambda e: e.memset(imp_sb[:, 0:1], 1e4), W=[imp_sb.d])
                    fw.op(dve, lambda e: e.max(out=m8[:, 0:8], in_=imp_sb[:]), R=[imp_sb.d], W=[m8.d])
                    fw.op(dve, lambda e: e.match_replace(out=work[:], in_to_replace=m8[:, 0:8], in_values=imp_sb[:], imm_value=-1e9),
                          R=[imp_sb.d, m8.d], W=[work.d])
                    fw.op(dve, lambda e: e.max(out=m8[:, 8:16], in_=work[:]), R=[work.d], W=[m8.d])
                    fw.op(dve, lambda e: e.tensor_scalar(out=sel[:], in0=imp_sb[:], scalar1=m8[:, 15:16], scalar2=1.0, op0=ALU.is_ge, op1=ALU.subtract),
                          R=[imp_sb.d, m8.d], W=[sel.d])
                    fw.op(pe, lambda e: e.transpose(out=pTb[:, qb * 128:(qb + 1) * 128], in_=sel[:], identity=ident[:]),
                          R=[sel.d, ident.d], W=[pT.d])
                fw.op(act, lambda e: e.copy(out=selT[:], in_=pTb[:, 0:512]), R=[pT.d], W=[selT.d])
                for h in heads:
                    h16 = g * 8 + h
                    pS_ = PS[3 + 2 * (ctr["acc"] % 2)]
                    pW_ = PS[4 + 2 * (ctr["acc"] % 2)]
                    ctr["acc"] += 1
                    units = [("w", w) for w in range(w0, 12)] + [("s", kb) for kb in range(nkb)]

                    def qk(u):
                        kind, i = u
                        ps = next_s()
                        if kind == "w":
                            fw.op(pe, lambda e: e.matmul(ps[:], lhsT=kw[:, i * 128:(i + 1) * 128], rhs=qa[:, h, :], start=True, stop=False),
                                  R=[kw.d, qa.d], W=[ps.d])
                            fw.op(pe, lambda e: e.matmul(ps[:], lhsT=ident[:], rhs=wneg[:, i, :], start=False, stop=True),
                                  R=[ident.d, wneg.d], W=[ps.d])
                        else:
                            a = i // 32
                            w = i - (nkb - 8)
                            fw.op(pe, lambda e: e.matmul(ps[:], lhsT=ksT[:, i * 128:(i + 1) * 128], rhs=qa[:, h, :], start=True, stop=False),
                                  R=[ksT.d, qa.d], W=[ps.d])
                            fw.op(pe, lambda e: e.matmul(ps[:], lhsT=inds[:, i, :], rhs=selT[:, :],
                                                         start=False, stop=(w < 0)), R=[inds.d, selT.d], W=[ps.d])
                            if w >= 0:
                                fw.op(pe, lambda e: e.matmul(ps[:], lhsT=ident[:], rhs=cneg[:, w, :], start=False, stop=True),
                                      R=[ident.d, cneg.d], W=[ps.d])
                        return ps

                    def pv(u, ps, first, last):
                        kind, i = u
                        pt = next_pt()
                        if kind == "w":
                            bcol = h16 * 12 + i
                            fw.op(act, lambda e: e.activation(out=pt[:], in_=ps[:], func=AF.Exp, scale=0.125, bias=bias_w[:, bcol:bcol + 1]),
                                  R=[ps.d, bias_w.d], W=[pt.d])
                            fw.op(pe, lambda e: e.matmul(pW_[:], lhsT=Vw[:, i, :], rhs=pt[:], start=first, stop=last), R=[Vw.d, pt.d], W=[pW_.d])
                        else:
                            bcol = h16 * 64 + (8 * j - i + 7)
                            fw.op(act, lambda e: e.activation(out=pt[:], in_=ps[:], func=AF.Exp, scale=0.125, bias=bias_s[:, bcol:bcol + 1]),
                                  R=[ps.d, bias_s.d], W=[pt.d])
                            fw.op(pe, lambda e: e.matmul(pS_[:], lhsT=Vs[:, i, :], rhs=pt[:], start=first, stop=last), R=[Vs.d, pt.d], W=[pS_.d])

                    nwin = 12 - w0
                    pend = []
                    pend.append(qk(units[0]))
                    if len(units) > 1:
                        pend.append(qk(units[1]))
                    for ui, u in enumerate(units):
                        if ui + 2 < len(units):
                            pend.append(qk(units[ui + 2]))
                        ps = pend.pop(0)
                        if u[0] == "w":
                            pv(u, ps, ui == 0, ui == nwin - 1)
                        else:
                            pv(u, ps, ui == nwin, ui == len(units) - 1)
                    for b in range(3):
                        r = h16 * 3 + b
                        pg = PS_MISC
                        fw.op(pe, lambda e: e.matmul(pg[0:64, :], lhsT=oh[:, r, :], rhs=gh[:], start=True, stop=False), R=[oh.d, gh.d], W=[pg.d])
                        fw.op(pe, lambda e: e.matmul(pg[0:64, :], lhsT=oh[:, r, :], rhs=gl[:], start=False, stop=True), R=[oh.d, gl.d], W=[pg.d])
                        fw.op(act, lambda e: e.copy(out=gb[b][:], in_=pg[0:64, :]), R=[pg.d], W=[gb[b].d])
                    fw.op(dve, lambda e: e.tensor_tensor(out=acc[:], in0=oc_sb[:, h, :], in1=gb[0][:], op=ALU.mult), R=[oc_sb.d, gb[0].d], W=[acc.d])
                    for b, pacc in ((1, pS_), (2, pW_)):
                        fw.op(dve, lambda e: e.reciprocal(out=rs[:], in_=pacc[64:128, :]), R=[pacc.d], W=[rs.d])
                        fw.op(dve, lambda e: e.tensor_tensor(out=rs[:], in0=rs[:], in1=gb[b][:], op=ALU.mult), R=[gb[b].d], W=[rs.d])
                        fw.op(dve, lambda e: e.tensor_tensor(out=tt[:], in0=pacc[0:64, :], in1=rs[:], op=ALU.mult), R=[pacc.d, rs.d], W=[tt.d])
                        if b == 1:
                            fw.op(dve, lambda e: e.tensor_tensor(out=acc[:], in0=acc[:], in1=tt[:], op=ALU.add), R=[tt.d], W=[acc.d])
                        else:
                            og = ost[ctr["ost"] % 2]
                            ctr["ost"] += 1
                            fw.op(dve, lambda e: e.tensor_tensor(out=og[:], in0=acc[:], in1=tt[:], op=ALU.add), R=[acc.d, tt.d], W=[og.d])
                            fw.dma(pool, o["attnT"][1024 + h16 * 64:1024 + (h16 + 1) * 64, q0:q0 + 512], og[:], og.sem, R=[og.d], W=[drd["attnT"]])
        fw.es = old_es
        fw.end_phase()


def build(dbg=(), phases=("f", "k", "q", "c", "n", "m")):
    nc = bass.Bass("TRN2", target_bir_lowering=False)
    es = ExitStack()
    IN = {}

    def din(name, shape, dt=F32):
        IN[name] = nc.dram_tensor(name, list(shape), dt, kind="ExternalInput").ap()
        return IN[name]

    drd = {}

    def dscr(name, shape, dt):
        kind = "ExternalOutput" if name in dbg else "Internal"
        drd[name] = Dep()
        return nc.dram_tensor(name, list(shape), dt, kind=kind).ap()

    xs = din("xs", [SEQ, D])
    xq = din("xq", [NQ, D])
    attn_g = din("attn_g", [1, D])
    wk = din("wk", [D, 1152])
    wq = din("wq", [D, 1584])
    gkv = din("gkv", [128, 2])
    gq = din("gq", [128, 4])
    cosk = din("cosk", [64, SEQ])
    sink = din("sink", [64, SEQ])
    identd = din("identd", [128, 128], BF16)
    o_extra = dict(w_uq=din("w_uq", [512, 2048]), w_uk=din("w_uk", [256, 1024]), w_uv=din("w_uv", [256, 1024]),
                   cosq=din("cosq", [64, NQ]), sinq=din("sinq", [64, NQ]),
                   cmaskA=din("cmaskA", [128, 8, 512], BF16), cmaskB=din("cmaskB", [128, 8, 512], BF16))
    out = nc.dram_tensor("out", [NQ, D], F32, kind="ExternalOutput").ap()
    drd["out"] = Dep()

    o = dict(gkv=gkv, gq=gq, cosk=cosk, sink=sink, xq=xq, out=out)
    o.update(o_extra)
    o.update(dict(w_o=din("w_o", [2048, 2048]), w_gate=din("w_gate", [2048, DFF]), w_up=din("w_up", [2048, DFF]),
                  w_down=din("w_down", [DFF, 2048]), ffn_g=din("ffn_g", [1, D]), fin_g=din("fin_g", [1, D])))
    for nm, shp, dt in [("w1k", [64, 32, 128], F32), ("w1v", [64, 32, 128], F32), ("w2k", [128, 64], F32), ("w2v", [128, 64], F32),
                        ("posk", [64, 32], F32), ("posv", [64, 32], F32), ("kaug", [6, max(SEQ, 1536)], BF16), ("caug", [6, 512], BF16),
                        ("qaugA", [6, 16, 512], BF16), ("qaugB", [6, 16, 512], BF16), ("ovl", [128, 4, 128], BF16),
                        ("inds4", [128, 64, 128], BF16), ("oh", [48, 48, 64], BF16), ("bias_s", [128, 16 * 64], F32),
                        ("bias_w", [128, 16 * 12], F32), ("bias_c", [128, 16 * 8 * 4], F32),
                        ("cnegA", [128, 8, 512], BF16), ("cnegB", [128, 8, 512], BF16),
                        ("wnegA", [128, 12, 512], BF16), ("wnegB", [128, 12, 512], BF16),
                        ("cmpnegA", [128, 2, 512], BF16), ("cmpnegB", [128, 2, 512], BF16),
                        ("vtabA", [128, 264], F32), ("vtabB", [128, 264], F32), ("atabA", [128, 264], F32), ("atabB", [128, 264], F32)]:
        o[nm] = din(nm, shp, dt)
    for nm, shp in [("wo_b", [2048, 2048]), ("wg_b", [2048, DFF]), ("wu_b", [2048, DFF]), ("wd_b", [DFF, 2048])]:
        o[nm] = dscr(nm, shp, BF16)
    for nm, shp in [("ckvT", [256, SEQ]), ("kropeT", [64, SEQ]), ("kcT", [128, SEQ]), ("vcT", [128, SEQ]),
                    ("ksT", [128, SEQ]), ("kwT", [128, SEQ]), ("vsw", [SEQ, 256]), ("cqT", [512, NQ]),
                    ("qnT", [1024, NQ]), ("gTh", [48, NQ]), ("gTl", [48, NQ]), ("attnT", [2048, NQ])]:
        o[nm] = dscr(nm, shp, BF16)

    with es:
        fw = FW(nc, es)
        pe, act, dve, pool, sp = fw.pe, fw.act, fw.dve, fw.pool, fw.sp
        PS = [T(fw.ps(f"ps{i}", [128, 512], F32), dep=Dep(excl=True)) for i in range(8)]
        ident = T(fw.sb("ident", [128, 128], BF16), sem=fw.dsem("ident"))
        ones = T(fw.sb("ones", [128, 128], BF16))
        fw.dma(sp, ident[:], identd, ident.sem, W=[ident.d])
        fw.op(pool, lambda e: e.memset(ones[:], 1.0), W=[ones.d])
        epst = T(fw.sb("epst", [128, 1], F32))
        fw.op(pool, lambda e: e.memset(epst[:], EPS), W=[epst.d])
        tinyt = T(fw.sb("tinyt", [128, 1], F32))
        fw.op(pool, lambda e: e.memset(tinyt[:], 1e-30), W=[tinyt.d])
        onet = T(fw.sb("onet", [128, 1], F32))
        fw.op(pool, lambda e: e.memset(onet[:], 1.0), W=[onet.d])
        C = dict(ident=ident, ones=ones, eps=epst, tiny=tinyt, one=onet)

        if "k" in phases:
            phase_proj(fw, PS, C, xs, SEQ, attn_g, wk, 1152, "k", o, drd)
        if "q" in phases:
            phase_proj(fw, PS, C, xq, NQ, attn_g, wq, 1584, "q", o, drd)

        conv_chunks = convert_weight_chunks(fw, o, drd) if "f" in phases else []
        if "c" in phases or "n" in phases:
            G = dict(kcmpT=[T(fw.sb(f"kcmpT{g}", [128, 512], BF16), sem=fw.dsem(f"kcmpT{g}")) for g in range(2)],
                     vcmp=[T(fw.sb(f"vcmp{g}", [128, 4, 128], BF16)) for g in range(2)])
            for g in range(2):
                fw.op(pool, lambda e: e.memset(G["kcmpT"][g][:, :], 0.0), W=[G["kcmpT"][g].d])
                fw.dma(sp, G["kcmpT"][g][64:70, :], o["caug"], G["kcmpT"][g].sem, W=[G["kcmpT"][g].d])
                fw.op(pool, lambda e: e.memset(G["vcmp"][g][:, :, 0:64], 0.0), W=[G["vcmp"][g].d])
                fw.op(pool, lambda e: e.memset(G["vcmp"][g][:, :, 64:128], 1.0), W=[G["vcmp"][g].d])
        es_kv = ExitStack()
        if "n" in phases:
            fw.es = es_kv
            nkb_all = SEQ // 128
            G["ksT"] = [T(fw.sb(f"ksT_sb{g}", [128, SEQ], BF16), sem=fw.dsem(f"ksT_sb{g}")) for g in range(2)]
            G["Vs"] = [T(fw.sb(f"Vs_sb{g}", [128, nkb_all, 128], BF16), sem=fw.dsem(f"Vs_sb{g}")) for g in range(2)]
            fw.es = es
            for g in range(2):
                ksT, Vs = G["ksT"][g], G["Vs"][g]
                fw.phase_dsems.remove(ksT.sem)
                fw.phase_dsems.remove(Vs.sem)
                fw.op(pool, lambda e: e.memset(ksT[64:128, :], 0.0), W=[ksT.d])
                fw.op(pool, lambda e: e.memset(Vs[:, :, 64:128], 1.0), W=[Vs.d])
                fw.dma(sp, ksT[0:64, :], o["ksT"][g * 64:(g + 1) * 64, :], ksT.sem, R=[drd["ksT"]], W=[ksT.d])
                fw.dma(sp, ksT[64:70, :], o["kaug"][:, 0:SEQ], ksT.sem, W=[ksT.d])
                fw.dma(sp, Vs[:, :, 0:64], o["vsw"][:, g * 64:(g + 1) * 64].rearrange("(k p) d -> p k d", p=128), Vs.sem,
                       R=[drd["vsw"]], W=[Vs.d])
        if "c" in phases:
            phase_cmp(fw, PS, C, o, drd, G)
        if "n" in phases:
            phase_nsa(fw, PS, C, o, drd, G, nslot=NSA_NSLOT_RUN, groups=NSA_GROUPS_RUN, heads=NSA_HEADS_RUN, bg=conv_chunks)
            es_kv.close()
        for c_ in conv_chunks:
            c_()
        conv_chunks.clear()
        if "m" in phases:
            phase_mla(fw, PS, C, o, drd, heads=MLA_HEADS_RUN, nslot=MLA_NSLOT_RUN)

        if "f" in phases:
            phase_ffn(fw, PS, C, o, drd, nslot=FFN_NSLOT_RUN)

        alld = []
        for v_ in drd.values():
            alld.extend(v_ if isinstance(v_, list) else [v_])
        fw.wait_all(sp, alld)
        fw.barrier()
    return nc


def own_token_index(half):
    return np.concatenate([np.arange(c * 512, (c + 1) * 512) for c in OWN_CHUNKS[half]])


def split3(x):
    x = x.astype(np.float32)
    a = x.astype(ml_dtypes.bfloat16)
    r = (x - a.astype(np.float32)).astype(np.float32)
    b = r.astype(ml_dtypes.bfloat16)
    c = (r - b.astype(np.float32)).astype(np.float32).astype(ml_dtypes.bfloat16)
    return a, b, c


def nsa_tables(half):
    bf = ml_dtypes.bfloat16
    NEGM = np.float32(-1e9)
    slopes = (2.0 ** (-8.0 * np.arange(1, 17, dtype=np.float32) / 16)).astype(np.float32)
    t = {}
    nka = max(SEQ, 1536)
    rk = (np.arange(nka) % 128).astype(np.float32)
    t["kaug"] = np.stack([np.ones(nka, np.float32)] * 3 + [-rk] * 3).astype(bf)
    rn = (16 * (np.arange(512) % 128)).astype(np.float32)
    t["caug"] = np.stack([np.ones(512, np.float32)] * 3 + [-rn] * 3).astype(bf)
    ch = (-8.0 * slopes).astype(np.float32)
    types = (0, 1) if half == 0 else (1, 0)
    for nm, ty in zip("AB", types):
        rq = (512 * ty + np.arange(512)).astype(np.float32)
        prod = (ch[:, None] * rq[None, :]).astype(np.float32)
        p3 = split3(prod)
        c3 = split3(np.broadcast_to(ch[:, None], (16, 512)).copy())
        t["qaug" + nm] = np.stack(list(p3) + list(c3)).astype(bf)
        kpos = (np.arange(8)[None, :, None] * 128 + np.arange(128)[:, None, None])
        qpos = (512 * ty + np.arange(512))[None, None, :]
        t["cneg" + nm] = np.where(kpos <= qpos, np.float32(0), NEGM).astype(bf)
        kposw = ((np.arange(12)[None, :, None] - 4) * 128 + np.arange(128)[:, None, None])
        dist = qpos - kposw
        t["wneg" + nm] = np.where((dist >= 0) & (dist < 512), np.float32(0), NEGM).astype(bf)
        x = np.arange(264)[None, :]
        relj = x - 8 - 8 * (4 * ty // 4) * 1 - 126 if False else (x - 8 - 8 * ty - 126)
        hb = (np.arange(128)[:, None] >= 64).astype(np.int64)
        valid = relj <= hb
        forced = (relj == hb) | (relj == hb - 1)
        t["vtab" + nm] = valid.astype(np.float32)
        t["atab" + nm] = (np.where(valid, 0.0, -1.0) + np.where(forced, 1e4, 0.0)).astype(np.float32)
    for nm, ty, par in (("A", types[0], 0), ("B", types[1], 1)):
        p = np.arange(128)[:, None, None]
        which = np.arange(2)[None, :, None]
        base = np.where(which == 1, 1024 * par, 1024 * par + 2048)
        qq = (512 * ty + np.arange(512))[None, None, :]
        ok = (16 * p + 31) <= (base + qq)
        t["cmpneg" + nm] = np.where(ok, np.float32(0), NEGM).astype(bf)
    n = np.arange(512)[:, None]
    jj = np.arange(128)[None, :]
    ov = np.clip(np.minimum(16 * n + 32, 64 * jj + 64) - np.maximum(16 * n, 64 * jj), 0, None).astype(np.float32) / 16.0
    t["ovl"] = np.ascontiguousarray(ov.reshape(4, 128, 128).transpose(1, 0, 2)).astype(bf)
    p = np.arange(128)[:, None, None]
    m = np.arange(64)[None, :, None]
    k = np.arange(128)[None, None, :]
    t["inds4"] = ((p == 2 * m + (k >= 64)).astype(np.float32) * np.float32(2.0 ** 30)).astype(bf)
    t["oh"] = np.broadcast_to(np.eye(48, dtype=np.float32)[:, :, None], (48, 48, 64)).astype(bf)
    off = np.arange(64) - 7
    bs = (-slopes[:, None] * 128.0 * off[None, :]).astype(np.float32).reshape(1, 16 * 64)
    t["bias_s"] = np.broadcast_to(bs, (128, 16 * 64)).astype(np.float32)
    bw = (-slopes[:, None] * 128.0 * (4 - np.arange(12))[None, :]).astype(np.float32).reshape(1, 16 * 12)
    t["bias_w"] = np.broadcast_to(bw, (128, 16 * 12)).astype(np.float32)
    jv = np.arange(8)[None, :, None]
    nbv = np.arange(4)[None, None, :]
    bc = (-slopes[:, None, None] * (1024.0 * jv - 2048.0 * nbv - 31.0)).astype(np.float32).reshape(1, 16 * 8 * 4)
    t["bias_c"] = np.broadcast_to(bc, (128, 16 * 8 * 4)).astype(np.float32)
    return {k_: np.ascontiguousarray(v) for k_, v in t.items()}


def prep_shared(inp):
    w_in = inp["w_in"][0]
    offs = np.cumsum([0, 512, 256, 64, 1024, 128, 128, 128, 128, 128, 128, 48])
    c_q, c_kv, k_rope, nsa_q, k_c, v_c, k_s, v_s, k_w, v_w, g_raw = [slice(offs[i], offs[i + 1]) for i in range(11)]
    kr = w_in[:, k_rope]
    kr_sw = np.concatenate([kr[:, 32:], kr[:, :32]], axis=1)
    wk = np.concatenate([w_in[:, c_kv], kr, kr_sw, w_in[:, k_c], w_in[:, v_c], w_in[:, k_s], w_in[:, k_w],
                         w_in[:, v_s], w_in[:, v_w]], axis=1)
    wq = np.concatenate([w_in[:, c_q], w_in[:, nsa_q], w_in[:, g_raw]], axis=1)
    cosk, sink = rope_tables(np.arange(SEQ))
    w_uq = inp["w_uq"][0].reshape(512, 8, 192)
    w_uq_ext = np.concatenate([w_uq, w_uq[:, :, 160:192], w_uq[:, :, 128:160]], axis=2).reshape(512, 2048)
    m = dict(
        attn_g=np.ascontiguousarray(inp["attn_norm_g"].reshape(1, D)),
        wk=np.ascontiguousarray(wk), wq=np.ascontiguousarray(wq),
        gkv=np.ascontiguousarray(inp["mla_kv_norm_g"].reshape(2, 128).T),
        gq=np.ascontiguousarray(inp["mla_q_norm_g"].reshape(4, 128).T),
        cosk=cosk, sink=sink,
        w_uq=np.ascontiguousarray(w_uq_ext), w_uk=np.ascontiguousarray(inp["w_uk"][0]), w_uv=np.ascontiguousarray(inp["w_uv"][0]),
        w_o=np.ascontiguousarray(inp["w_o"][0]), w_gate=np.ascontiguousarray(inp["w_gate"][0]),
        w_up=np.ascontiguousarray(inp["w_up"][0]), w_down=np.ascontiguousarray(inp["w_down"][0]),
        ffn_g=np.ascontiguousarray(inp["ffn_norm_g"].reshape(1, D)), fin_g=np.ascontiguousarray(inp["final_norm_g"].reshape(1, D)),
        w1k=np.ascontiguousarray(inp["w_cmp_k1"][0].reshape(32, 64, 128).transpose(1, 0, 2)),
        w1v=np.ascontiguousarray(inp["w_cmp_v1"][0].reshape(32, 64, 128).transpose(1, 0, 2)),
        w2k=np.ascontiguousarray(inp["w_cmp_k2"][0]), w2v=np.ascontiguousarray(inp["w_cmp_v2"][0]),
        posk=np.ascontiguousarray(inp["cmp_pos_k"][0].T), posv=np.ascontiguousarray(inp["cmp_pos_v"][0].T),
        identd=np.eye(128, dtype=np.float32).astype(ml_dtypes.bfloat16),
    )
    return m


_HALF_CACHE = {}


def prep_half(half):
    if half in _HALF_CACHE:
        return _HALF_CACHE[half]
    own = own_token_index(half)
    cosq, sinq = rope_tables(own)
    kpos = (np.arange(8)[None, :, None] * 128 + np.arange(128)[:, None, None])
    qrel = np.arange(512)[None, None, :]
    mE = (kpos <= qrel).astype(np.float32).astype(ml_dtypes.bfloat16)
    mO = (kpos <= qrel + 512).astype(np.float32).astype(ml_dtypes.bfloat16)
    cmA, cmB = (mE, mO) if half == 0 else (mO, mE)
    m = dict(cosq=cosq, sinq=sinq, cmaskA=np.ascontiguousarray(cmA), cmaskB=np.ascontiguousarray(cmB))
    m.update(nsa_tables(half))
    _HALF_CACHE[half] = m
    return m


def prep_inputs(inp, core, shared=None):
    b, half = core // 2, core % 2
    own = own_token_index(half)
    x = inp["x"]
    m = dict(shared if shared is not None else prep_shared(inp))
    m["xs"] = np.ascontiguousarray(x[b])
    m["xq"] = np.ascontiguousarray(x[b][own])
    m.update(prep_half(half))
    return m


def kernel(**inputs):
    inp = {k: np.asarray(v) for k, v in inputs.items()}
    nc = build()
    shared = prep_shared(inp)
    maps = [prep_inputs(inp, c, shared) for c in range(8)]
    res = run_bass_kernel_spmd(nc, maps, core_ids=list(range(8)))
    out = np.empty((4, SEQ, D), np.float32)
    for c in range(8):
        out[c // 2][own_token_index(c % 2)] = np.asarray(res.results[c]["out"], dtype=np.float32)
    return out
```
